# Optimizing a Trainium2 kernel written in Bass

```python
import jax, jax.numpy as jnp
from jax import lax
import numpy as np

D_MODEL = 1024
BATCH = 32
SEQ = 2048
DEPTH = 1
DEC_BATCH = 4
DEC_SEQ = 8192
PAST_LEN = 128

MLA_HEADS = 8
Q_LORA = 256
KV_LORA = 128
NOPE_DIM = 64
ROPE_DIM = 32
MLA_V_DIM = 64
Q_BLOCK = 128
RET_HEADS = 8
RET_QK_DIM = 64
RET_V_DIM = 128
CHUNK = 128
D_FF = 4 * D_MODEL
ROPE_THETA = 10000.0
EPS = 1e-6

SPLITS = (Q_LORA, KV_LORA, ROPE_DIM,
          RET_HEADS * RET_QK_DIM, RET_HEADS * RET_QK_DIM,
          RET_HEADS * RET_V_DIM, RET_HEADS * RET_V_DIM,
          D_MODEL, D_MODEL)
D_IN = sum(SPLITS)

kernel_name = "hybrid_mla_retention_encoder"


def rms_norm(x, g=None):
    xf = x.astype(jnp.float32)
    y = xf * lax.rsqrt(jnp.mean(xf * xf, axis=-1, keepdims=True) + EPS)
    if g is not None:
        y = y * g.astype(jnp.float32)
    return y.astype(x.dtype)


def rope_tables(seq, dim):
    inv = 1.0 / (ROPE_THETA ** (jnp.arange(0, dim, 2, dtype=jnp.float32) / dim))
    ang = jnp.arange(seq, dtype=jnp.float32)[:, None] * inv[None, :]
    return jnp.cos(ang), jnp.sin(ang)


def apply_rope(x, cos, sin):
    half = x.shape[-1] // 2
    x1 = x[..., :half].astype(jnp.float32)
    x2 = x[..., half:].astype(jnp.float32)
    return jnp.concatenate([x1 * cos - x2 * sin, x1 * sin + x2 * cos], axis=-1).astype(x.dtype)


def mla_attention(q_nope, q_rope, k_nope, k_rope, v):
    B, S, H, _ = q_nope.shape
    nb = S // Q_BLOCK
    scale = (NOPE_DIM + ROPE_DIM) ** -0.5

    def block(args):
        qn, qr = args
        s = (jnp.einsum('bqhd,bkhd->bhqk', qn, k_nope)
             + jnp.einsum('bqhd,bkd->bhqk', qr, k_rope))
        p = jax.nn.softmax(s.astype(jnp.float32) * scale, axis=-1).astype(v.dtype)
        return jnp.einsum('bhqk,bkhd->bqhd', p, v)

    qn_b = q_nope.reshape(B, nb, Q_BLOCK, H, NOPE_DIM).transpose(1, 0, 2, 3, 4)
    qr_b = q_rope.reshape(B, nb, Q_BLOCK, H, ROPE_DIM).transpose(1, 0, 2, 3, 4)
    out = lax.map(block, (qn_b, qr_b))
    return out.transpose(1, 0, 2, 3, 4).reshape(B, S, H * MLA_V_DIM)


def retention_dir(q, k, v, log_gamma, inclusive):
    B, H, S, Dk = q.shape
    Dv = v.shape[-1]
    n = S // CHUNK
    lg = log_gamma.astype(jnp.float32)
    idx = jnp.arange(CHUNK, dtype=jnp.float32)
    diff = idx[:, None] - idx[None, :]
    mask = (diff >= 0) if inclusive else (diff > 0)
    intra = jnp.where(mask[None], jnp.exp(lg[:, None, None] * jnp.maximum(diff, 0.0)[None]), 0.0)
    q_dec = jnp.exp(lg[:, None] * (idx[None, :] + 1.0))
    k_dec = jnp.exp(lg[:, None] * (CHUNK - 1.0 - idx[None, :]))
    chunk_dec = jnp.exp(lg * CHUNK)

    qc = q.reshape(B, H, n, CHUNK, Dk)
    kc = k.reshape(B, H, n, CHUNK, Dk)
    vc = v.reshape(B, H, n, CHUNK, Dv)

    s = jnp.einsum('bhnid,bhnjd->bhnij', qc, kc) * intra[:, None]
    o_intra = jnp.einsum('bhnij,bhnjv->bhniv', s, vc)

    kv = jnp.einsum('bhnjd,bhnjv->nbhdv', kc * k_dec[:, None, :, None], vc).astype(jnp.float32)

    def step(state, kv_n):
        return state * chunk_dec[None, :, None, None] + kv_n, state

    _, states = lax.scan(step, jnp.zeros((B, H, Dk, Dv), jnp.float32), kv)
    o_cross = jnp.einsum('bhnid,nbhdv->bhniv', qc * q_dec[:, None, :, None], states)
    return (o_intra + o_cross).reshape(B, H, S, Dv).astype(v.dtype)


def mixer(h, w_in, g_q_norm, w_q_up, g_kv_norm, w_kv_up, w_branch_a,
          ret_log_decay_fwd, ret_log_decay_bwd, w_branch_b, w_out):
    B, S, _ = h.shape
    offsets = np.cumsum(np.array(SPLITS))[:-1].tolist()
    c_q, c_kv, k_r, rq, rk, rv, rg, ga, gb = jnp.split(h @ w_in, offsets, axis=-1)

    cos_m, sin_m = rope_tables(S, ROPE_DIM)
    q = (rms_norm(c_q, g_q_norm) @ w_q_up).reshape(B, S, MLA_HEADS, NOPE_DIM + ROPE_DIM)
    q_nope, q_rope = q[..., :NOPE_DIM], q[..., NOPE_DIM:]
    q_rope = apply_rope(q_rope, cos_m[:, None, :], sin_m[:, None, :])
    kv = (rms_norm(c_kv, g_kv_norm) @ w_kv_up).reshape(B, S, MLA_HEADS, NOPE_DIM + MLA_V_DIM)
    k_nope, v_a = kv[..., :NOPE_DIM], kv[..., NOPE_DIM:]
    k_rope = apply_rope(k_r, cos_m, sin_m)
    a = mla_attention(q_nope, q_rope, k_nope, k_rope, v_a) @ w_branch_a

    cos_r, sin_r = rope_tables(S, RET_QK_DIM)
    rq = apply_rope(rq.reshape(B, S, RET_HEADS, RET_QK_DIM), cos_r[:, None, :], sin_r[:, None, :])
    rk = apply_rope(rk.reshape(B, S, RET_HEADS, RET_QK_DIM), cos_r[:, None, :], sin_r[:, None, :])
    rq = rq.transpose(0, 2, 1, 3)
    rk = (rk * (RET_QK_DIM ** -0.5)).transpose(0, 2, 1, 3)
    rv = rv.reshape(B, S, RET_HEADS, RET_V_DIM).transpose(0, 2, 1, 3)
    o_f = retention_dir(rq, rk, rv, ret_log_decay_fwd, True)
    o_b = jnp.flip(retention_dir(jnp.flip(rq, 2), jnp.flip(rk, 2), jnp.flip(rv, 2),
                                 ret_log_decay_bwd, False), 2)
    o = rms_norm((o_f + o_b).transpose(0, 2, 1, 3))
    o = o.reshape(B, S, RET_HEADS * RET_V_DIM) * jax.nn.silu(rg)
    b = o @ w_branch_b

    merged = jax.nn.sigmoid(ga) * a + jax.nn.sigmoid(gb) * b
    return merged @ w_out


def trunk(x, g_pre_mix, w_in, g_q_norm, w_q_up, g_kv_norm, w_kv_up, w_branch_a,
          ret_log_decay_fwd, ret_log_decay_bwd, w_branch_b, w_out, g_post_mix,
          g_pre_mlp, w_up, w_down, g_post_mlp):
    for l in range(DEPTH):
        h = rms_norm(x, g_pre_mix[l])
        m = mixer(h, w_in[l], g_q_norm[l], w_q_up[l], g_kv_norm[l], w_kv_up[l], w_branch_a[l],
                  ret_log_decay_fwd[l], ret_log_decay_bwd[l], w_branch_b[l], w_out[l])
        x = x + rms_norm(m, g_post_mix[l])
        h = rms_norm(x, g_pre_mlp[l])
        u = jnp.square(jax.nn.relu(h @ w_up[l]))
        x = x + rms_norm(u @ w_down[l], g_post_mlp[l])
    return x


def setup_inputs(seed: int = 0) -> dict:
    key = jax.random.key(seed)
    ks = jax.random.split(key, 24)

    def w(k, fan_in, fan_out):
        return jax.random.normal(k, (DEPTH, fan_in, fan_out), jnp.float32) * fan_in ** -0.5

    def gain(k, dim):
        return 1.0 + 0.05 * jax.random.normal(k, (DEPTH, dim), jnp.float32)

    base = jnp.log1p(-jnp.exp2(-5.0 - jnp.arange(RET_HEADS, dtype=jnp.float32)))
    dec_f = base[None] * (1.0 + 0.05 * jax.random.normal(ks[20], (DEPTH, RET_HEADS), jnp.float32))
    dec_b = base[None] * (1.0 + 0.05 * jax.random.normal(ks[21], (DEPTH, RET_HEADS), jnp.float32))
    return {
        "x_prompt": jax.random.normal(ks[0], (BATCH, SEQ, D_MODEL), jnp.float32),
        "x_sample": jax.random.normal(ks[1], (DEC_BATCH, DEC_SEQ, D_MODEL), jnp.float32),
        "g_pre_mix": gain(ks[2], D_MODEL),
        "w_in": w(ks[3], D_MODEL, D_IN),
        "g_q_norm": gain(ks[4], Q_LORA),
        "w_q_up": w(ks[5], Q_LORA, MLA_HEADS * (NOPE_DIM + ROPE_DIM)),
        "g_kv_norm": gain(ks[6], KV_LORA),
        "w_kv_up": w(ks[7], KV_LORA, MLA_HEADS * (NOPE_DIM + MLA_V_DIM)),
        "w_branch_a": w(ks[8], MLA_HEADS * MLA_V_DIM, D_MODEL),
        "ret_log_decay_fwd": dec_f,
        "ret_log_decay_bwd": dec_b,
        "w_branch_b": w(ks[9], RET_HEADS * RET_V_DIM, D_MODEL),
        "w_out": w(ks[10], D_MODEL, D_MODEL),
        "g_post_mix": gain(ks[11], D_MODEL),
        "g_pre_mlp": gain(ks[12], D_MODEL),
        "w_up": w(ks[13], D_MODEL, D_FF),
        "w_down": w(ks[14], D_FF, D_MODEL),
        "g_post_mlp": gain(ks[15], D_MODEL),
    }


def reference(x_prompt, x_sample, g_pre_mix, w_in, g_q_norm, w_q_up, g_kv_norm, w_kv_up,
              w_branch_a, ret_log_decay_fwd, ret_log_decay_bwd, w_branch_b, w_out, g_post_mix,
              g_pre_mlp, w_up, w_down, g_post_mlp):
    y_prompt = trunk(x_prompt, g_pre_mix, w_in, g_q_norm, w_q_up, g_kv_norm, w_kv_up, w_branch_a,
                     ret_log_decay_fwd, ret_log_decay_bwd, w_branch_b, w_out, g_post_mix,
                     g_pre_mlp, w_up, w_down, g_post_mlp)
    y_sample = trunk(x_sample, g_pre_mix, w_in, g_q_norm, w_q_up, g_kv_norm, w_kv_up, w_branch_a,
                     ret_log_decay_fwd, ret_log_decay_bwd, w_branch_b, w_out, g_post_mix,
                     g_pre_mlp, w_up, w_down, g_post_mlp)
    return (y_prompt, y_sample)
```

```python
import math
import types
from contextlib import ExitStack
import numpy as np
import concourse.bass as bass
import concourse.mybir as mybir
from concourse.bass_utils import run_bass_kernel_spmd

F32 = mybir.dt.float32
BF16 = mybir.dt.bfloat16
I32 = mybir.dt.int32
AF = mybir.ActivationFunctionType
ALU = mybir.AluOpType
AX = mybir.AxisListType

D = 1024
OFF = dict(cq=0, ckv=256, kr=384, rq=416, rk=928, rv=1440, rg=2464, ga=3488, gb=4512)
EPS = 1e-6
ATT_SCALE = 96 ** -0.5


def freeze(fn):
    if fn.__closure__ is None:
        return fn
    cells = []
    for c in fn.__closure__:
        try:
            cells.append(types.CellType(c.cell_contents))
        except ValueError:
            cells.append(c)
    return types.FunctionType(fn.__code__, fn.__globals__, fn.__name__, fn.__defaults__, tuple(cells))


class Res:
    __slots__ = ("w", "r")

    def __init__(self):
        self.w = None
        self.r = []


class Reg:
    def __init__(self, nbytes, gran):
        self.gran = gran
        self.n = (nbytes + gran - 1) // gran
        self.res = [(Res(), Res()) for _ in range(self.n)]

    def r(self, lo=0, hi=None, half=None):
        if hi is None:
            hi = self.n * self.gran
        out = []
        for s in range(lo // self.gran, (hi - 1) // self.gran + 1):
            if half in (None, 0):
                out.append(self.res[s][0])
            if half in (None, 1):
                out.append(self.res[s][1])
        return out


class LB:
    def __init__(self, tile, reg, off, size, dt):
        self.tile, self.reg, self.off, self.size, self.dt = tile, reg, off, size, dt
        self.isz = 4 if dt in (F32, I32) else 2
        a = tile[:, off // 2:(off + size) // 2]
        self.ap = a if dt == BF16 else a.bitcast(dt)

    def v(self, pat=None, **kw):
        return self.ap if pat is None else self.ap.rearrange(pat, **kw)

    def r(self, lo=0, hi=None, half=None):
        hi_b = self.size if hi is None else hi * self.isz
        return self.reg.r(self.off + lo * self.isz, self.off + hi_b, half)


class Prog:
    COMPUTE = ("pe", "dve", "act", "pool")
    ALL = ("pe", "dve", "act", "pool", "sp")

    def __init__(self, nc, n_dma_sems=14):
        self.nc = nc
        self.q = {e: [] for e in self.ALL}
        self.cnt = {e: 0 for e in self.COMPUTE}
        self.psem = {}
        self.seen = {e: {} for e in self.ALL}
        self._ctx = []
        for e in self.COMPUTE:
            g = nc.semaphore("prog_" + e)
            self.psem[e] = g.__enter__()
            self._ctx.append(g)
        self.dsem, self.dsem_val, self.dsem_next = {}, {}, {}
        for qn in ("sp", "pool"):
            lst = []
            for i in range(n_dma_sems):
                g = nc.semaphore("dma_%s_%d" % (qn, i))
                lst.append(g.__enter__())
                self._ctx.append(g)
            self.dsem[qn] = lst
            self.dsem_val[qn] = [0] * n_dma_sems
            self.dsem_next[qn] = 0
        self.final_tokens = []
        self.n_ops = 0

    def _deps(self, reads, writes):
        toks = []
        for r in reads:
            if r.w is not None:
                toks.append(r.w)
        for w in writes:
            if w.w is not None:
                toks.append(w.w)
            toks.extend(w.r)
        return toks

    def _emit_waits(self, e, toks):
        need = {}
        seen = self.seen[e]
        for (sem, val, src) in toks:
            if src == e and e == "pe":
                continue
            k = id(sem)
            if seen.get(k, 0) >= val:
                continue
            if k not in need or need[k][1] < val:
                need[k] = (sem, val)
        for k, (sem, val) in need.items():
            seen[k] = val
            self.q[e].append(("wait", sem, val))

    def _commit(self, tok, reads, writes):
        for r in reads:
            r.r.append(tok)
        for w in writes:
            w.w = tok
            w.r = []

    def op(self, e, fn, reads=(), writes=()):
        self._emit_waits(e, self._deps(reads, writes))
        self.cnt[e] += 1
        tok = (self.psem[e], self.cnt[e], e)
        self.q[e].append(("op", freeze(fn), self.psem[e]))
        self._commit(tok, reads, writes)
        self.n_ops += 1

    def dma(self, out_ap, in_ap, reads=(), writes=(), final=False, qn="sp", **kw):
        i = self.dsem_next[qn]
        self.dsem_next[qn] = (i + 1) % len(self.dsem[qn])
        sem = self.dsem[qn][i]
        toks = self._deps(reads, writes)
        prev = self.dsem_val[qn][i]
        if prev > 0:
            toks.append((sem, prev, None))
        self._emit_waits(qn, toks)
        val = prev + 16
        self.dsem_val[qn][i] = val
        tok = (sem, val, None)
        self.q[qn].append(("dma", out_ap, in_ap, sem, kw))
        self._commit(tok, reads, writes)
        if final:
            self.final_tokens.append(tok)
        self.n_ops += 1

    def finish(self):
        nc = self.nc
        self._emit_waits("sp", self.final_tokens)
        qs = self.q

        def run(e, engine):
            for item in qs[e]:
                if item[0] == "wait":
                    engine.wait_ge(item[1], item[2])
                elif item[0] == "op":
                    item[1](engine).then_inc(item[2], 1)
                else:
                    _, o, i_, sem, kw = item
                    engine.dma_start(out=o, in_=i_, **kw).then_inc(sem, 16)

        with nc.Block() as block:
            @block.tensor
            def _(t):
                run("pe", t)

            @block.vector
            def _(v):
                run("dve", v)

            @block.scalar
            def _(s):
                run("act", s)

            @block.gpsimd
            def _(g):
                run("pool", g)

            @block.sync
            def _(s):
                run("sp", s)
        for g in reversed(self._ctx):
            g.__exit__(None, None, None)


def build(cfg):
    S_OWN = cfg["S_OWN"]
    NCTX = list(cfg["nctx"])
    NJ = len(NCTX)
    NT = S_OWN // 128
    NB = S_OWN // 512
    NKT = [NT + c for c in NCTX]
    MAXKT = max(NKT)
    TOTKT = sum(NKT)
    TOTC = max(1, sum(NCTX))
    MAXC = max(1, max(NCTX))

    nc = bass.Bass("TRN2", target_bir_lowering=False)
    es = ExitStack()

    def din(name, shape, dt=F32):
        return nc.dram_tensor(name, list(shape), dt, kind="ExternalInput").ap()

    xo = din("xo", [NJ * S_OWN, D])
    xc = din("xc", [TOTC * 128, D])
    tabT = din("tabT", [TOTKT * 128, 64])
    tabM = din("tabM", [32, 2, TOTKT * 128])
    tabF = din("tabF", [128, 2, NJ * S_OWN])
    cD = din("cD", [128, TOTC])
    cF = din("cF", [128, TOTC])
    w_in = din("w_in", [D, 5536])
    w_q_up = din("w_q_up", [256, 768])
    w_kv_up = din("w_kv_up", [128, 1024])
    w_a = din("w_branch_a", [512, 1024])
    w_b = din("w_branch_b", [1024, 1024])
    w_out = din("w_out", [1024, 1024])
    w_up = din("w_up", [1024, 4096])
    w_down = din("w_down", [4096, 1024])
    g_pre_mix = din("g_pre_mix", [D])
    g_q_norm = din("g_q_norm", [256])
    g_kv_norm = din("g_kv_norm", [128])
    g_post_mix = din("g_post_mix", [D])
    g_pre_mlp = din("g_pre_mlp", [D])
    g_post_mlp = din("g_post_mlp", [D])
    lgf = din("lgf", [8])
    lgb = din("lgb", [8])
    y = nc.dram_tensor("y", [NJ * S_OWN, D], F32, kind="ExternalOutput").ap()

    P = Prog(nc)
    dbg = nc.dram_tensor("dbg", [NJ * S_OWN, D], F32, kind="ExternalOutput").ap() if cfg.get("dbg") else None
    if cfg.get("dbg"):
        d_mt = nc.dram_tensor("d_mt", [128, 8 * 512], BF16, kind="ExternalOutput").ap()
        d_obt = nc.dram_tensor("d_obt", [128, 8 * 512], BF16, kind="ExternalOutput").ap()
        d_ot = nc.dram_tensor("d_ot", [128, 8 * S_OWN], BF16, kind="ExternalOutput").ap()
        d_rqt = nc.dram_tensor("d_rqt", [128, 8 * 512], BF16, kind="ExternalOutput").ap()
        d_rkt = nc.dram_tensor("d_rkt", [128, 8 * 512], BF16, kind="ExternalOutput").ap()
        d_st = nc.dram_tensor("d_st", [128, NT * 1024], BF16, kind="ExternalOutput").ap()

    def sbuf(name, nbytes, gran=512):
        t = es.enter_context(nc.sbuf_tensor(name, [128, nbytes // 2], BF16))
        return t, Reg(nbytes, gran)

    def mk(name, nbytes, dt, gran=512):
        t, reg = sbuf(name, nbytes, gran)
        return LB(t, reg, 0, nbytes, dt)

    ARENA = 80 * 1024 - 512
    ar_t, ar_reg = sbuf("arena", ARENA, 512)

    def AR(off, size, dt):
        assert off + size <= ARENA, (off, size)
        assert off % 512 == 0
        return LB(ar_t, ar_reg, off, size, dt)

    K = 1024
    OT = mk("OT", 8 * S_OWN * 2, BF16, 1024)
    ST = mk("ST", NT * 2048, BF16, 2048)
    RING = [mk("ring%d" % i, 8192, BF16, 8192) for i in range(4)]
    IDENT = mk("ident", 256, BF16)
    ONESB = mk("onesb", 256, BF16)
    DUP = mk("dup", 256, BF16)
    ONESF = mk("onesf", 256, F32)
    DT_ = mk("dt", 8 * 128 * 4, F32, 4096)
    QDEC = mk("qdec", 8 * 128 * 4, F32, 4096)
    KDEC = mk("kdec", 64, F32)
    DEC = mk("dec", 32, F32)
    LG = mk("lg", 32, F32)
    LGF = mk("lgf_t", 32, F32)
    LGB = mk("lgb_t", 32, F32)
    EPST = mk("eps", 4, F32)
    G1 = mk("g1", 4096, F32, 4096)
    G2 = mk("g2", 4096, F32, 4096)
    WQ = mk("wq", 2 * 1024 * 2, BF16, 4096)
    WKV = mk("wkv", 8 * 128 * 2, BF16, 4096)
    ACC = mk("acc", 4096, F32, 4096)
    COEF = mk("coef", MAXC * 8 * 4, F32, MAXC * 32)
    CDT = mk("cdt", MAXC * 4, F32, MAXC * 4)
    CFT = mk("cft", MAXC * 4, F32, MAXC * 4)
    STAT = mk("stat", 64 * 4, F32, 256)
    STS = mk("sts", 512, F32, 16)

    def SC(c0, n=1):
        return STS.ap[:, c0:c0 + n], STS.r(c0, c0 + n)

    PSB = []
    for i in range(8):
        t = es.enter_context(nc.psum_tensor("ps%d" % i, [128, 512], F32))
        PSB.append((t, Reg(2048, 2048)))

    def PS(i):
        return PSB[i][0]

    def PSR(i, half=None):
        return PSB[i][1].r(half=half)

    def PSbf(i):
        return PSB[i][0][:, :].bitcast(BF16)

    packs = {}

    def pack_tensor(name, rows, cols):
        t = nc.dram_tensor("wp_" + name, [rows, cols], BF16, kind="Internal").ap()
        packs[name] = (t, Res())
        return t

    def act_rstd(out_ap, in_ap, dim, reads, writes):
        P.op("act", lambda e: e.activation(out=out_ap, in_=in_ap, func=AF.Ln,
                                           scale=1.0 / dim, bias=EPST.ap[:, 0:1]),
             reads=list(reads) + EPST.r(), writes=writes)
        P.op("act", lambda e: e.activation(out=out_ap, in_=out_ap, func=AF.Exp, scale=-0.5),
             reads=writes, writes=writes)

    rr = {"i": 0}

    def alt(engs=("dve", "pool")):
        rr["i"] += 1
        return engs[rr["i"] % len(engs)]

    NSTG = 4
    STG = [AR(i * 4096, 4096, F32) for i in range(NSTG)]
    STB = [AR(16384 + i * 2048, 2048, BF16) for i in range(NSTG)]
    o_ = 24576
    TMPA = AR(o_, 4096, F32); o_ += 4096
    TMPB = AR(o_, 4096, F32); o_ += 4096
    TMPI = AR(o_, 512, I32); o_ += 512
    DIFF = AR(o_, 512, F32); o_ += 512
    MGE = AR(o_, 512, F32); o_ += 512
    MLT = AR(o_, 512, F32); o_ += 512
    PPOS = AR(o_, 512, F32); o_ += 512
    PNEG = AR(o_, 512, F32); o_ += 512
    AQ = AR(o_, 512, F32); o_ += 512
    IFR = AR(o_, 512, F32); o_ += 512
    JP = AR(o_, 512, F32); o_ += 512
    IDF = AR(o_, 512, F32); o_ += 512

    P.op("pool", lambda e: e.memset(EPST.ap, EPS), writes=EPST.r())
    P.op("pool", lambda e: e.memset(ONESB.ap, 1.0), writes=ONESB.r())
    P.op("pool", lambda e: e.memset(ONESF.ap, 1.0), writes=ONESF.r())
    P.op("pool", lambda e: e.memset(IDF.ap, 0.0), writes=IDF.r())
    P.op("pool", lambda e: e.affine_select(out=IDF.ap, in_=IDF.ap, pattern=[[-1, 128]],
                                           compare_op=ALU.not_equal, fill=1.0, base=0,
                                           channel_multiplier=1),
         reads=IDF.r(), writes=IDF.r())
    P.op("dve", lambda e: e.tensor_copy(out=IDENT.ap, in_=IDF.ap), reads=IDF.r(), writes=IDENT.r())
    for (pr, cs, ps_, cs2) in ((0, 0, 0, 0), (0, 64, 0, 0), (64, 0, 64, 64), (64, 64, 64, 64)):
        P.op("dve", lambda e, pr=pr, cs=cs, cs2=cs2: e.tensor_copy(
            out=DUP.ap[pr:pr + 64, cs:cs + 64], in_=IDF.ap[pr:pr + 64, cs2:cs2 + 64]),
            reads=IDF.r(), writes=DUP.r())
    P.op("pool", lambda e: e.iota(TMPI.ap, pattern=[[1, 128]], base=0, channel_multiplier=-1),
         writes=TMPI.r())
    P.op("dve", lambda e: e.tensor_copy(out=DIFF.ap, in_=TMPI.ap), reads=TMPI.r(), writes=DIFF.r())
    P.op("pool", lambda e: e.iota(TMPI.ap, pattern=[[1, 128]], base=0, channel_multiplier=0),
         reads=TMPI.r(), writes=TMPI.r())
    P.op("dve", lambda e: e.tensor_copy(out=IFR.ap, in_=TMPI.ap), reads=TMPI.r(), writes=IFR.r())
    P.op("pool", lambda e: e.iota(TMPI.ap, pattern=[[0, 128]], base=0, channel_multiplier=1),
         reads=TMPI.r(), writes=TMPI.r())
    P.op("dve", lambda e: e.tensor_copy(out=JP.ap, in_=TMPI.ap), reads=TMPI.r(), writes=JP.r())
    P.op("dve", lambda e: e.tensor_scalar(out=MGE.ap, in0=DIFF.ap, scalar1=0.0, scalar2=None,
                                          op0=ALU.is_ge), reads=DIFF.r(), writes=MGE.r())
    P.op("dve", lambda e: e.tensor_scalar(out=MLT.ap, in0=DIFF.ap, scalar1=0.0, scalar2=None,
                                          op0=ALU.is_lt), reads=DIFF.r(), writes=MLT.r())
    P.op("dve", lambda e: e.tensor_scalar(out=PPOS.ap, in0=DIFF.ap, scalar1=0.0, scalar2=None,
                                          op0=ALU.max), reads=DIFF.r(), writes=PPOS.r())
    P.op("dve", lambda e: e.tensor_scalar(out=PNEG.ap, in0=DIFF.ap, scalar1=-1.0, scalar2=0.0,
                                          op0=ALU.mult, op1=ALU.max), reads=DIFF.r(), writes=PNEG.r())
    P.dma(LGF.ap, lgf.partition_broadcast(128), writes=LGF.r())
    P.dma(LGB.ap, lgb.partition_broadcast(128), writes=LGB.r())
    P.dma(LG.ap[0:64, :], lgf.partition_broadcast(64), writes=LG.r())
    P.dma(LG.ap[64:128, :], lgb.partition_broadcast(64), writes=LG.r())
    P.dma(G1.ap, g_post_mix.partition_broadcast(128), writes=G1.r())
    P.dma(G2.ap, g_post_mlp.partition_broadcast(128), writes=G2.r())
    P.op("dve", lambda e: e.tensor_scalar(out=AQ.ap[0:64, :], in0=IFR.ap[0:64, :], scalar1=1.0,
                                          scalar2=None, op0=ALU.add), reads=IFR.r(), writes=AQ.r())
    P.op("dve", lambda e: e.tensor_scalar(out=AQ.ap[64:128, :], in0=IFR.ap[64:128, :], scalar1=-1.0,
                                          scalar2=128.0, op0=ALU.mult, op1=ALU.add),
         reads=IFR.r(), writes=AQ.r())
    DTv = DT_.v("p (h i) -> p h i", h=8)
    QDv = QDEC.v("p (h i) -> p h i", h=8)
    for h in range(8):
        P.op("act", lambda e, h=h: e.activation(out=TMPA.ap[:, 0:128], in_=PPOS.ap, func=AF.Exp,
                                                scale=LGF.ap[:, h:h + 1]),
             reads=PPOS.r() + LGF.r(), writes=TMPA.r())
        P.op("act", lambda e, h=h: e.activation(out=TMPB.ap[:, 0:128], in_=PNEG.ap, func=AF.Exp,
                                                scale=LGB.ap[:, h:h + 1]),
             reads=PNEG.r() + LGB.r(), writes=TMPB.r())
        P.op("dve", lambda e: e.tensor_tensor(out=TMPA.ap[:, 0:128], in0=TMPA.ap[:, 0:128],
                                              in1=MGE.ap, op=ALU.mult),
             reads=TMPA.r() + MGE.r(), writes=TMPA.r())
        P.op("dve", lambda e: e.tensor_tensor(out=TMPB.ap[:, 0:128], in0=TMPB.ap[:, 0:128],
                                              in1=MLT.ap, op=ALU.mult),
             reads=TMPB.r() + MLT.r(), writes=TMPB.r())
        P.op("dve", lambda e: e.tensor_tensor(out=TMPA.ap[:, 0:128], in0=TMPA.ap[:, 0:128],
                                              in1=TMPB.ap[:, 0:128], op=ALU.add),
             reads=TMPA.r() + TMPB.r(), writes=TMPA.r())
        P.op("dve", lambda e, h=h: e.tensor_scalar(out=DTv[:, h, :], in0=TMPA.ap[:, 0:128],
                                                   scalar1=0.125, scalar2=None, op0=ALU.mult),
             reads=TMPA.r(), writes=DT_.r())
        P.op("act", lambda e, h=h: e.activation(out=QDv[:, h, :], in_=AQ.ap, func=AF.Exp,
                                                scale=LG.ap[:, h:h + 1]),
             reads=AQ.r() + LG.r(), writes=QDEC.r())
    KDv = KDEC.v("p (a h) -> p a h", a=2)
    P.op("dve", lambda e: e.tensor_scalar(out=TMPA.ap[:, 0:1], in0=JP.ap[:, 0:1], scalar1=-1.0,
                                          scalar2=127.0, op0=ALU.mult, op1=ALU.add),
         reads=JP.r() + TMPA.r(), writes=TMPA.r())
    P.op("act", lambda e: e.activation(out=KDv[:, 0, :], in_=LGF.ap, func=AF.Exp,
                                       scale=TMPA.ap[:, 0:1]),
         reads=TMPA.r() + LGF.r(), writes=KDEC.r())
    P.op("act", lambda e: e.activation(out=KDv[:, 1, :], in_=LGB.ap, func=AF.Exp,
                                       scale=JP.ap[:, 0:1]),
         reads=JP.r() + LGB.r(), writes=KDEC.r())
    P.op("dve", lambda e: e.tensor_scalar(out=KDEC.ap, in0=KDEC.ap, scalar1=0.125, scalar2=None,
                                          op0=ALU.mult), reads=KDEC.r(), writes=KDEC.r())
    P.op("act", lambda e: e.activation(out=DEC.ap, in_=LG.ap, func=AF.Exp, scale=128.0),
         reads=LG.r(), writes=DEC.r())

    chunks = []

    def build_pack(name, src, cs, kc_n, segs, gain):
        NC = sum(s_[1] for s_ in segs)
        dst, dres = pack_tensor(name, cs, kc_n * NC), packs[name][1]
        gt = None
        if gain is not None:
            gt = mk("gt_" + name, 128, F32, 128)
            P.dma(gt.ap[0:cs, 0:kc_n], gain.rearrange("(c p) -> p c", p=cs), writes=gt.r(),
                  allow_slow_non_contiguous=True)
        for kc in range(kc_n):
            off = 0
            for (c0, n, sign) in segs:
                a = 0
                while a < n:
                    m = min(1024, n - a)
                    chunks.append((src[kc * cs:(kc + 1) * cs, c0 + a:c0 + a + m], cs, m, gt, kc, sign,
                                   dst[:, kc * NC + off + a: kc * NC + off + a + m], dres))
                    a += m
                off += n
        return dst

    def emit_chunks():
        n_ = len(chunks)
        LAH = 3
        for k in range(n_ + LAH):
            if k < n_:
                src_, cs, m, gt, kc, sign, dst_, dres = chunks[k]
                sg = STG[k % NSTG]
                P.dma(sg.ap[0:cs, 0:m], src_, writes=sg.r())
            if k >= LAH:
                kk = k - LAH
                src_, cs, m, gt, kc, sign, dst_, dres = chunks[kk]
                sg, sb = STG[kk % NSTG], STB[kk % NSTG]
                use_act = (kk % 2 == 0) and sign > 0
                if gt is not None:
                    if use_act:
                        P.op("act", lambda e: e.activation(out=sb.ap[0:cs, 0:m], in_=sg.ap[0:cs, 0:m], func=AF.Copy,
                                                           scale=gt.ap[0:cs, kc:kc + 1]),
                             reads=sg.r() + gt.r(), writes=sb.r())
                    else:
                        P.op("dve", lambda e: e.tensor_scalar(
                            out=sb.ap[0:cs, 0:m], in0=sg.ap[0:cs, 0:m], scalar1=gt.ap[0:cs, kc:kc + 1],
                            scalar2=float(sign), op0=ALU.mult, op1=ALU.mult),
                            reads=sg.r() + gt.r(), writes=sb.r())
                else:
                    if use_act:
                        P.op("act", lambda e: e.copy(out=sb.ap[0:cs, 0:m], in_=sg.ap[0:cs, 0:m]),
                             reads=sg.r(), writes=sb.r())
                    else:
                        P.op("dve", lambda e: e.tensor_scalar(
                            out=sb.ap[0:cs, 0:m], in0=sg.ap[0:cs, 0:m], scalar1=float(sign),
                            scalar2=None, op0=ALU.mult),
                            reads=sg.r(), writes=sb.r())
                P.dma(dst_, sb.ap[0:cs, 0:m], reads=sb.r(), writes=[dres])

    OUTB = [AR(40960 + i * 6144, 6144, BF16) for i in range(2)]
    cjobs = []

    def custom_pack(name, src, kc_n, NC, loads, opsfn, gain):
        dst, dres = pack_tensor(name, 128, kc_n * NC), packs[name][1]
        gt = mk("gt_" + name, 128, F32, 128)
        P.dma(gt.ap[:, 0:kc_n], gain.rearrange("(c p) -> p c", p=128), writes=gt.r(),
              allow_slow_non_contiguous=True)
        for kc in range(kc_n):
            cjobs.append((src, kc, NC, loads, opsfn, gt, dst, dres))

    def emit_custom():
        k_stg = [0]
        for ji, (src, kc, NC, loads, opsfn, gt, dst, dres) in enumerate(cjobs):
            stgs = []
            for (c0, n) in loads:
                sg = STG[k_stg[0] % NSTG]
                k_stg[0] += 1
                P.dma(sg.ap[:, 0:n], src[kc * 128:(kc + 1) * 128, c0:c0 + n], writes=sg.r())
                stgs.append(sg)
            ob = OUTB[ji % 2]
            oap = ob.ap[:, 0:NC]
            rd = []
            for sg in stgs:
                rd += sg.r()
            for oi, (iap, oap_, sign) in enumerate(opsfn([sg.ap for sg in stgs], oap)):
                if sign > 0 and oi % 2 == 0:
                    P.op("act", lambda e: e.activation(out=oap_, in_=iap, func=AF.Copy, scale=gt.ap[:, kc:kc + 1]),
                         reads=rd + gt.r(), writes=ob.r())
                else:
                    P.op("dve", lambda e: e.tensor_scalar(out=oap_, in0=iap, scalar1=gt.ap[:, kc:kc + 1],
                                                          scalar2=float(sign), op0=ALU.mult, op1=ALU.mult),
                         reads=rd + gt.r(), writes=ob.r())
            P.dma(dst[:, kc * NC:(kc + 1) * NC], oap, reads=ob.r(), writes=[dres])

    def rot_segs(base, nheads, hd):
        half = hd // 2
        s = []
        for h in range(nheads):
            s.append((base + h * hd + half, half, -1))
            s.append((base + h * hd, half, 1))
        return s

    def ops_p1a(st, o):
        a = st[0]
        return [(a[:, 256:416], o[:, 0:160], 1), (a[:, 400:416], o[:, 160:176], -1),
                (a[:, 384:400], o[:, 176:192], 1), (a[:, 0:256], o[:, 192:448], 1)]
    custom_pack("p1a", w_in, 8, 448, [(0, 416)], ops_p1a, g_pre_mix)
    pP1A = packs["p1a"][0]
    pP1T = build_pack("p1t", w_in, 128, 8, [(OFF["rk"], 512, 1), (OFF["rv"], 1024, 1)], g_pre_mix)
    def ops_p2f(st, o):
        a = st[0]
        rq3 = a[:, 0:512].rearrange("p (h c) -> p h c", h=8)
        rk3 = a[:, 512:1024].rearrange("p (h c) -> p h c", h=8)
        qa4 = o[:, 0:1024].rearrange("p (h d c) -> p h d c", h=8, d=2)
        qb5 = o[:, 1024:2048].rearrange("p (h d a c) -> p h d a c", h=8, d=2, a=2)
        kb4 = o[:, 2560:3072].rearrange("p (h a c) -> p h a c", h=8, a=2)
        res = []
        for d in range(2):
            res.append((rq3, qa4[:, :, d, :], 1))
            res.append((rq3[:, :, 32:64], qb5[:, :, d, 0, :], -1))
            res.append((rq3[:, :, 0:32], qb5[:, :, d, 1, :], 1))
        res.append((a[:, 512:1024], o[:, 2048:2560], 1))
        res.append((rk3[:, :, 32:64], kb4[:, :, 0, :], -1))
        res.append((rk3[:, :, 0:32], kb4[:, :, 1, :], 1))
        return res
    custom_pack("p2f", w_in, 8, 3072, [(OFF["rq"], 1024)], ops_p2f, g_pre_mix)
    pP2F = packs["p2f"][0]
    pP2T = build_pack("p2t", w_in, 128, 8, [(OFF["rv"], 1024, 1), (OFF["rg"], 1024, 1)], g_pre_mix)
    def ops_pg(st, o):
        o4 = o.rearrange("p (c g n) -> p c g n", c=8, g=2)
        return [(st[0].rearrange("p (c n) -> p c n", c=8), o4[:, :, 0, :], 1),
                (st[1].rearrange("p (c n) -> p c n", c=8), o4[:, :, 1, :], 1)]
    custom_pack("pg", w_in, 8, 2048, [(OFF["ga"], 1024), (OFF["gb"], 1024)], ops_pg, g_pre_mix)
    pPG = packs["pg"][0]
    pPB = build_pack("pb", w_b, 128, 8, [(0, 1024, 1)], None)
    pPA = build_pack("pa", w_a, 64, 8, [(0, 1024, 1)], None)
    pPO = build_pack("po", w_out, 128, 8, [(0, 1024, 1)], None)
    pPU = build_pack("pu", w_up, 128, 8, [(0, 4096, 1)], g_pre_mlp)
    pPD = build_pack("pd", w_down, 128, 32, [(0, 1024, 1)], None)
    def ops_wq(st, o):
        a3 = st[0][:, 0:768].rearrange("p (h c) -> p h c", h=8)
        o3 = o.rearrange("p (h c) -> p h c", h=8)
        return [(a3, o3[:, :, 0:96], 1), (a3[:, :, 80:96], o3[:, :, 96:112], -1), (a3[:, :, 64:80], o3[:, :, 112:128], 1)]
    custom_pack("wq", w_q_up, 2, 1024, [(0, 768)], ops_wq, g_q_norm)
    pWQ = packs["wq"][0]
    pWKV = build_pack("wkv", w_kv_up, 128, 1, [(0, 1024, 1)], g_kv_norm)
    emit_custom()
    emit_chunks()
    P.dma(WQ.ap, pWQ, reads=[packs["wq"][1]], writes=WQ.r())
    P.dma(WKV.ap, pWKV, reads=[packs["wkv"][1]], writes=WKV.r())

    class WStream:
        def __init__(self, seq):
            self.seq = seq
            self.issued = 0
            self.free = [0, 1, 2, 3]
            self.slot_of = {}
            self.pos = 0

        def _issue(self):
            while self.free and self.issued < len(self.seq):
                s = self.free.pop(0)
                name, parts = self.seq[self.issued]
                for (dst_fn, src, res) in parts:
                    P.dma(dst_fn(RING[s]), src, reads=[res], writes=RING[s].r())
                self.slot_of[self.issued] = s
                self.issued += 1

        def get(self, name):
            i = self.pos
            assert self.seq[i][0] == name, (self.seq[i][0], name)
            if i not in self.slot_of:
                self._issue()
            assert i in self.slot_of, "ring exhausted at %s" % name
            self.pos += 1
            return i, RING[self.slot_of[i]]

        def release(self, i):
            self.free.append(self.slot_of.pop(i))
            self._issue()

    def wt3(pack, kc_n, NC, c0, n):
        src = packs[pack][0].rearrange("p (k c) -> p k c", k=kc_n)[:, :, c0:c0 + n]
        return (lambda rb: rb.ap[:, 0:kc_n * n].rearrange("p (k c) -> p k c", k=kc_n), src, packs[pack][1])

    seq = []
    for j in range(NJ):
        seq.append(("p1a", [wt3("p1a", 8, 448, 0, 448)]))
        seq.append(("p1k", [wt3("p1t", 8, 1536, 0, 512)]))
        seq.append(("p1v0", [wt3("p1t", 8, 1536, 512, 512)]))
        seq.append(("p1v1", [wt3("p1t", 8, 1536, 1024, 512)]))
        for b in range(NB if cfg.get("stop", 9) >= 4 else 0):
            for i, nm in enumerate(("rqa0", "rqa1", "rqb0", "rqb1", "rka", "rkb")):
                seq.append((nm, [wt3("p2f", 8, 3072, i * 512, 512)]))
            for i, nm in enumerate(("rv0", "rv1", "rg0", "rg1")):
                seq.append((nm, [wt3("p2t", 8, 2048, i * 512, 512)]))
            for c in range(8):
                srcG = pPG.rearrange("p (k c) -> p k c", k=8)[:, :, c * 256:(c + 1) * 256]
                srcB = pPB.rearrange("p (k c) -> p k c", k=8)[:, :, c * 128:(c + 1) * 128]
                srcA = pPA.rearrange("p (k c) -> p k c", k=8)[:, :, c * 128:(c + 1) * 128]
                seq.append(("mix%d" % c, [
                    (lambda rb: rb.ap[:, 0:2048].rearrange("p (k c) -> p k c", k=8), srcG, packs["pg"][1]),
                    (lambda rb: rb.ap[:, 2048:3072].rearrange("p (k c) -> p k c", k=8), srcB, packs["pb"][1]),
                    (lambda rb: rb.ap[0:64, 3072:4096].rearrange("p (k c) -> p k c", k=8), srcA, packs["pa"][1]),
                ]))
            for i in range(2):
                seq.append(("wo%d" % i, [wt3("po", 8, 1024, i * 512, 512)]))
            for i in range(8):
                seq.append(("wu%d" % i, [wt3("pu", 8, 4096, i * 512, 512)]))
            for half in range(2):
                for g in range(4):
                    src = pPD.rearrange("p (k c) -> p k c", k=32)[:, g * 8:(g + 1) * 8, half * 512:(half + 1) * 512]
                    seq.append(("wd%d_%d" % (half, g), [
                        (lambda rb: rb.ap[:, 0:4096].rearrange("p (k c) -> p k c", k=8), src, packs["pd"][1])]))
    WS = WStream(seq)

    KB = 1024
    NKEYMAX = MAXKT * 128
    o = 0
    LAT = AR(o, NKEYMAX * 2, BF16); o += NKEYMAX * 2
    KT = AR(o, NKEYMAX * 2, BF16); o += NKEYMAX * 2
    CQT = AR(o, 2 * S_OWN * 2, BF16); o += 2 * S_OWN * 2
    OV = o
    o = OV
    VH_SZ = ((MAXKT * 65 * 2 + 511) // 512) * 512
    VH = AR(o, VH_SZ, BF16); o += VH_SZ
    QH = AR(o, S_OWN * 2, BF16); o += S_OWN * 2
    TBM = [AR(o + i * 4096, 4096, F32) for i in range(2)]; o += 8192
    PT = [AR(o + i * 1024, 1024, BF16) for i in range(3)]; o += 3072
    T1 = AR(o, 2048, F32); o += 2048
    T2 = AR(o, 2048, F32); o += 2048
    BCS = AR(o, 2048, F32); o += 2048
    RS = AR(o, 2048, F32); o += 2048
    PHA_END = o
    assert PHA_END <= ARENA, PHA_END
    o = OV
    XT = [AR(o + i * 4096, 4096, F32) for i in range(2)]; o += 8192
    XNd = [AR(o + i * 2048, 2048, BF16) for i in range(2)]; o += 4096
    XNT1d = [AR(o + i * 2048, 2048, BF16) for i in range(2)]; o += 4096
    SQ = AR(o, 1024, BF16); o += 1024
    RBC = AR(o, 1024, F32); o += 1024
    TTT = [AR(o + i * 512, 512, F32) for i in range(2)]; o += 1024
    TMT = [AR(o + i * 1024, 1024, F32) for i in range(2)]; o += 2048
    K1 = AR(o, 512, F32); o += 512
    K2 = AR(o, 512, F32); o += 512
    RA = AR(o, 1024, F32); o += 1024
    RB_ = AR(o, 1024, F32); o += 1024
    RKR = AR(o, 2048, F32); o += 2048
    KS = AR(o, 2048, BF16); o += 2048
    RV1d = [AR(o + i * 2048, 2048, BF16) for i in range(2)]; o += 4096
    RKC = [AR(o + i * 1024, 1024, BF16) for i in range(2)]; o += 2048
    TKV = AR(o, 4096, F32); o += 4096
    assert o <= ARENA, o
    o = 0
    X2 = [AR(o + i * 4096, 4096, F32) for i in range(2)]; o += 8192
    XN2 = AR(o, 2048, BF16); o += 2048
    ST2 = AR(o, 512, F32); o += 512
    XNT = AR(o, 8192, BF16); o += 8192
    RQT = AR(o, 8192, BF16); o += 8192
    RKT = AR(o, 8192, BF16); o += 8192
    UTA = LB(ar_t, ar_reg, RQT.off, 16384, BF16)
    RVB = AR(o, 8192, BF16); o += 8192
    SRG = AR(o, 8192, BF16); o += 8192
    X1 = LB(ar_t, ar_reg, RVB.off, 16384, F32)
    TA = AR(o, 4096, F32); o += 4096
    UTB_OFF = o
    TB = AR(o, 4096, F32); o += 4096
    SMB = [AR(o + i * 1024, 1024, BF16) for i in range(2)]; o += 2048
    QP = [AR(o + i * 1024, 1024, BF16) for i in range(2)]; o += 2048
    OB = XN2
    OBT = AR(o, 8192, BF16); o += 8192
    MT = AR(o, 8192, BF16); o += 8192
    YF0 = LB(ar_t, ar_reg, MT.off, 8192, F32)
    TF = X2[0]
    JUNK2 = AR(o, 1024, BF16); o += 1024
    assert o <= ARENA, o
    UTB = AR(UTB_OFF, 16384, BF16)
    assert UTB_OFF + 16384 <= MT.off
    RL = [LB(ar_t, ar_reg, X2[0].off + i * 1024, 1024, BF16) for i in range(4)]
    OUTT = X2[1]

    def norm_x_tile(xt, xn, stat_col, src_reads):
        ssap = STAT.ap[:, stat_col:stat_col + 1]
        sr = STAT.r(stat_col * 4, stat_col * 4 + 4) if False else STAT.r()
        P.op("act", lambda e: e.activation(out=xn.ap, in_=xt.ap, func=AF.Square, accum_out=ssap),
             reads=xt.r(), writes=xn.r() + STAT.r())
        act_rstd(ssap, ssap, D, STAT.r(), STAT.r())
        P.op("dve", lambda e: e.tensor_scalar(out=xn.ap, in0=xt.ap, scalar1=ssap, scalar2=None,
                                              op0=ALU.mult),
             reads=xt.r() + STAT.r(), writes=xn.r())


    def transpose_to(xn, dst_ap3, dst_reads_writes, bank, evac="dve"):
        pb = PSbf(bank)

        def tr(e):
            for kc in range(8):
                ins = e.transpose(out=pb[:, kc * 128:(kc + 1) * 128], in_=xn.ap[:, kc * 128:(kc + 1) * 128],
                                  identity=IDENT.ap)
            return ins
        P.op("pe", tr, reads=xn.r() + IDENT.r(), writes=PSR(bank))
        if evac == "act":
            P.op("act", lambda e: e.copy(out=dst_ap3, in_=pb.rearrange("p (k c) -> p k c", k=8)),
                 reads=PSR(bank), writes=dst_reads_writes)
        else:
            P.op("dve", lambda e: e.tensor_copy(out=dst_ap3, in_=pb.rearrange("p (k c) -> p k c", k=8)),
                 reads=PSR(bank), writes=dst_reads_writes)

    key_off = 0
    ctx_off = 0
    LATv = LAT.ap
    KTv = KT.ap
    CQTv = CQT.v("p (k s) -> p k s", k=2)
    STv = ST.v("p (n h v) -> p n h v", n=NT, h=8)
    OTv = OT.v("p (h s) -> p h s", h=8)
    ACCv = ACC.v("p (h v) -> p h v", h=8)
    COEFv = COEF.v("p (m h) -> p m h", h=8)
    DECb = DEC.ap.unsqueeze(2).broadcast_to([128, 8, 128])

    STOP = cfg.get("stop", 9)
    for j in range(NJ):
        if STOP == 0:
            break
        nctx = NCTX[j]
        nkt = NKT[j]
        NKEY = nkt * 128
        own0 = j * S_OWN
        P.op("pool", lambda e: e.memset(ACC.ap, 0.0), writes=ACC.r())
        if nctx > 0:
            P.dma(CDT.ap[:, 0:nctx], cD[:, ctx_off:ctx_off + nctx], writes=CDT.r())
            P.dma(CFT.ap[:, 0:nctx], cF[:, ctx_off:ctx_off + nctx], writes=CFT.r())
            P.op("dve", lambda e, nctx=nctx: e.tensor_tensor(
                out=COEFv[:, 0:nctx, :], in0=CDT.ap[:, 0:nctx].unsqueeze(2).broadcast_to([128, nctx, 8]),
                in1=LG.ap.unsqueeze(1).broadcast_to([128, nctx, 8]), op=ALU.mult),
                reads=CDT.r() + LG.r(), writes=COEF.r())
            P.op("act", lambda e, nctx=nctx: e.activation(out=COEFv[:, 0:nctx, :], in_=COEFv[:, 0:nctx, :],
                                                          func=AF.Exp),
                 reads=COEF.r(), writes=COEF.r())
            P.op("dve", lambda e, nctx=nctx: e.tensor_tensor(
                out=COEFv[:, 0:nctx, :], in0=COEFv[:, 0:nctx, :],
                in1=CFT.ap[:, 0:nctx].unsqueeze(2).broadcast_to([128, nctx, 8]), op=ALU.mult),
                reads=COEF.r() + CFT.r(), writes=COEF.r())

        iA, WA_ = WS.get("p1a")
        iK, WK_ = WS.get("p1k")
        iV0, WV0 = WS.get("p1v0")
        iV1, WV1 = WS.get("p1v1")
        wA = WA_.ap[:, 0:8 * 448].rearrange("p (k c) -> p k c", k=8)
        wK = WK_.ap.rearrange("p (k c) -> p k c", k=8)
        wV = [WV0.ap.rearrange("p (k c) -> p k c", k=8), WV1.ap.rearrange("p (k c) -> p k c", k=8)]

        def x_src(kt):
            if kt < NT:
                return xo[own0 + kt * 128: own0 + (kt + 1) * 128, :]
            m = ctx_off + (kt - NT)
            return xc[m * 128:(m + 1) * 128, :]

        def p1_xload(kt):
            P.dma(XT[kt % 2].ap, x_src(kt), writes=XT[kt % 2].r())

        def p1_stageA(kt):
            norm_x_tile(XT[kt % 2], XNd[kt % 2], kt % 2, None)
            transpose_to(XNd[kt % 2], XNT1d[kt % 2].v("p (k c) -> p k c", k=8), XNT1d[kt % 2].r(), 0)

        def p1_load(kt):
            tk = key_off + kt * 128
            P.dma(TTT[kt % 2].ap[:, 0:64], tabT[tk:tk + 128, :], writes=TTT[kt % 2].r())
            P.dma(TMT[kt % 2].ap[64:96, :].rearrange("p (a c) -> p a c", a=2), tabM[:, :, tk:tk + 128],
                  writes=TMT[kt % 2].r())

        def p1_B(kt, phase):
            own = kt < NT
            tmt = TMT[kt % 2].ap[64:96, :].rearrange("p (a c) -> p a c", a=2)
            kc0 = kt * 128
            XNT1 = XNT1d[kt % 2]
            XNT1v = XNT1.v("p (k c) -> p k c", k=8)
            def fm(e, own=own):
                groups = [(0, 128, PS(1)[:, 0:128]), (64, 96, PS(2)[0:96, 0:128]), (96, 96, PS(2)[0:96, 128:256])]
                if own:
                    groups += [(192, 128, PS(1)[:, 128:256]), (320, 128, PS(1)[:, 256:384])]
                for (c0, m, out) in groups:
                    for kc in range(8):
                        ins = e.matmul(out, lhsT=wA[:, kc, c0:c0 + m], rhs=XNT1v[:, kc, :],
                                       start=(kc == 0), stop=(kc == 7))
                return ins
            if phase == 0:
                P.op("pe", fm, reads=XNT1.r() + WA_.r(), writes=PSR(1) + PSR(2))
            def tm(e):
                for (bank, w) in ((3, wK), (4, wV[0]), (5, wV[1])):
                    for kc in range(8):
                        ins = e.matmul(PS(bank)[:, :], lhsT=XNT1v[:, kc, :], rhs=w[:, kc, :],
                                       start=(kc == 0), stop=(kc == 7))
                return ins
            if phase == 0:
                P.op("pe", tm, reads=XNT1.r() + WK_.r() + WV0.r() + WV1.r(), writes=PSR(3) + PSR(4) + PSR(5))
                return
            rkc = RKC[kt % 2]
            rv1 = RV1d[kt % 2]
            P.op("act", lambda e: e.copy(out=rkc.ap, in_=PS(3)[:, :]), reads=PSR(3), writes=rkc.r())
            P.op("act", lambda e: e.copy(out=rv1.ap[:, 0:512], in_=PS(4)[:, :]), reads=PSR(4), writes=rv1.r())
            P.op("act", lambda e: e.copy(out=rv1.ap[:, 512:1024], in_=PS(5)[:, :]), reads=PSR(5), writes=rv1.r())
            nsq = 3 if own else 1
            SQv = SQ.ap[:, 0:384].rearrange("p (a c) -> p a c", a=3)
            P.op("act", lambda e, nsq=nsq: e.activation(out=SQv[:, 0:nsq, :],
                                                        in_=PS(1)[:, 0:nsq * 128].rearrange("p (a c) -> p a c", a=nsq),
                                                        func=AF.Square),
                 reads=PSR(1), writes=SQ.r())

            def ssmm(e, own=own):
                ins = e.matmul(PS(2)[:, 256:384], lhsT=ONESB.ap, rhs=SQv[:, 0, :], start=True, stop=True)
                if own:
                    e.matmul(PS(2)[:, 384:512], lhsT=ONESB.ap, rhs=SQv[:, 1, :], start=True, stop=False)
                    ins = e.matmul(PS(2)[:, 384:512], lhsT=ONESB.ap, rhs=SQv[:, 2, :], start=False, stop=True)
                return ins
            P.op("pe", ssmm, reads=SQ.r() + ONESB.r(), writes=PSR(2))
            RBCv = RBC.v("p (a c) -> p a c", a=2)
            P.op("act", lambda e: e.activation(out=RBCv[:, 0, :], in_=PS(2)[:, 256:384], func=AF.Ln,
                                               scale=1.0 / 128, bias=EPST.ap[:, 0:1]),
                 reads=PSR(2) + EPST.r(), writes=RBC.r())
            if own:
                P.op("act", lambda e: e.activation(out=RBCv[:, 1, :], in_=PS(2)[:, 384:512], func=AF.Ln,
                                                   scale=1.0 / 256, bias=EPST.ap[:, 0:1]),
                     reads=PSR(2) + EPST.r(), writes=RBC.r())
            na = 2 if own else 1
            P.op("act", lambda e, na=na: e.activation(out=RBCv[:, 0:na, :], in_=RBCv[:, 0:na, :],
                                                      func=AF.Exp, scale=-0.5),
                 reads=RBC.r(), writes=RBC.r())
            P.op("dve", lambda e, kc0=kc0: e.tensor_tensor(out=LATv[:, kc0:kc0 + 128], in0=PS(1)[:, 0:128],
                                                           in1=RBCv[:, 0, :], op=ALU.mult),
                 reads=PSR(1) + RBC.r(), writes=LAT.r(kc0, kc0 + 128))
            if own:
                P.op("dve", lambda e, kc0=kc0: e.tensor_tensor(
                    out=CQTv[:, :, kc0:kc0 + 128], in0=PS(1)[:, 128:384].rearrange("p (a c) -> p a c", a=2),
                    in1=RBCv[:, 1, :].unsqueeze(1).broadcast_to([128, 2, 128]), op=ALU.mult),
                    reads=PSR(1) + RBC.r(), writes=CQT.r(kc0, kc0 + 128) + CQT.r(S_OWN + kc0, S_OWN + kc0 + 128))
            P.op("dve", lambda e, tmt=tmt: e.tensor_tensor(out=K1.ap[64:96, :], in0=PS(2)[64:96, 0:128],
                                                           in1=tmt[:, 0, :], op=ALU.mult),
                 reads=PSR(2) + TMT[kt % 2].r(), writes=K1.r())
            P.op("dve", lambda e, tmt=tmt: e.tensor_tensor(out=K2.ap[64:96, :], in0=PS(2)[64:96, 128:256],
                                                           in1=tmt[:, 1, :], op=ALU.mult),
                 reads=PSR(2) + TMT[kt % 2].r(), writes=K2.r())
            P.op("dve", lambda e, kc0=kc0: e.tensor_tensor(out=KTv[64:96, kc0:kc0 + 128], in0=K1.ap[64:96, :],
                                                            in1=K2.ap[64:96, :], op=ALU.add),
                 reads=K1.r() + K2.r(), writes=KT.r(kc0, kc0 + 128, half=1))

        def p1_C(kt):
            own = kt < NT
            ttt = TTT[kt % 2]
            rkc = RKC[kt % 2]
            rv1 = RV1d[kt % 2]
            rk4 = rkc.ap.rearrange("p (h a c) -> p h a c", h=8, a=2)
            cosb = ttt.ap[:, 0:32].unsqueeze(1).broadcast_to([128, 8, 32])
            sinb = ttt.ap[:, 32:64].unsqueeze(1).broadcast_to([128, 8, 32])
            RAv = RA.v("p (h c) -> p h c", h=8)
            RBv = RB_.v("p (h c) -> p h c", h=8)
            RKRv = RKR.v("p (h a c) -> p h a c", h=8, a=2)
            P.op("dve", lambda e: e.tensor_tensor(out=RAv, in0=rk4[:, :, 0, :], in1=cosb, op=ALU.mult),
                 reads=rkc.r() + ttt.r(), writes=RA.r())
            P.op("dve", lambda e: e.tensor_tensor(out=RBv, in0=rk4[:, :, 1, :], in1=sinb, op=ALU.mult),
                 reads=rkc.r() + ttt.r(), writes=RB_.r())
            P.op("dve", lambda e: e.tensor_tensor(out=RKRv[:, :, 0, :], in0=RAv, in1=RBv, op=ALU.subtract),
                 reads=RA.r() + RB_.r(), writes=RKR.r())
            P.op("dve", lambda e: e.tensor_tensor(out=RAv, in0=rk4[:, :, 0, :], in1=sinb, op=ALU.mult),
                 reads=rkc.r() + ttt.r() + RA.r(), writes=RA.r())
            P.op("dve", lambda e: e.tensor_tensor(out=RBv, in0=rk4[:, :, 1, :], in1=cosb, op=ALU.mult),
                 reads=rkc.r() + ttt.r() + RB_.r(), writes=RB_.r())
            P.op("dve", lambda e: e.tensor_tensor(out=RKRv[:, :, 1, :], in0=RAv, in1=RBv, op=ALU.add),
                 reads=RA.r() + RB_.r() + RKR.r(), writes=RKR.r())
            KSv = KS.v("p (h c) -> p h c", h=8)
            RKR3 = RKR.v("p (h c) -> p h c", h=8)
            for a in range(2):
                P.op("dve", lambda e, a=a: e.tensor_tensor(
                    out=KSv[:, :, a * 64:(a + 1) * 64], in0=RKR3,
                    in1=KDv[:, a, :].unsqueeze(2).broadcast_to([128, 8, 64]), op=ALU.mult),
                    reads=RKR.r() + KDEC.r(), writes=KS.r())

            def kvmm(e):
                for h in range(8):
                    ins = e.matmul(PS(6 + h // 4)[:, (h % 4) * 128:(h % 4 + 1) * 128], lhsT=KSv[:, h, :],
                                   rhs=rv1.ap[:, h * 128:(h + 1) * 128], start=True, stop=True)
                return ins
            P.op("pe", kvmm, reads=KS.r() + rv1.r(), writes=PSR(6) + PSR(7))
            if own:
                for hh in range(2):
                    P.op("dve", lambda e, hh=hh, kt=kt: e.tensor_copy(
                        out=STv[:, kt, hh * 4:(hh + 1) * 4, :],
                        in_=PS(6 + hh)[:, :].rearrange("p (h v) -> p h v", h=4)),
                        reads=PSR(6 + hh), writes=ST.r(kt * 1024, (kt + 1) * 1024))
            else:
                m = kt - NT
                TKVv = TKV.v("p (h v) -> p h v", h=8)
                for hh in range(2):
                    P.op("dve", lambda e, hh=hh, m=m: e.tensor_tensor(
                        out=TKVv[:, hh * 4:(hh + 1) * 4, :], in0=PS(6 + hh)[:, :].rearrange("p (h v) -> p h v", h=4),
                        in1=COEFv[:, m, hh * 4:(hh + 1) * 4].unsqueeze(2).broadcast_to([128, 4, 128]), op=ALU.mult),
                        reads=PSR(6 + hh) + COEF.r(), writes=TKV.r())
                P.op("dve", lambda e: e.tensor_tensor(out=ACC.ap, in0=ACC.ap, in1=TKV.ap, op=ALU.add),
                     reads=TKV.r() + ACC.r(), writes=ACC.r())

        p1_xload(0)
        if nkt > 1:
            p1_xload(1)
        p1_stageA(0)
        p1_load(0)
        p1_B(0, 0)
        p1_B(0, 1)
        for kt in range(nkt):
            if kt + 2 < nkt:
                p1_xload(kt + 2)
            if kt + 1 < nkt:
                p1_stageA(kt + 1)
                p1_load(kt + 1)
                p1_B(kt + 1, 0)
            p1_C(kt)
            if kt + 1 < nkt:
                p1_B(kt + 1, 1)
        WS.release(iA); WS.release(iK); WS.release(iV0); WS.release(iV1)

        if STOP == 1:
            key_off += nkt * 128
            ctx_off += nctx
            continue
        TKVv = TKV.v("p (h v) -> p h v", h=8)
        accs = [(ACC, ACCv), (TKV, TKVv)]
        for i_ in range(NT):
            ca, cav = accs[i_ % 2]
            cb, cbv = accs[(i_ + 1) % 2]
            for (half, n) in ((0, i_), (1, NT - 1 - i_)):
                p0, p1 = half * 64, half * 64 + 64
                rs = ST.r(n * 1024, (n + 1) * 1024, half=half)
                P.op("dve", lambda e, p0=p0, p1=p1, cav=cav, cbv=cbv: e.tensor_tensor(
                    out=cbv[p0:p1], in0=cav[p0:p1], in1=DECb[p0:p1], op=ALU.mult),
                    reads=ca.r(half=half) + DEC.r(), writes=cb.r(half=half))
                P.op("dve", lambda e, p0=p0, p1=p1, n=n, cbv=cbv: e.tensor_tensor(
                    out=cbv[p0:p1], in0=cbv[p0:p1], in1=STv[p0:p1, n], op=ALU.add),
                    reads=cb.r(half=half) + rs, writes=cb.r(half=half))
                P.op("act", lambda e, p0=p0, p1=p1, n=n, ca=ca: e.copy(
                    out=ST.ap[p0:p1, n * 1024:(n + 1) * 1024], in_=ca.ap[p0:p1, :]),
                    reads=ca.r(half=half), writes=rs)

        if STOP == 2:
            key_off += nkt * 128
            ctx_off += nctx
            continue
        VHv = VH.ap[:, 0:MAXKT * 65].rearrange("p (c d) -> p c d", d=65)
        P.op("pool", lambda e: e.memset(VH.ap, 1.0), writes=VH.r())
        WQv = WQ.v("p (k h c) -> p k h c", k=2, h=8)
        WKVv = WKV.v("p (h c) -> p h c", h=8)
        nkb = (NKEY + 511) // 512
        for h in range(8):
            for kb in range(nkb):
                c0 = kb * 512
                n = min(512, NKEY - c0)
                bank = kb % 2
                P.op("pe", lambda e, c0=c0, n=n, bank=bank, h=h: e.matmul(
                    PS(bank)[0:64, 0:n], lhsT=WKVv[:, h, 0:64], rhs=LATv[:, c0:c0 + n], start=True, stop=True),
                    reads=WKV.r() + LAT.r(c0, c0 + n), writes=PSR(bank))
                if kb % 2:
                    P.op("act", lambda e, c0=c0, n=n, bank=bank: e.copy(out=KTv[0:64, c0:c0 + n],
                                                                        in_=PS(bank)[0:64, 0:n]),
                         reads=PSR(bank), writes=KT.r(c0, c0 + n, half=0))
                else:
                    P.op("dve", lambda e, c0=c0, n=n, bank=bank: e.tensor_copy(out=KTv[0:64, c0:c0 + n],
                                                                               in_=PS(bank)[0:64, 0:n]),
                         reads=PSR(bank), writes=KT.r(c0, c0 + n, half=0))
            for g in range((nkt + 7) // 8):
                cc = list(range(g * 8, min(nkt, g * 8 + 8)))
                bank = 2 + g % 2

                def vmm(e, cc=cc, bank=bank, h=h):
                    for i, c in enumerate(cc):
                        ins = e.matmul(PS(bank)[:, i * 64:(i + 1) * 64], lhsT=LATv[:, c * 128:(c + 1) * 128],
                                       rhs=WKVv[:, h, 64:128], start=True, stop=True)
                    return ins
                P.op("pe", vmm, reads=WKV.r() + LAT.r(cc[0] * 128, (cc[-1] + 1) * 128), writes=PSR(bank))
                ncc = len(cc)
                P.op("dve" if g % 2 else "act",
                     (lambda e, cc=cc, bank=bank, ncc=ncc: e.tensor_copy(
                         out=VHv[:, cc[0]:cc[0] + ncc, 0:64],
                         in_=PS(bank)[:, 0:ncc * 64].rearrange("p (c d) -> p c d", d=64))) if g % 2 else
                     (lambda e, cc=cc, bank=bank, ncc=ncc: e.copy(
                         out=VHv[:, cc[0]:cc[0] + ncc, 0:64],
                         in_=PS(bank)[:, 0:ncc * 64].rearrange("p (c d) -> p c d", d=64))),
                     reads=PSR(bank), writes=VH.r())
            for b in range(NB):
                s0 = b * 512
                tb = TBM[b % 2]
                tbv = tb.ap[64:96, :].rearrange("p (a c) -> p a c", a=2)
                P.dma(tbv, tabM[:, :, key_off + s0: key_off + s0 + 512], writes=tb.r())

                def qmm(e, s0=s0, h=h):
                    for kc in range(2):
                        e.matmul(PS(4)[0:96, :], lhsT=WQv[:, kc, h, 0:96], rhs=CQTv[:, kc, s0:s0 + 512],
                                 start=(kc == 0), stop=(kc == 1))
                    for kc in range(2):
                        ins = e.matmul(PS(5)[0:96, :], lhsT=WQv[:, kc, h, 32:128], rhs=CQTv[:, kc, s0:s0 + 512],
                                       start=(kc == 0), stop=(kc == 1))
                    return ins
                P.op("pe", qmm, reads=WQ.r() + CQT.r(), writes=PSR(4) + PSR(5))
                P.op("dve", lambda e, tbv=tbv: e.tensor_tensor(out=T1.ap[64:96, :], in0=PS(4)[64:96, :],
                                                               in1=tbv[:, 0, :], op=ALU.mult),
                     reads=PSR(4) + tb.r(), writes=T1.r())
                P.op("dve", lambda e, tbv=tbv: e.tensor_tensor(out=T2.ap[64:96, :], in0=PS(5)[64:96, :],
                                                               in1=tbv[:, 1, :], op=ALU.mult),
                     reads=PSR(5) + tb.r(), writes=T2.r())
                P.op("dve", lambda e, s0=s0: e.tensor_tensor(out=QH.ap[64:96, s0:s0 + 512], in0=T1.ap[64:96, :],
                                                              in1=T2.ap[64:96, :], op=ALU.add),
                     reads=T1.r() + T2.r(), writes=QH.r(s0, s0 + 512, half=1))
                P.op("act", lambda e, s0=s0: e.copy(out=QH.ap[0:64, s0:s0 + 512], in_=PS(4)[0:64, :]),
                     reads=PSR(4), writes=QH.r(s0, s0 + 512, half=0))
            LA = 2
            steps = [(b, c) for b in range(NB) for c in range(nkt)]
            deferred = []

            def score_exp(idx, h=h):
                b, c = steps[idx]
                s0 = b * 512
                sbank = idx % 3
                pt = PT[idx % 3]
                P.op("pe", lambda e: e.matmul(
                    PS(sbank)[:, :], lhsT=KTv[0:96, c * 128:(c + 1) * 128], rhs=QH.ap[0:96, s0:s0 + 512],
                    start=True, stop=True),
                    reads=KT.r(c * 128, (c + 1) * 128) + QH.r(s0, s0 + 512), writes=PSR(sbank))
                P.op("act", lambda e: e.activation(out=pt.ap, in_=PS(sbank)[:, :], func=AF.Exp, scale=ATT_SCALE),
                     reads=PSR(sbank), writes=pt.r())

            def pv(idx, h=h):
                b, c = steps[idx]
                pt = PT[idx % 3]
                obank = 6 + b % 2
                P.op("pe", lambda e: e.matmul(
                    PS(obank)[0:65, :], lhsT=VHv[:, c, 0:65], rhs=pt.ap, start=(c == 0), stop=(c == nkt - 1)),
                    reads=VH.r() + pt.r(), writes=PSR(obank))
                if c == nkt - 1:
                    deferred.append((idx + LA + 2, b))

            def epilogue(b, h=h):
                s0 = b * 512
                obank = 6 + b % 2
                P.op("dve", lambda e: e.reciprocal(out=RS.ap[64:65, :], in_=PS(obank)[64:65, :]),
                     reads=PSR(obank), writes=RS.r())
                P.op("pe", lambda e: e.matmul(PS(3)[0:64, :], lhsT=ONESF.ap[64:65, 0:64], rhs=RS.ap[64:65, :],
                                              start=True, stop=True),
                     reads=RS.r() + ONESF.r(), writes=PSR(3))
                P.op("dve", lambda e: e.tensor_copy(out=BCS.ap[0:64, :], in_=PS(3)[0:64, :]), reads=PSR(3),
                     writes=BCS.r())
                P.op("dve", lambda e: e.tensor_tensor(
                    out=OTv[0:64, h, s0:s0 + 512], in0=PS(obank)[0:64, :], in1=BCS.ap[0:64, :], op=ALU.mult),
                    reads=PSR(obank) + BCS.r(), writes=OT.r(h * S_OWN + s0, h * S_OWN + s0 + 512))

            for i_ in range(len(steps) + LA):
                if i_ < len(steps):
                    score_exp(i_)
                if i_ >= LA:
                    pv(i_ - LA)
                while deferred and deferred[0][0] <= i_:
                    epilogue(deferred.pop(0)[1])
            while deferred:
                epilogue(deferred.pop(0)[1])

        if STOP == 3:
            key_off += nkt * 128
            ctx_off += nctx
            continue
        XNTv = XNT.v("p (k s) -> p k s", k=8)
        RQTv = RQT.v("p (a s) -> p a s", a=8)
        RKTv = RKT.v("p (a s) -> p a s", a=8)
        RVBv = RVB.v("p (t c) -> p t c", t=4)
        SRGv = SRG.v("p (t c) -> p t c", t=4)
        X1v = X1.v("p (t c) -> p t c", t=4)
        OBTv = OBT.v("p (k s) -> p k s", k=8)
        MTv = MT.v("p (k s) -> p k s", k=8)
        TFv = TF.v("p (a s) -> p a s", a=2)
        UTAv = UTA.v("p (k s) -> p k s", k=16)
        UTBv = UTB.v("p (k s) -> p k s", k=16)

        def UTc(ch):
            return (UTAv if ch < 16 else UTBv)[:, ch % 16, :]

        def UTr(ch):
            return (UTA if ch < 16 else UTB).r((ch % 16) * 512, (ch % 16 + 1) * 512)
        YF0v = YF0.v("p (t c) -> p t c", t=4)
        def p2_step1(b):
            g0 = own0 + b * 512
            for t in range(4):
                x2 = X2[t % 2]
                P.dma(x2.ap, xo[g0 + t * 128: g0 + (t + 1) * 128, :], writes=x2.r())
                ssap, ssr = SC(4 * t)
                P.op("act", lambda e: e.activation(out=TA.ap, in_=x2.ap, func=AF.Square, accum_out=ssap),
                     reads=x2.r(), writes=TA.r() + ssr)
                act_rstd(ssap, ssap, D, ssr, ssr)
                P.op("dve", lambda e: e.tensor_scalar(out=XN2.ap, in0=x2.ap, scalar1=ssap, scalar2=None,
                                                      op0=ALU.mult),
                     reads=x2.r() + ssr, writes=XN2.r())
                transpose_to(XN2, XNTv[:, :, t * 128:(t + 1) * 128], XNT.r(), 0, evac="act")

        for b in range(NB):
            s0 = b * 512
            g0 = own0 + s0
            if b == 0:
                p2_step1(0)
            if STOP == 4:
                break
            P.dma(TFv, tabF[:, :, g0:g0 + 512], writes=TF.r())

            def proj_rope(WAx, WBx, dstv, dst, hs, M, cw):
                wa = WAx.ap.rearrange("p (k c) -> p k c", k=8)
                wb = WBx.ap.rearrange("p (k c) -> p k c", k=8)
                for i_, h in enumerate(hs):
                    ba, bb = (h % 2) * 2, (h % 2) * 2 + 1

                    def mm(e, i_=i_, ba=ba, bb=bb):
                        for kc in range(8):
                            e.matmul(PS(ba)[0:M, :], lhsT=wa[:, kc, i_ * cw:i_ * cw + M], rhs=XNTv[:, kc, :],
                                     start=(kc == 0), stop=(kc == 7))
                        for kc in range(8):
                            ins = e.matmul(PS(bb)[0:M, :], lhsT=wb[:, kc, i_ * cw:i_ * cw + M], rhs=XNTv[:, kc, :],
                                           start=(kc == 0), stop=(kc == 7))
                        return ins
                    P.op("pe", mm, reads=XNT.r() + WAx.r() + WBx.r(), writes=PSR(ba) + PSR(bb))
                    P.op("dve", lambda e, ba=ba: e.tensor_tensor(out=TA.ap[0:M, 0:512], in0=PS(ba)[0:M, :],
                                                                 in1=TFv[0:M, 0, :], op=ALU.mult),
                         reads=PSR(ba) + TF.r(), writes=TA.r(0, 512))
                    P.op("dve", lambda e, bb=bb: e.tensor_tensor(out=TB.ap[0:M, 0:512], in0=PS(bb)[0:M, :],
                                                                 in1=TFv[0:M, 1, :], op=ALU.mult),
                         reads=PSR(bb) + TF.r(), writes=TB.r(0, 512))
                    P.op("dve", lambda e, h=h, dstv=dstv: e.tensor_tensor(out=dstv[0:M, h, :], in0=TA.ap[0:M, 0:512],
                                                                           in1=TB.ap[0:M, 0:512], op=ALU.add),
                         reads=TA.r(0, 512) + TB.r(0, 512), writes=dst.r(h * 512, (h + 1) * 512))
            iqa0, Wqa0 = WS.get("rqa0")
            iqa1, Wqa1 = WS.get("rqa1")
            iqb0, Wqb0 = WS.get("rqb0")
            iqb1, Wqb1 = WS.get("rqb1")
            proj_rope(Wqa0, Wqb0, RQTv, RQT, [0, 1, 2, 3], 128, 128)
            proj_rope(Wqa1, Wqb1, RQTv, RQT, [4, 5, 6, 7], 128, 128)
            WS.release(iqa0); WS.release(iqa1); WS.release(iqb0); WS.release(iqb1)
            ika, Wka = WS.get("rka")
            ikb, Wkb = WS.get("rkb")
            proj_rope(Wka, Wkb, RKTv, RKT, list(range(8)), 64, 64)
            WS.release(ika); WS.release(ikb)
            if STOP == 5:
                break
            for wi, nm in enumerate(("rv0", "rv1", "rg0", "rg1")):
                iw, Ww = WS.get(nm)
                wv = Ww.ap.rearrange("p (k c) -> p k c", k=8)
                hf = wi % 2
                for t in range(4):
                    bank = 4 + (t % 2)

                    def mm(e, t=t, bank=bank, wv=wv):
                        for kc in range(8):
                            ins = e.matmul(PS(bank)[:, :], lhsT=XNTv[:, kc, t * 128:(t + 1) * 128], rhs=wv[:, kc, :],
                                           start=(kc == 0), stop=(kc == 7))
                        return ins
                    P.op("pe", mm, reads=XNT.r() + Ww.r(), writes=PSR(bank))
                    if wi < 2:
                        P.op("act", lambda e, t=t, bank=bank, hf=hf: e.copy(out=RVBv[:, t, hf * 512:(hf + 1) * 512],
                                                                            in_=PS(bank)[:, :]),
                             reads=PSR(bank), writes=RVB.r(t * 1024 + hf * 512, t * 1024 + (hf + 1) * 512))
                    else:
                        P.op("act", lambda e, t=t, bank=bank, hf=hf: e.activation(
                            out=SRGv[:, t, hf * 512:(hf + 1) * 512], in_=PS(bank)[:, :], func=AF.Silu),
                            reads=PSR(bank), writes=SRG.r(t * 1024 + hf * 512, t * 1024 + (hf + 1) * 512))
                WS.release(iw)
            if STOP == 6:
                break
            if STOP >= 70:
                pass
            def ret_A(t, b=b):
                n = b * 4 + t
                tc0 = t * 128
                ob0 = 4 if t % 2 == 0 else 2
                for hg in range(2):
                    sb_ = hg
                    smb, qp = SMB[hg], QP[hg]
                    smv = smb.v("p (h c) -> p h c", h=4)
                    qpv = qp.v("p (h c) -> p h c", h=4)

                    def mm(e):
                        for hi in range(4):
                            h = hg * 4 + hi
                            ins = e.matmul(PS(sb_)[:, hi * 128:(hi + 1) * 128],
                                           lhsT=RKTv[0:64, h, tc0:tc0 + 128],
                                           rhs=RQTv[0:64, h, tc0:tc0 + 128], start=True, stop=True)
                        return ins
                    P.op("pe", mm, reads=RKT.r() + RQT.r(), writes=PSR(sb_))
                    P.op("dve", lambda e: e.tensor_tensor(
                        out=smv, in0=PS(sb_)[:, :].rearrange("p (h c) -> p h c", h=4),
                        in1=DTv[:, hg * 4:(hg + 1) * 4, :], op=ALU.mult),
                        reads=PSR(sb_) + DT_.r(), writes=smb.r())
                    P.op("dve", lambda e: e.tensor_tensor(
                        out=qpv, in0=RQTv[:, hg * 4:(hg + 1) * 4, tc0:tc0 + 128],
                        in1=QDv[:, hg * 4:(hg + 1) * 4, :], op=ALU.mult),
                        reads=RQT.r() + QDEC.r(), writes=qp.r())

                    def omm(e):
                        for hi in range(4):
                            h = hg * 4 + hi
                            e.matmul(PS(ob0 + hg)[:, hi * 128:(hi + 1) * 128], lhsT=smv[:, hi, :],
                                     rhs=RVBv[:, t, h * 128:(h + 1) * 128], start=True, stop=False)
                            ins = e.matmul(PS(ob0 + hg)[:, hi * 128:(hi + 1) * 128], lhsT=qpv[:, hi, :],
                                           rhs=STv[:, n, h, :], start=False, stop=True)
                        return ins
                    P.op("pe", omm, reads=smb.r() + qp.r() + RVB.r(t * 1024, (t + 1) * 1024) +
                         ST.r(n * 1024, (n + 1) * 1024), writes=PSR(ob0 + hg))

            def ret_B(t, b=b):
                tc0 = t * 128
                ob0 = 4 if t % 2 == 0 else 2
                TAv = TA.v("p (h v) -> p h v", h=8)
                for hg in range(2):
                    P.op("act", lambda e: e.activation(out=TB.ap[:, hg * 512:(hg + 1) * 512],
                                                       in_=PS(ob0 + hg)[:, :], func=AF.Square),
                         reads=PSR(ob0 + hg), writes=TB.r(hg * 512, (hg + 1) * 512))
                ss8, ss8r = SC(16 + 8 * (t % 2), 8)
                P.op("dve", lambda e: e.tensor_reduce(out=ss8, in_=TB.v("p (h v) -> p h v", h=8), axis=AX.X,
                                                      op=ALU.add),
                     reads=TB.r(), writes=ss8r)
                act_rstd(ss8, ss8, 128, ss8r, ss8r)
                for hg in range(2):
                    P.op("dve", lambda e: e.tensor_tensor(
                        out=TAv[:, hg * 4:(hg + 1) * 4, :], in0=PS(ob0 + hg)[:, :].rearrange("p (h v) -> p h v", h=4),
                        in1=ss8[:, hg * 4:(hg + 1) * 4].unsqueeze(2).broadcast_to([128, 4, 128]), op=ALU.mult),
                        reads=PSR(ob0 + hg) + ss8r, writes=TA.r(hg * 512, (hg + 1) * 512))
                P.op("dve", lambda e: e.tensor_tensor(out=OB.ap, in0=TA.ap, in1=SRGv[:, t, :], op=ALU.mult),
                     reads=TA.r() + SRG.r(t * 1024, (t + 1) * 1024), writes=OB.r())
                transpose_to(OB, OBTv[:, :, tc0:tc0 + 128], OBT.r(), 7, evac="act")

            ret_A(0)
            for t in range(4):
                if t + 1 < 4:
                    ret_A(t + 1)
                ret_B(t)
            if STOP in (7, 70, 71, 72, 73):
                break
            for c in range(8):
                im, Wm = WS.get("mix%d" % c)
                wG = Wm.ap[:, 0:2048].rearrange("p (k c) -> p k c", k=8)
                wB = Wm.ap[:, 2048:3072].rearrange("p (k c) -> p k c", k=8)
                wAh = Wm.ap[0:64, 3072:4096].rearrange("p (k c) -> p k c", k=8)
                bb = (c % 2) * 4

                def mm(e, wG=wG, wB=wB, wAh=wAh, bb=bb, s0=s0):
                    for kc in range(8):
                        e.matmul(PS(bb)[:, :], lhsT=wG[:, kc, 0:128], rhs=XNTv[:, kc, :], start=(kc == 0), stop=(kc == 7))
                    for kc in range(8):
                        e.matmul(PS(bb + 1)[:, :], lhsT=wG[:, kc, 128:256], rhs=XNTv[:, kc, :], start=(kc == 0),
                                 stop=(kc == 7))
                    for hh in range(8):
                        e.matmul(PS(bb + 2)[:, :], lhsT=wAh[:, hh, :], rhs=OTv[0:64, hh, s0:s0 + 512],
                                 start=(hh == 0), stop=(hh == 7))
                    for kc in range(8):
                        ins = e.matmul(PS(bb + 3)[:, :], lhsT=wB[:, kc, :], rhs=OBTv[:, kc, :], start=(kc == 0),
                                       stop=(kc == 7))
                    return ins
                P.op("pe", mm, reads=Wm.r() + XNT.r() + OT.r() + OBT.r(),
                     writes=PSR(bb) + PSR(bb + 1) + PSR(bb + 2) + PSR(bb + 3))
                for gi, (tmp, lo) in enumerate(((TA, 0), (TA, 512))):
                    tv = tmp.ap[:, lo:lo + 512]
                    rr_ = tmp.r(lo, lo + 512)
                    P.op("act", lambda e, tv=tv, gi=gi, bb=bb: e.activation(out=tv, in_=PS(bb + gi)[:, :],
                                                                           func=AF.Sigmoid),
                         reads=PSR(bb + gi), writes=rr_)
                    P.op("dve", lambda e, tv=tv, gi=gi, bb=bb: e.tensor_tensor(out=tv, in0=PS(bb + 2 + gi)[:, :],
                                                                               in1=tv, op=ALU.mult),
                         reads=PSR(bb + 2 + gi) + rr_, writes=rr_)
                P.op("dve", lambda e, c=c: e.tensor_tensor(out=MTv[:, c, :], in0=TA.ap[:, 0:512],
                                                            in1=TA.ap[:, 512:1024], op=ALU.add),
                     reads=TA.r(), writes=MT.r(c * 512, (c + 1) * 512))
                WS.release(im)
            if STOP == 8:
                break
            if dbg is not None and j == 0 and b == 0:
                P.dma(d_mt, MT.ap, reads=MT.r(), final=True)
                P.dma(d_obt, OBT.ap, reads=OBT.r(), final=True)
                P.dma(d_ot, OT.ap, reads=OT.r(), final=True)
                P.dma(d_rqt, RQT.ap, reads=RQT.r(), final=True)
                P.dma(d_rkt, RKT.ap, reads=RKT.r(), final=True)
                P.dma(d_st, ST.ap, reads=ST.r(), final=True)
            io0, Wo0 = WS.get("wo0")
            io1, Wo1 = WS.get("wo1")
            wo = [Wo0.ap.rearrange("p (k c) -> p k c", k=8), Wo1.ap.rearrange("p (k c) -> p k c", k=8)]
            for t in range(4):
                P.dma(X1v[:, t, :], xo[g0 + t * 128: g0 + (t + 1) * 128, :], writes=X1.r(t * 1024, (t + 1) * 1024))
            for t in range(4):
                b0 = (t % 2) * 2

                def mm(e, t=t, b0=b0):
                    for hf in range(2):
                        for kc in range(8):
                            ins = e.matmul(PS(b0 + hf)[:, :], lhsT=MTv[:, kc, t * 128:(t + 1) * 128],
                                           rhs=wo[hf][:, kc, :], start=(kc == 0), stop=(kc == 7))
                    return ins
                P.op("pe", mm, reads=MT.r() + Wo0.r() + Wo1.r(), writes=PSR(b0) + PSR(b0 + 1))
                s6, s6r = SC(32 + 4 * t, 3)
                for hf in range(2):
                    P.op("act", lambda e, hf=hf: e.activation(
                        out=TB.ap[:, hf * 512:(hf + 1) * 512], in_=PS(b0 + hf)[:, :], func=AF.Square,
                        accum_out=s6[:, hf:hf + 1]),
                        reads=PSR(b0 + hf), writes=TB.r(hf * 512, (hf + 1) * 512) + s6r)
                ssm = s6[:, 2:3]
                P.op("dve", lambda e: e.tensor_tensor(out=ssm, in0=s6[:, 0:1], in1=s6[:, 1:2], op=ALU.add),
                     reads=s6r, writes=s6r)
                act_rstd(ssm, ssm, D, s6r, s6r)
                for hf in range(2):
                    P.op("dve", lambda e, hf=hf: e.scalar_tensor_tensor(
                        out=TA.ap[:, hf * 512:(hf + 1) * 512], in0=PS(b0 + hf)[:, :], scalar=ssm,
                        in1=G1.ap[:, hf * 512:(hf + 1) * 512], op0=ALU.mult, op1=ALU.mult),
                        reads=PSR(b0 + hf) + s6r + G1.r(), writes=TA.r(hf * 512, (hf + 1) * 512))
                P.op("dve", lambda e, t=t: e.tensor_tensor(out=X1v[:, t, :], in0=X1v[:, t, :], in1=TA.ap, op=ALU.add),
                     reads=TA.r() + X1.r(t * 1024, (t + 1) * 1024), writes=X1.r(t * 1024, (t + 1) * 1024))
            WS.release(io0); WS.release(io1)
            if STOP == 81:
                break
            if dbg is not None:
                for t in range(4):
                    P.dma(dbg[g0 + t * 128: g0 + (t + 1) * 128, :], X1v[:, t, :], reads=X1.r(t * 1024, (t + 1) * 1024),
                          final=True)
            for t in range(4):
                ssap, s7r = SC(48 + 4 * t)
                P.op("act", lambda e, t=t, ssap=ssap: e.activation(out=TA.ap, in_=X1v[:, t, :], func=AF.Square,
                                                                    accum_out=ssap),
                     reads=X1.r(t * 1024, (t + 1) * 1024), writes=TA.r() + s7r)
                act_rstd(ssap, ssap, D, s7r, s7r)
                P.op("dve", lambda e, t=t, ssap=ssap: e.tensor_scalar(out=XN2.ap, in0=X1v[:, t, :], scalar1=ssap,
                                                                      scalar2=None, op0=ALU.mult),
                     reads=X1.r(t * 1024, (t + 1) * 1024) + s7r, writes=XN2.r())
                transpose_to(XN2, XNTv[:, :, t * 128:(t + 1) * 128], XNT.r(), 7, evac="act")
            if STOP == 82:
                break
            for g in range(8):
                iu, Wu = WS.get("wu%d" % g)
                wu = Wu.ap.rearrange("p (k c) -> p k c", k=8)
                for mc in range(4):
                    ch = g * 4 + mc
                    bank = ch % 4
                    rl = RL[ch % 4]

                    def mm(e, mc=mc, bank=bank, wu=wu):
                        for kc in range(8):
                            ins = e.matmul(PS(bank)[:, :], lhsT=wu[:, kc, mc * 128:(mc + 1) * 128], rhs=XNTv[:, kc, :],
                                           start=(kc == 0), stop=(kc == 7))
                        return ins
                    P.op("pe", mm, reads=Wu.r() + XNT.r(), writes=PSR(bank))
                    P.op("act", lambda e, bank=bank, rl=rl: e.activation(out=rl.ap, in_=PS(bank)[:, :], func=AF.Relu),
                         reads=PSR(bank), writes=rl.r())
                    P.op("dve", lambda e, ch=ch, rl=rl: e.tensor_tensor(
                        out=UTc(ch), in0=rl.ap, in1=rl.ap, op=ALU.mult),
                        reads=rl.r(), writes=UTr(ch))
                WS.release(iu)
            if STOP == 83:
                break
            for hf in range(2):
                for g in range(4):
                    idn, Wd = WS.get("wd%d_%d" % (hf, g))
                    wd = Wd.ap.rearrange("p (k c) -> p k c", k=8)
                    for t in range(4):
                        def mm(e, t=t, g=g, wd=wd):
                            for k8 in range(8):
                                kc = g * 8 + k8
                                ins = e.matmul(PS(4 + t)[:, :], lhsT=UTc(kc)[:, t * 128:(t + 1) * 128], rhs=wd[:, k8, :],
                                               start=(kc == 0), stop=(kc == 31))
                            return ins
                        P.op("pe", mm, reads=Wd.r() + (UTA if g < 2 else UTB).r((g % 2) * 4096, (g % 2 + 1) * 4096),
                             writes=PSR(4 + t))
                    WS.release(idn)
                if hf == 0:
                    for t in range(4):
                        sf, sfr = SC(64 + 4 * t, 3)
                        P.op("act", lambda e, t=t, sf=sf: e.activation(out=YF0v[:, t, :], in_=PS(4 + t)[:, :], func=AF.Square,
                                                                       accum_out=sf[:, 0:1]),
                             reads=PSR(4 + t), writes=YF0.r(t * 512, (t + 1) * 512) + sfr)
                        P.op("dve", lambda e, t=t: e.tensor_copy(out=YF0v[:, t, :], in_=PS(4 + t)[:, :]),
                             reads=PSR(4 + t) + YF0.r(t * 512, (t + 1) * 512), writes=YF0.r(t * 512, (t + 1) * 512))
            if STOP == 84:
                break
            if b + 1 < NB:
                p2_step1(b + 1)
            for t in range(4):
                sf, sfr = SC(64 + 4 * t, 3)
                tmp = TA if t % 2 == 0 else TB
                P.op("act", lambda e, t=t, sf=sf, tmp=tmp: e.activation(out=tmp.ap[:, 0:512], in_=PS(4 + t)[:, :],
                                                                        func=AF.Square, accum_out=sf[:, 1:2]),
                     reads=PSR(4 + t), writes=tmp.r(0, 512) + sfr)
                ssy = sf[:, 2:3]
                P.op("dve", lambda e, sf=sf, ssy=ssy: e.tensor_tensor(out=ssy, in0=sf[:, 0:1], in1=sf[:, 1:2], op=ALU.add),
                     reads=sfr, writes=sfr)
                act_rstd(ssy, ssy, D, sfr, sfr)
                P.op("dve", lambda e, t=t, ssy=ssy, tmp=tmp: e.scalar_tensor_tensor(
                    out=tmp.ap[:, 0:512], in0=YF0v[:, t, :], scalar=ssy, in1=G2.ap[:, 0:512], op0=ALU.mult,
                    op1=ALU.mult), reads=YF0.r(t * 512, (t + 1) * 512) + sfr + G2.r() + tmp.r(0, 512),
                    writes=tmp.r(0, 512))
                P.op("dve", lambda e, t=t, ssy=ssy, tmp=tmp: e.scalar_tensor_tensor(
                    out=tmp.ap[:, 512:1024], in0=PS(4 + t)[:, :], scalar=ssy, in1=G2.ap[:, 512:1024], op0=ALU.mult,
                    op1=ALU.mult), reads=PSR(4 + t) + sfr + G2.r(), writes=tmp.r(512, 1024))
                P.op("dve", lambda e, t=t, tmp=tmp: e.tensor_tensor(out=X1v[:, t, :], in0=tmp.ap, in1=X1v[:, t, :], op=ALU.add),
                     reads=tmp.r() + X1.r(t * 1024, (t + 1) * 1024), writes=X1.r(t * 1024, (t + 1) * 1024))
                P.dma(y[g0 + t * 128: g0 + (t + 1) * 128, :], X1v[:, t, :], reads=X1.r(t * 1024, (t + 1) * 1024),
                      final=True)

        key_off += nkt * 128
        ctx_off += nctx
        if 4 <= STOP < 9 or STOP >= 70:
            break

    P.finish()
    es.close()
    return nc, P


def rope_tab(pos, dim):
    inv = (1.0 / (10000.0 ** (np.arange(0, dim, 2, dtype=np.float32) / np.float32(dim)))).astype(np.float32)
    ang = pos.astype(np.float32)[:, None] * inv[None, :]
    return np.cos(ang).astype(np.float32), np.sin(ang).astype(np.float32)


def make_core_inputs(cfg, jobs, weights):
    S_OWN = cfg["S_OWN"]
    NT = S_OWN // 128
    xo = np.concatenate([jb["x_own"] for jb in jobs], 0)
    xcs = [jb["x_ctx"] for jb in jobs if jb["x_ctx"].shape[0] > 0]
    xc = np.concatenate(xcs, 0) if xcs else np.zeros((128, D), np.float32)
    tabT, tabM, tabF, cDl, cFl = [], [], [], [], []
    for jb in jobs:
        pos = np.concatenate([jb["pos_own"], jb["pos_ctx"]]).astype(np.int64)
        cr, sr = rope_tab(pos, 64)
        tabT.append(np.concatenate([cr, sr], 1))
        cm, sm = rope_tab(pos, 32)
        tabM.append(np.stack([np.concatenate([cm, cm], 1).T, np.concatenate([sm, sm], 1).T], 1))
        cro, sro = rope_tab(jb["pos_own"].astype(np.int64), 64)
        c64 = np.concatenate([cro, cro], 1).T
        s64 = np.concatenate([sro, sro], 1).T
        tabF.append(np.stack([np.concatenate([c64, c64], 0), np.concatenate([s64, s64], 0)], 1))
        nc_ = jb["pos_ctx"].shape[0] // 128
        if nc_:
            g0 = int(jb["pos_own"][0]) // 128
            gl = g0 + NT - 1
            gm = jb["pos_ctx"][::128].astype(np.int64) // 128
            dD = np.zeros((128, nc_), np.float32)
            dF = np.zeros((128, nc_), np.float32)
            left = gm < g0
            right = gm > gl
            dD[0:64, left] = 128.0 * (g0 - gm[left] - 1)
            dF[0:64, left] = 1.0
            dD[64:128, right] = 128.0 * (gm[right] - gl - 1)
            dF[64:128, right] = 1.0
            cDl.append(dD)
            cFl.append(dF)
    m = {
        "xo": np.ascontiguousarray(xo, np.float32),
        "xc": np.ascontiguousarray(xc, np.float32),
        "tabT": np.ascontiguousarray(np.concatenate(tabT, 0), np.float32),
        "tabM": np.ascontiguousarray(np.concatenate(tabM, 2), np.float32),
        "tabF": np.ascontiguousarray(np.concatenate(tabF, 2), np.float32),
        "cD": np.ascontiguousarray(np.concatenate(cDl, 1) if cDl else np.zeros((128, 1), np.float32)),
        "cF": np.ascontiguousarray(np.concatenate(cFl, 1) if cFl else np.zeros((128, 1), np.float32)),
    }
    m.update(weights)
    return m


def prep_weights(inp):
    w = {}
    for k in ("w_in", "w_q_up", "w_kv_up", "w_branch_a", "w_branch_b", "w_out", "w_up", "w_down",
              "g_pre_mix", "g_q_norm", "g_kv_norm", "g_post_mix", "g_pre_mlp", "g_post_mlp"):
        w[k] = np.ascontiguousarray(np.asarray(inp[k], np.float32)[0])
    w["lgf"] = np.ascontiguousarray(np.asarray(inp["ret_log_decay_fwd"], np.float32)[0])
    w["lgb"] = np.ascontiguousarray(np.asarray(inp["ret_log_decay_bwd"], np.float32)[0])
    return w


CFG = {"S_OWN": 2048, "nctx": [0, 0, 0, 0, 48, 48]}
_CACHE = {}


def kernel(**inputs):
    xp = np.asarray(inputs["x_prompt"], np.float32)
    xs = np.asarray(inputs["x_sample"], np.float32)
    weights = prep_weights(inputs)
    cfg = CFG
    S = cfg["S_OWN"]
    in_maps = []
    for core in range(8):
        jobs = []
        for i in range(4):
            jobs.append(dict(x_own=xp[core * 4 + i], pos_own=np.arange(S),
                             x_ctx=np.zeros((0, D), np.float32), pos_ctx=np.zeros((0,), np.int64)))
        sb, half = core // 2, core % 2
        for sj in range(2):
            a0 = half * 4096 + sj * S
            own_pos = np.arange(a0, a0 + S)
            ctx_pos = np.concatenate([np.arange(0, a0), np.arange(a0 + S, 8192)])
            jobs.append(dict(x_own=xs[sb, a0:a0 + S], pos_own=own_pos,
                             x_ctx=xs[sb][ctx_pos], pos_ctx=ctx_pos))
        in_maps.append(make_core_inputs(cfg, jobs, weights))
    if "nc" not in _CACHE:
        _CACHE["nc"] = build(cfg)[0]
    res = run_bass_kernel_spmd(_CACHE["nc"], in_maps, core_ids=list(range(8)))
    yp = np.zeros_like(xp)
    ys = np.zeros_like(xs)
    for core in range(8):
        yy = res.results[core]["y"]
        for i in range(4):
            yp[core * 4 + i] = yy[i * S:(i + 1) * S]
        sb, half = core // 2, core % 2
        for sj in range(2):
            a0 = half * 4096 + sj * S
            ys[sb, a0:a0 + S] = yy[(4 + sj) * S:(5 + sj) * S]
    return (yp, ys)
```

```python
import math
import types
from contextlib import ExitStack
import numpy as np
import concourse.bass as bass
import concourse.mybir as mybir
from concourse.bass_utils import run_bass_kernel_spmd

F32 = mybir.dt.float32
BF16 = mybir.dt.bfloat16
I32 = mybir.dt.int32
AF = mybir.ActivationFunctionType
ALU = mybir.AluOpType
AX = mybir.AxisListType

D = 1024
OFF = dict(cq=0, ckv=256, kr=384, rq=416, rk=928, rv=1440, rg=2464, ga=3488, gb=4512)
EPS = 1e-6
ATT_SCALE = 96 ** -0.5


def freeze(fn):
    if fn.__closure__ is None:
        return fn
    cells = []
    for c in fn.__closure__:
        try:
            cells.append(types.CellType(c.cell_contents))
        except ValueError:
            cells.append(c)
    return types.FunctionType(fn.__code__, fn.__globals__, fn.__name__, fn.__defaults__, tuple(cells))


class Res:
    __slots__ = ("w", "r")

    def __init__(self):
        self.w = None
        self.r = []


class Reg:
    def __init__(self, nbytes, gran):
        self.gran = gran
        self.n = (nbytes + gran - 1) // gran
        self.res = [(Res(), Res()) for _ in range(self.n)]

    def r(self, lo=0, hi=None, half=None):
        if hi is None:
            hi = self.n * self.gran
        out = []
        for s in range(lo // self.gran, (hi - 1) // self.gran + 1):
            if half in (None, 0):
                out.append(self.res[s][0])
            if half in (None, 1):
                out.append(self.res[s][1])
        return out


class LB:
    def __init__(self, tile, reg, off, size, dt):
        self.tile, self.reg, self.off, self.size, self.dt = tile, reg, off, size, dt
        self.isz = 4 if dt in (F32, I32) else 2
        a = tile[:, off // 2:(off + size) // 2]
        self.ap = a if dt == BF16 else a.bitcast(dt)

    def v(self, pat=None, **kw):
        return self.ap if pat is None else self.ap.rearrange(pat, **kw)

    def r(self, lo=0, hi=None, half=None):
        hi_b = self.size if hi is None else hi * self.isz
        return self.reg.r(self.off + lo * self.isz, self.off + hi_b, half)


class Prog:
    COMPUTE = ("pe", "dve", "act", "pool")
    ALL = ("pe", "dve", "act", "pool", "sp")

    def __init__(self, nc, n_dma_sems=14):
        self.nc = nc
        self.q = {e: [] for e in self.ALL}
        self.cnt = {e: 0 for e in self.COMPUTE}
        self.psem = {}
        self.seen = {e: {} for e in self.ALL}
        self._ctx = []
        for e in self.COMPUTE:
            g = nc.semaphore("prog_" + e)
            self.psem[e] = g.__enter__()
            self._ctx.append(g)
        self.dsem, self.dsem_val, self.dsem_next = {}, {}, {}
        for qn in ("sp", "pool"):
            lst = []
            for i in range(n_dma_sems):
                g = nc.semaphore("dma_%s_%d" % (qn, i))
                lst.append(g.__enter__())
                self._ctx.append(g)
            self.dsem[qn] = lst
            self.dsem_val[qn] = [0] * n_dma_sems
            self.dsem_next[qn] = 0
        self.final_tokens = []
        self.n_ops = 0

    def _deps(self, reads, writes):
        toks = []
        for r in reads:
            if r.w is not None:
                toks.append(r.w)
        for w in writes:
            if w.w is not None:
                toks.append(w.w)
            toks.extend(w.r)
        return toks

    def _emit_waits(self, e, toks):
        need = {}
        seen = self.seen[e]
        for (sem, val, src) in toks:
            if src == e and e == "pe":
                continue
            k = id(sem)
            if seen.get(k, 0) >= val:
                continue
            if k not in need or need[k][1] < val:
                need[k] = (sem, val)
        for k, (sem, val) in need.items():
            seen[k] = val
            self.q[e].append(("wait", sem, val))

    def _commit(self, tok, reads, writes):
        for r in reads:
            r.r.append(tok)
        for w in writes:
            w.w = tok
            w.r = []

    def op(self, e, fn, reads=(), writes=()):
        self._emit_waits(e, self._deps(reads, writes))
        self.cnt[e] += 1
        tok = (self.psem[e], self.cnt[e], e)
        self.q[e].append(("op", freeze(fn), self.psem[e]))
        self._commit(tok, reads, writes)
        self.n_ops += 1

    def dma(self, out_ap, in_ap, reads=(), writes=(), final=False, qn="sp", **kw):
        i = self.dsem_next[qn]
        self.dsem_next[qn] = (i + 1) % len(self.dsem[qn])
        sem = self.dsem[qn][i]
        toks = self._deps(reads, writes)
        prev = self.dsem_val[qn][i]
        if prev > 0:
            toks.append((sem, prev, None))
        self._emit_waits(qn, toks)
        val = prev + 16
        self.dsem_val[qn][i] = val
        tok = (sem, val, None)
        self.q[qn].append(("dma", out_ap, in_ap, sem, kw))
        self._commit(tok, reads, writes)
        if final:
            self.final_tokens.append(tok)
        self.n_ops += 1

    def finish(self):
        nc = self.nc
        self._emit_waits("sp", self.final_tokens)
        qs = self.q

        def run(e, engine):
            for item in qs[e]:
                if item[0] == "wait":
                    engine.wait_ge(item[1], item[2])
                elif item[0] == "op":
                    item[1](engine).then_inc(item[2], 1)
                else:
                    _, o, i_, sem, kw = item
                    engine.dma_start(out=o, in_=i_, **kw).then_inc(sem, 16)

        with nc.Block() as block:
            @block.tensor
            def _(t):
                run("pe", t)

            @block.vector
            def _(v):
                run("dve", v)

            @block.scalar
            def _(s):
                run("act", s)

            @block.gpsimd
            def _(g):
                run("pool", g)

            @block.sync
            def _(s):
                run("sp", s)
        for g in reversed(self._ctx):
            g.__exit__(None, None, None)


def build(cfg):
    S_OWN = cfg["S_OWN"]
    NCTX = list(cfg["nctx"])
    NJ = len(NCTX)
    NT = S_OWN // 128
    NB = S_OWN // 512
    NKT = [NT + c for c in NCTX]
    MAXKT = max(NKT)
    TOTKT = sum(NKT)
    TOTC = max(1, sum(NCTX))
    MAXC = max(1, max(NCTX))

    nc = bass.Bass("TRN2", target_bir_lowering=False)
    es = ExitStack()

    def din(name, shape, dt=F32):
        return nc.dram_tensor(name, list(shape), dt, kind="ExternalInput").ap()

    xo = din("xo", [NJ * S_OWN, D])
    xc = din("xc", [TOTC * 128, D])
    tabT = din("tabT", [TOTKT * 128, 64])
    tabM = din("tabM", [32, 2, TOTKT * 128])
    tabF = din("tabF", [128, 2, NJ * S_OWN])
    cD = din("cD", [128, TOTC])
    cF = din("cF", [128, TOTC])
    w_in = din("w_in", [D, 5536])
    w_q_up = din("w_q_up", [256, 768])
    w_kv_up = din("w_kv_up", [128, 1024])
    w_a = din("w_branch_a", [512, 1024])
    w_b = din("w_branch_b", [1024, 1024])
    w_out = din("w_out", [1024, 1024])
    w_up = din("w_up", [1024, 4096])
    w_down = din("w_down", [4096, 1024])
    g_pre_mix = din("g_pre_mix", [D])
    g_q_norm = din("g_q_norm", [256])
    g_kv_norm = din("g_kv_norm", [128])
    g_post_mix = din("g_post_mix", [D])
    g_pre_mlp = din("g_pre_mlp", [D])
    g_post_mlp = din("g_post_mlp", [D])
    lgf = din("lgf", [8])
    lgb = din("lgb", [8])
    y = nc.dram_tensor("y", [NJ * S_OWN, D], F32, kind="ExternalOutput").ap()

    P = Prog(nc)
    dbg = nc.dram_tensor("dbg", [NJ * S_OWN, D], F32, kind="ExternalOutput").ap() if cfg.get("dbg") else None
    if cfg.get("dbg"):
        d_mt = nc.dram_tensor("d_mt", [128, 8 * 512], BF16, kind="ExternalOutput").ap()
        d_obt = nc.dram_tensor("d_obt", [128, 8 * 512], BF16, kind="ExternalOutput").ap()
        d_ot = nc.dram_tensor("d_ot", [128, 8 * S_OWN], BF16, kind="ExternalOutput").ap()
        d_rqt = nc.dram_tensor("d_rqt", [128, 8 * 512], BF16, kind="ExternalOutput").ap()
        d_rkt = nc.dram_tensor("d_rkt", [128, 8 * 512], BF16, kind="ExternalOutput").ap()
        d_st = nc.dram_tensor("d_st", [128, NT * 1024], BF16, kind="ExternalOutput").ap()

    def sbuf(name, nbytes, gran=512):
        t = es.enter_context(nc.sbuf_tensor(name, [128, nbytes // 2], BF16))
        return t, Reg(nbytes, gran)

    def mk(name, nbytes, dt, gran=512):
        t, reg = sbuf(name, nbytes, gran)
        return LB(t, reg, 0, nbytes, dt)

    ARENA = 80 * 1024 - 512
    ar_t, ar_reg = sbuf("arena", ARENA, 512)

    def AR(off, size, dt):
        assert off + size <= ARENA, (off, size)
        assert off % 512 == 0
        return LB(ar_t, ar_reg, off, size, dt)

    K = 1024
    OT = mk("OT", 8 * S_OWN * 2, BF16, 1024)
    ST = mk("ST", NT * 2048, BF16, 2048)
    RING = [mk("ring%d" % i, 8192, BF16, 8192) for i in range(4)]
    IDENT = mk("ident", 256, BF16)
    ONESB = mk("onesb", 256, BF16)
    DUP = mk("dup", 256, BF16)
    ONESF = mk("onesf", 256, F32)
    DT_ = mk("dt", 8 * 128 * 4, F32, 4096)
    QDEC = mk("qdec", 8 * 128 * 4, F32, 4096)
    KDEC = mk("kdec", 64, F32)
    DEC = mk("dec", 32, F32)
    LG = mk("lg", 32, F32)
    LGF = mk("lgf_t", 32, F32)
    LGB = mk("lgb_t", 32, F32)
    EPST = mk("eps", 4, F32)
    G1 = mk("g1", 4096, F32, 4096)
    G2 = mk("g2", 4096, F32, 4096)
    WQ = mk("wq", 2 * 1024 * 2, BF16, 4096)
    WKV = mk("wkv", 8 * 128 * 2, BF16, 4096)
    ACC = mk("acc", 4096, F32, 4096)
    COEF = mk("coef", MAXC * 8 * 4, F32, MAXC * 32)
    CDT = mk("cdt", MAXC * 4, F32, MAXC * 4)
    CFT = mk("cft", MAXC * 4, F32, MAXC * 4)
    STAT = mk("stat", 64 * 4, F32, 256)
    STS = mk("sts", 512, F32, 16)

    def SC(c0, n=1):
        return STS.ap[:, c0:c0 + n], STS.r(c0, c0 + n)

    PSB = []
    for i in range(8):
        t = es.enter_context(nc.psum_tensor("ps%d" % i, [128, 512], F32))
        PSB.append((t, Reg(2048, 2048)))

    def PS(i):
        return PSB[i][0]

    def PSR(i, half=None):
        return PSB[i][1].r(half=half)

    def PSbf(i):
        return PSB[i][0][:, :].bitcast(BF16)

    packs = {}

    def pack_tensor(name, rows, cols):
        t = nc.dram_tensor("wp_" + name, [rows, cols], BF16, kind="Internal").ap()
        packs[name] = (t, Res())
        return t

    def act_rstd(out_ap, in_ap, dim, reads, writes):
        P.op("act", lambda e: e.activation(out=out_ap, in_=in_ap, func=AF.Ln,
                                           scale=1.0 / dim, bias=EPST.ap[:, 0:1]),
             reads=list(reads) + EPST.r(), writes=writes)
        P.op("act", lambda e: e.activation(out=out_ap, in_=out_ap, func=AF.Exp, scale=-0.5),
             reads=writes, writes=writes)

    rr = {"i": 0}

    def alt(engs=("dve", "pool")):
        rr["i"] += 1
        return engs[rr["i"] % len(engs)]

    NSTG = 4
    STG = [AR(i * 4096, 4096, F32) for i in range(NSTG)]
    STB = [AR(16384 + i * 2048, 2048, BF16) for i in range(NSTG)]
    o_ = 24576
    TMPA = AR(o_, 4096, F32); o_ += 4096
    TMPB = AR(o_, 4096, F32); o_ += 4096
    TMPI = AR(o_, 512, I32); o_ += 512
    DIFF = AR(o_, 512, F32); o_ += 512
    MGE = AR(o_, 512, F32); o_ += 512
    MLT = AR(o_, 512, F32); o_ += 512
    PPOS = AR(o_, 512, F32); o_ += 512
    PNEG = AR(o_, 512, F32); o_ += 512
    AQ = AR(o_, 512, F32); o_ += 512
    IFR = AR(o_, 512, F32); o_ += 512
    JP = AR(o_, 512, F32); o_ += 512
    IDF = AR(o_, 512, F32); o_ += 512

    P.op("pool", lambda e: e.memset(EPST.ap, EPS), writes=EPST.r())
    P.op("pool", lambda e: e.memset(ONESB.ap, 1.0), writes=ONESB.r())
    P.op("pool", lambda e: e.memset(ONESF.ap, 1.0), writes=ONESF.r())
    P.op("pool", lambda e: e.memset(IDF.ap, 0.0), writes=IDF.r())
    P.op("pool", lambda e: e.affine_select(out=IDF.ap, in_=IDF.ap, pattern=[[-1, 128]],
                                           compare_op=ALU.not_equal, fill=1.0, base=0,
                                           channel_multiplier=1),
         reads=IDF.r(), writes=IDF.r())
    P.op("dve", lambda e: e.tensor_copy(out=IDENT.ap, in_=IDF.ap), reads=IDF.r(), writes=IDENT.r())
    for (pr, cs, ps_, cs2) in ((0, 0, 0, 0), (0, 64, 0, 0), (64, 0, 64, 64), (64, 64, 64, 64)):
        P.op("dve", lambda e, pr=pr, cs=cs, cs2=cs2: e.tensor_copy(
            out=DUP.ap[pr:pr + 64, cs:cs + 64], in_=IDF.ap[pr:pr + 64, cs2:cs2 + 64]),
            reads=IDF.r(), writes=DUP.r())
    P.op("pool", lambda e: e.iota(TMPI.ap, pattern=[[1, 128]], base=0, channel_multiplier=-1),
         writes=TMPI.r())
    P.op("dve", lambda e: e.tensor_copy(out=DIFF.ap, in_=TMPI.ap), reads=TMPI.r(), writes=DIFF.r())
    P.op("pool", lambda e: e.iota(TMPI.ap, pattern=[[1, 128]], base=0, channel_multiplier=0),
         reads=TMPI.r(), writes=TMPI.r())
    P.op("dve", lambda e: e.tensor_copy(out=IFR.ap, in_=TMPI.ap), reads=TMPI.r(), writes=IFR.r())
    P.op("pool", lambda e: e.iota(TMPI.ap, pattern=[[0, 128]], base=0, channel_multiplier=1),
         reads=TMPI.r(), writes=TMPI.r())
    P.op("dve", lambda e: e.tensor_copy(out=JP.ap, in_=TMPI.ap), reads=TMPI.r(), writes=JP.r())
    P.op("dve", lambda e: e.tensor_scalar(out=MGE.ap, in0=DIFF.ap, scalar1=0.0, scalar2=None,
                                          op0=ALU.is_ge), reads=DIFF.r(), writes=MGE.r())
    P.op("dve", lambda e: e.tensor_scalar(out=MLT.ap, in0=DIFF.ap, scalar1=0.0, scalar2=None,
                                          op0=ALU.is_lt), reads=DIFF.r(), writes=MLT.r())
    P.op("dve", lambda e: e.tensor_scalar(out=PPOS.ap, in0=DIFF.ap, scalar1=0.0, scalar2=None,
                                          op0=ALU.max), reads=DIFF.r(), writes=PPOS.r())
    P.op("dve", lambda e: e.tensor_scalar(out=PNEG.ap, in0=DIFF.ap, scalar1=-1.0, scalar2=0.0,
                                          op0=ALU.mult, op1=ALU.max), reads=DIFF.r(), writes=PNEG.r())
    P.dma(LGF.ap, lgf.partition_broadcast(128), writes=LGF.r())
    P.dma(LGB.ap, lgb.partition_broadcast(128), writes=LGB.r())
    P.dma(LG.ap[0:64, :], lgf.partition_broadcast(64), writes=LG.r())
    P.dma(LG.ap[64:128, :], lgb.partition_broadcast(64), writes=LG.r())
    P.dma(G1.ap, g_post_mix.partition_broadcast(128), writes=G1.r())
    P.dma(G2.ap, g_post_mlp.partition_broadcast(128), writes=G2.r())
    P.op("dve", lambda e: e.tensor_scalar(out=AQ.ap[0:64, :], in0=IFR.ap[0:64, :], scalar1=1.0,
                                          scalar2=None, op0=ALU.add), reads=IFR.r(), writes=AQ.r())
    P.op("dve", lambda e: e.tensor_scalar(out=AQ.ap[64:128, :], in0=IFR.ap[64:128, :], scalar1=-1.0,
                                          scalar2=128.0, op0=ALU.mult, op1=ALU.add),
         reads=IFR.r(), writes=AQ.r())
    DTv = DT_.v("p (h i) -> p h i", h=8)
    QDv = QDEC.v("p (h i) -> p h i", h=8)
    for h in range(8):
        P.op("act", lambda e, h=h: e.activation(out=TMPA.ap[:, 0:128], in_=PPOS.ap, func=AF.Exp,
                                                scale=LGF.ap[:, h:h + 1]),
             reads=PPOS.r() + LGF.r(), writes=TMPA.r())
        P.op("act", lambda e, h=h: e.activation(out=TMPB.ap[:, 0:128], in_=PNEG.ap, func=AF.Exp,
                                                scale=LGB.ap[:, h:h + 1]),
             reads=PNEG.r() + LGB.r(), writes=TMPB.r())
        P.op("dve", lambda e: e.tensor_tensor(out=TMPA.ap[:, 0:128], in0=TMPA.ap[:, 0:128],
                                              in1=MGE.ap, op=ALU.mult),
             reads=TMPA.r() + MGE.r(), writes=TMPA.r())
        P.op("dve", lambda e: e.tensor_tensor(out=TMPB.ap[:, 0:128], in0=TMPB.ap[:, 0:128],
                                              in1=MLT.ap, op=ALU.mult),
             reads=TMPB.r() + MLT.r(), writes=TMPB.r())
        P.op("dve", lambda e: e.tensor_tensor(out=TMPA.ap[:, 0:128], in0=TMPA.ap[:, 0:128],
                                              in1=TMPB.ap[:, 0:128], op=ALU.add),
             reads=TMPA.r() + TMPB.r(), writes=TMPA.r())
        P.op("dve", lambda e, h=h: e.tensor_scalar(out=DTv[:, h, :], in0=TMPA.ap[:, 0:128],
                                                   scalar1=0.125, scalar2=None, op0=ALU.mult),
             reads=TMPA.r(), writes=DT_.r())
        P.op("act", lambda e, h=h: e.activation(out=QDv[:, h, :], in_=AQ.ap, func=AF.Exp,
                                                scale=LG.ap[:, h:h + 1]),
             reads=AQ.r() + LG.r(), writes=QDEC.r())
    KDv = KDEC.v("p (a h) -> p a h", a=2)
    P.op("dve", lambda e: e.tensor_scalar(out=TMPA.ap[:, 0:1], in0=JP.ap[:, 0:1], scalar1=-1.0,
                                          scalar2=127.0, op0=ALU.mult, op1=ALU.add),
         reads=JP.r() + TMPA.r(), writes=TMPA.r())
    P.op("act", lambda e: e.activation(out=KDv[:, 0, :], in_=LGF.ap, func=AF.Exp,
                                       scale=TMPA.ap[:, 0:1]),
         reads=TMPA.r() + LGF.r(), writes=KDEC.r())
    P.op("act", lambda e: e.activation(out=KDv[:, 1, :], in_=LGB.ap, func=AF.Exp,
                                       scale=JP.ap[:, 0:1]),
         reads=JP.r() + LGB.r(), writes=KDEC.r())
    P.op("dve", lambda e: e.tensor_scalar(out=KDEC.ap, in0=KDEC.ap, scalar1=0.125, scalar2=None,
                                          op0=ALU.mult), reads=KDEC.r(), writes=KDEC.r())
    P.op("act", lambda e: e.activation(out=DEC.ap, in_=LG.ap, func=AF.Exp, scale=128.0),
         reads=LG.r(), writes=DEC.r())

    chunks = []

    def build_pack(name, src, cs, kc_n, segs, gain):
        NC = sum(s_[1] for s_ in segs)
        dst, dres = pack_tensor(name, cs, kc_n * NC), packs[name][1]
        gt = None
        if gain is not None:
            gt = mk("gt_" + name, 128, F32, 128)
            P.dma(gt.ap[0:cs, 0:kc_n], gain.rearrange("(c p) -> p c", p=cs), writes=gt.r(),
                  allow_slow_non_contiguous=True)
        for kc in range(kc_n):
            off = 0
            for (c0, n, sign) in segs:
                a = 0
                while a < n:
                    m = min(1024, n - a)
                    chunks.append((src[kc * cs:(kc + 1) * cs, c0 + a:c0 + a + m], cs, m, gt, kc, sign,
                                   dst[:, kc * NC + off + a: kc * NC + off + a + m], dres))
                    a += m
                off += n
        return dst

    def emit_chunks():
        n_ = len(chunks)
        LAH = 3
        for k in range(n_ + LAH):
            if k < n_:
                src_, cs, m, gt, kc, sign, dst_, dres = chunks[k]
                sg = STG[k % NSTG]
                P.dma(sg.ap[0:cs, 0:m], src_, writes=sg.r())
            if k >= LAH:
                kk = k - LAH
                src_, cs, m, gt, kc, sign, dst_, dres = chunks[kk]
                sg, sb = STG[kk % NSTG], STB[kk % NSTG]
                use_act = (kk % 2 == 0) and sign > 0
                if gt is not None:
                    if use_act:
                        P.op("act", lambda e: e.activation(out=sb.ap[0:cs, 0:m], in_=sg.ap[0:cs, 0:m], func=AF.Copy,
                                                           scale=gt.ap[0:cs, kc:kc + 1]),
                             reads=sg.r() + gt.r(), writes=sb.r())
                    else:
                        P.op("dve", lambda e: e.tensor_scalar(
                            out=sb.ap[0:cs, 0:m], in0=sg.ap[0:cs, 0:m], scalar1=gt.ap[0:cs, kc:kc + 1],
                            scalar2=float(sign), op0=ALU.mult, op1=ALU.mult),
                            reads=sg.r() + gt.r(), writes=sb.r())
                else:
                    if use_act:
                        P.op("act", lambda e: e.copy(out=sb.ap[0:cs, 0:m], in_=sg.ap[0:cs, 0:m]),
                             reads=sg.r(), writes=sb.r())
                    else:
                        P.op("dve", lambda e: e.tensor_scalar(
                            out=sb.ap[0:cs, 0:m], in0=sg.ap[0:cs, 0:m], scalar1=float(sign),
                            scalar2=None, op0=ALU.mult),
                            reads=sg.r(), writes=sb.r())
                P.dma(dst_, sb.ap[0:cs, 0:m], reads=sb.r(), writes=[dres])

    OUTB = [AR(40960 + i * 6144, 6144, BF16) for i in range(2)]
    cjobs = []

    def custom_pack(name, src, kc_n, NC, loads, opsfn, gain):
        dst, dres = pack_tensor(name, 128, kc_n * NC), packs[name][1]
        gt = mk("gt_" + name, 128, F32, 128)
        P.dma(gt.ap[:, 0:kc_n], gain.rearrange("(c p) -> p c", p=128), writes=gt.r(),
              allow_slow_non_contiguous=True)
        for kc in range(kc_n):
            cjobs.append((src, kc, NC, loads, opsfn, gt, dst, dres))

    def emit_custom():
        k_stg = [0]
        for ji, (src, kc, NC, loads, opsfn, gt, dst, dres) in enumerate(cjobs):
            stgs = []
            for (c0, n) in loads:
                sg = STG[k_stg[0] % NSTG]
                k_stg[0] += 1
                P.dma(sg.ap[:, 0:n], src[kc * 128:(kc + 1) * 128, c0:c0 + n], writes=sg.r())
                stgs.append(sg)
            ob = OUTB[ji % 2]
            oap = ob.ap[:, 0:NC]
            rd = []
            for sg in stgs:
                rd += sg.r()
            for oi, (iap, oap_, sign) in enumerate(opsfn([sg.ap for sg in stgs], oap)):
                if sign > 0 and oi % 2 == 0:
                    P.op("act", lambda e: e.activation(out=oap_, in_=iap, func=AF.Copy, scale=gt.ap[:, kc:kc + 1]),
                         reads=rd + gt.r(), writes=ob.r())
                else:
                    P.op("dve", lambda e: e.tensor_scalar(out=oap_, in0=iap, scalar1=gt.ap[:, kc:kc + 1],
                                                          scalar2=float(sign), op0=ALU.mult, op1=ALU.mult),
                         reads=rd + gt.r(), writes=ob.r())
            P.dma(dst[:, kc * NC:(kc + 1) * NC], oap, reads=ob.r(), writes=[dres])

    def rot_segs(base, nheads, hd):
        half = hd // 2
        s = []
        for h in range(nheads):
            s.append((base + h * hd + half, half, -1))
            s.append((base + h * hd, half, 1))
        return s

    def ops_p1a(st, o):
        a = st[0]
        return [(a[:, 256:416], o[:, 0:160], 1), (a[:, 400:416], o[:, 160:176], -1),
                (a[:, 384:400], o[:, 176:192], 1), (a[:, 0:256], o[:, 192:448], 1)]
    custom_pack("p1a", w_in, 8, 448, [(0, 416)], ops_p1a, g_pre_mix)
    pP1A = packs["p1a"][0]
    pP1T = build_pack("p1t", w_in, 128, 8, [(OFF["rk"], 512, 1), (OFF["rv"], 1024, 1)], g_pre_mix)
    def ops_p2f(st, o):
        a = st[0]
        res = []
        for i in range(2):
            src3 = a[:, i * 512:(i + 1) * 512].rearrange("p (h c) -> p h c", h=8)
            rot4 = o[:, i * 1024 + 512:(i + 1) * 1024].rearrange("p (h a c) -> p h a c", h=8, a=2)
            res.append((a[:, i * 512:(i + 1) * 512], o[:, i * 1024:i * 1024 + 512], 1))
            res.append((src3[:, :, 32:64], rot4[:, :, 0, :], -1))
            res.append((src3[:, :, 0:32], rot4[:, :, 1, :], 1))
        return res
    custom_pack("p2f", w_in, 8, 2048, [(OFF["rq"], 1024)], ops_p2f, g_pre_mix)
    pP2F = packs["p2f"][0]
    pP2T = build_pack("p2t", w_in, 128, 8, [(OFF["rv"], 1024, 1), (OFF["rg"], 1024, 1)], g_pre_mix)
    def ops_pg(st, o):
        o4 = o.rearrange("p (c g n) -> p c g n", c=8, g=2)
        return [(st[0].rearrange("p (c n) -> p c n", c=8), o4[:, :, 0, :], 1),
                (st[1].rearrange("p (c n) -> p c n", c=8), o4[:, :, 1, :], 1)]
    custom_pack("pg", w_in, 8, 2048, [(OFF["ga"], 1024), (OFF["gb"], 1024)], ops_pg, g_pre_mix)
    pPG = packs["pg"][0]
    pPB = build_pack("pb", w_b, 128, 8, [(0, 1024, 1)], None)
    pPA = build_pack("pa", w_a, 64, 8, [(0, 1024, 1)], None)
    pPO = build_pack("po", w_out, 128, 8, [(0, 1024, 1)], None)
    pPU = build_pack("pu", w_up, 128, 8, [(0, 4096, 1)], g_pre_mlp)
    pPD = build_pack("pd", w_down, 128, 32, [(0, 1024, 1)], None)
    def ops_wq(st, o):
        a3 = st[0][:, 0:768].rearrange("p (h c) -> p h c", h=8)
        o3 = o.rearrange("p (h c) -> p h c", h=8)
        return [(a3, o3[:, :, 0:96], 1), (a3[:, :, 80:96], o3[:, :, 96:112], -1), (a3[:, :, 64:80], o3[:, :, 112:128], 1)]
    custom_pack("wq", w_q_up, 2, 1024, [(0, 768)], ops_wq, g_q_norm)
    pWQ = packs["wq"][0]
    pWKV = build_pack("wkv", w_kv_up, 128, 1, [(0, 1024, 1)], g_kv_norm)
    emit_custom()
    emit_chunks()
    P.dma(WQ.ap, pWQ, reads=[packs["wq"][1]], writes=WQ.r())
    P.dma(WKV.ap, pWKV, reads=[packs["wkv"][1]], writes=WKV.r())

    class WStream:
        def __init__(self, seq):
            self.seq = seq
            self.issued = 0
            self.free = [0, 1, 2, 3]
            self.slot_of = {}
            self.pos = 0

        def _issue(self):
            while self.free and self.issued < len(self.seq):
                s = self.free.pop(0)
                name, parts = self.seq[self.issued]
                for (dst_fn, src, res) in parts:
                    P.dma(dst_fn(RING[s]), src, reads=[res], writes=RING[s].r())
                self.slot_of[self.issued] = s
                self.issued += 1

        def get(self, name):
            i = self.pos
            assert self.seq[i][0] == name, (self.seq[i][0], name)
            if i not in self.slot_of:
                self._issue()
            assert i in self.slot_of, "ring exhausted at %s" % name
            self.pos += 1
            return i, RING[self.slot_of[i]]

        def release(self, i):
            self.free.append(self.slot_of.pop(i))
            self._issue()

    def wt3(pack, kc_n, NC, c0, n):
        src = packs[pack][0].rearrange("p (k c) -> p k c", k=kc_n)[:, :, c0:c0 + n]
        return (lambda rb: rb.ap[:, 0:kc_n * n].rearrange("p (k c) -> p k c", k=kc_n), src, packs[pack][1])

    seq = []
    for j in range(NJ):
        seq.append(("p1a", [wt3("p1a", 8, 448, 0, 448)]))
        seq.append(("p1k", [wt3("p1t", 8, 1536, 0, 512)]))
        seq.append(("p1v0", [wt3("p1t", 8, 1536, 512, 512)]))
        seq.append(("p1v1", [wt3("p1t", 8, 1536, 1024, 512)]))
        for b in range(NB if cfg.get("stop", 9) >= 4 else 0):
            for i, nm in enumerate(("rqa", "rqb", "rka", "rkb")):
                seq.append((nm, [wt3("p2f", 8, 2048, i * 512, 512)]))
            for i, nm in enumerate(("rv0", "rv1", "rg0", "rg1")):
                seq.append((nm, [wt3("p2t", 8, 2048, i * 512, 512)]))
            for c in range(8):
                srcG = pPG.rearrange("p (k c) -> p k c", k=8)[:, :, c * 256:(c + 1) * 256]
                srcB = pPB.rearrange("p (k c) -> p k c", k=8)[:, :, c * 128:(c + 1) * 128]
                srcA = pPA.rearrange("p (k c) -> p k c", k=8)[:, :, c * 128:(c + 1) * 128]
                seq.append(("mix%d" % c, [
                    (lambda rb: rb.ap[:, 0:2048].rearrange("p (k c) -> p k c", k=8), srcG, packs["pg"][1]),
                    (lambda rb: rb.ap[:, 2048:3072].rearrange("p (k c) -> p k c", k=8), srcB, packs["pb"][1]),
                    (lambda rb: rb.ap[0:64, 3072:4096].rearrange("p (k c) -> p k c", k=8), srcA, packs["pa"][1]),
                ]))
            for i in range(2):
                seq.append(("wo%d" % i, [wt3("po", 8, 1024, i * 512, 512)]))
            for i in range(8):
                seq.append(("wu%d" % i, [wt3("pu", 8, 4096, i * 512, 512)]))
            for half in range(2):
                for g in range(4):
                    src = pPD.rearrange("p (k c) -> p k c", k=32)[:, g * 8:(g + 1) * 8, half * 512:(half + 1) * 512]
                    seq.append(("wd%d_%d" % (half, g), [
                        (lambda rb: rb.ap[:, 0:4096].rearrange("p (k c) -> p k c", k=8), src, packs["pd"][1])]))
    WS = WStream(seq)

    KB = 1024
    NKEYMAX = MAXKT * 128
    o = 0
    LAT = AR(o, NKEYMAX * 2, BF16); o += NKEYMAX * 2
    KT = AR(o, NKEYMAX * 2, BF16); o += NKEYMAX * 2
    CQT = AR(o, 2 * S_OWN * 2, BF16); o += 2 * S_OWN * 2
    OV = o
    o = OV
    VH_SZ = ((MAXKT * 65 * 2 + 511) // 512) * 512
    VH = AR(o, VH_SZ, BF16); o += VH_SZ
    QH = AR(o, S_OWN * 2, BF16); o += S_OWN * 2
    TBM = [AR(o + i * 4096, 4096, F32) for i in range(2)]; o += 8192
    PT = [AR(o + i * 1024, 1024, BF16) for i in range(3)]; o += 3072
    T1 = AR(o, 2048, F32); o += 2048
    T2 = AR(o, 2048, F32); o += 2048
    BCS = AR(o, 2048, F32); o += 2048
    RS = AR(o, 2048, F32); o += 2048
    PHA_END = o
    assert PHA_END <= ARENA, PHA_END
    o = OV
    XT = [AR(o + i * 4096, 4096, F32) for i in range(2)]; o += 8192
    XNd = [AR(o + i * 2048, 2048, BF16) for i in range(2)]; o += 4096
    XNT1d = [AR(o + i * 2048, 2048, BF16) for i in range(2)]; o += 4096
    SQ = AR(o, 1024, BF16); o += 1024
    RBC = AR(o, 1024, F32); o += 1024
    TTT = [AR(o + i * 512, 512, F32) for i in range(2)]; o += 1024
    TMT = [AR(o + i * 1024, 1024, F32) for i in range(2)]; o += 2048
    K1 = AR(o, 512, F32); o += 512
    K2 = AR(o, 512, F32); o += 512
    RA = AR(o, 1024, F32); o += 1024
    RB_ = AR(o, 1024, F32); o += 1024
    RKR = AR(o, 2048, F32); o += 2048
    KS = AR(o, 2048, BF16); o += 2048
    RV1d = [AR(o + i * 2048, 2048, BF16) for i in range(2)]; o += 4096
    RKC = [AR(o + i * 1024, 1024, BF16) for i in range(2)]; o += 2048
    TKV = AR(o, 4096, F32); o += 4096
    assert o <= ARENA, o
    o = 0
    X2 = [AR(o + i * 4096, 4096, F32) for i in range(2)]; o += 8192
    XN2 = AR(o, 2048, BF16); o += 2048
    ST2 = AR(o, 512, F32); o += 512
    XNT = AR(o, 8192, BF16); o += 8192
    RQT = AR(o, 8192, BF16); o += 8192
    RKT = AR(o, 8192, BF16); o += 8192
    UTA = LB(ar_t, ar_reg, RQT.off, 16384, BF16)
    RVB = AR(o, 8192, BF16); o += 8192
    SRG = AR(o, 8192, BF16); o += 8192
    X1 = LB(ar_t, ar_reg, RVB.off, 16384, F32)
    TA = AR(o, 4096, F32); o += 4096
    UTB_OFF = o
    TB = AR(o, 4096, F32); o += 4096
    SMB = [AR(o + i * 1024, 1024, BF16) for i in range(2)]; o += 2048
    QP = [AR(o + i * 1024, 1024, BF16) for i in range(2)]; o += 2048
    OB = XN2
    OBT = AR(o, 8192, BF16); o += 8192
    MT = AR(o, 8192, BF16); o += 8192
    YF0 = LB(ar_t, ar_reg, MT.off, 8192, F32)
    TF = X2[0]
    JUNK2 = AR(o, 1024, BF16); o += 1024
    assert o <= ARENA, o
    UTB = AR(UTB_OFF, 16384, BF16)
    assert UTB_OFF + 16384 <= MT.off
    RL = [LB(ar_t, ar_reg, X2[0].off + i * 1024, 1024, BF16) for i in range(4)]
    OUTT = X2[1]

    def norm_x_tile(xt, xn, stat_col, src_reads):
        ssap = STAT.ap[:, stat_col:stat_col + 1]
        sr = STAT.r(stat_col * 4, stat_col * 4 + 4) if False else STAT.r()
        P.op("act", lambda e: e.activation(out=xn.ap, in_=xt.ap, func=AF.Square, accum_out=ssap),
             reads=xt.r(), writes=xn.r() + STAT.r())
        act_rstd(ssap, ssap, D, STAT.r(), STAT.r())
        P.op("dve", lambda e: e.tensor_scalar(out=xn.ap, in0=xt.ap, scalar1=ssap, scalar2=None,
                                              op0=ALU.mult),
             reads=xt.r() + STAT.r(), writes=xn.r())


    def transpose_to(xn, dst_ap3, dst_reads_writes, bank, evac="dve"):
        pb = PSbf(bank)

        def tr(e):
            for kc in range(8):
                ins = e.transpose(out=pb[:, kc * 128:(kc + 1) * 128], in_=xn.ap[:, kc * 128:(kc + 1) * 128],
                                  identity=IDENT.ap)
            return ins
        P.op("pe", tr, reads=xn.r() + IDENT.r(), writes=PSR(bank))
        if evac == "act":
            P.op("act", lambda e: e.copy(out=dst_ap3, in_=pb.rearrange("p (k c) -> p k c", k=8)),
                 reads=PSR(bank), writes=dst_reads_writes)
        else:
            P.op("dve", lambda e: e.tensor_copy(out=dst_ap3, in_=pb.rearrange("p (k c) -> p k c", k=8)),
                 reads=PSR(bank), writes=dst_reads_writes)

    key_off = 0
    ctx_off = 0
    LATv = LAT.ap
    KTv = KT.ap
    CQTv = CQT.v("p (k s) -> p k s", k=2)
    STv = ST.v("p (n h v) -> p n h v", n=NT, h=8)
    OTv = OT.v("p (h s) -> p h s", h=8)
    ACCv = ACC.v("p (h v) -> p h v", h=8)
    COEFv = COEF.v("p (m h) -> p m h", h=8)
    DECb = DEC.ap.unsqueeze(2).broadcast_to([128, 8, 128])

    STOP = cfg.get("stop", 9)
    for j in range(NJ):
        if STOP == 0:
            break
        nctx = NCTX[j]
        nkt = NKT[j]
        NKEY = nkt * 128
        own0 = j * S_OWN
        P.op("pool", lambda e: e.memset(ACC.ap, 0.0), writes=ACC.r())
        if nctx > 0:
            P.dma(CDT.ap[:, 0:nctx], cD[:, ctx_off:ctx_off + nctx], writes=CDT.r())
            P.dma(CFT.ap[:, 0:nctx], cF[:, ctx_off:ctx_off + nctx], writes=CFT.r())
            P.op("dve", lambda e, nctx=nctx: e.tensor_tensor(
                out=COEFv[:, 0:nctx, :], in0=CDT.ap[:, 0:nctx].unsqueeze(2).broadcast_to([128, nctx, 8]),
                in1=LG.ap.unsqueeze(1).broadcast_to([128, nctx, 8]), op=ALU.mult),
                reads=CDT.r() + LG.r(), writes=COEF.r())
            P.op("act", lambda e, nctx=nctx: e.activation(out=COEFv[:, 0:nctx, :], in_=COEFv[:, 0:nctx, :],
                                                          func=AF.Exp),
                 reads=COEF.r(), writes=COEF.r())
            P.op("dve", lambda e, nctx=nctx: e.tensor_tensor(
                out=COEFv[:, 0:nctx, :], in0=COEFv[:, 0:nctx, :],
                in1=CFT.ap[:, 0:nctx].unsqueeze(2).broadcast_to([128, nctx, 8]), op=ALU.mult),
                reads=COEF.r() + CFT.r(), writes=COEF.r())

        iA, WA_ = WS.get("p1a")
        iK, WK_ = WS.get("p1k")
        iV0, WV0 = WS.get("p1v0")
        iV1, WV1 = WS.get("p1v1")
        wA = WA_.ap[:, 0:8 * 448].rearrange("p (k c) -> p k c", k=8)
        wK = WK_.ap.rearrange("p (k c) -> p k c", k=8)
        wV = [WV0.ap.rearrange("p (k c) -> p k c", k=8), WV1.ap.rearrange("p (k c) -> p k c", k=8)]

        def x_src(kt):
            if kt < NT:
                return xo[own0 + kt * 128: own0 + (kt + 1) * 128, :]
            m = ctx_off + (kt - NT)
            return xc[m * 128:(m + 1) * 128, :]

        def p1_xload(kt):
            P.dma(XT[kt % 2].ap, x_src(kt), writes=XT[kt % 2].r())

        def p1_stageA(kt):
            norm_x_tile(XT[kt % 2], XNd[kt % 2], kt % 2, None)
            transpose_to(XNd[kt % 2], XNT1d[kt % 2].v("p (k c) -> p k c", k=8), XNT1d[kt % 2].r(), 0)

        def p1_load(kt):
            tk = key_off + kt * 128
            P.dma(TTT[kt % 2].ap[:, 0:64], tabT[tk:tk + 128, :], writes=TTT[kt % 2].r())
            P.dma(TMT[kt % 2].ap[64:96, :].rearrange("p (a c) -> p a c", a=2), tabM[:, :, tk:tk + 128],
                  writes=TMT[kt % 2].r())

        def p1_B(kt, phase):
            own = kt < NT
            tmt = TMT[kt % 2].ap[64:96, :].rearrange("p (a c) -> p a c", a=2)
            kc0 = kt * 128
            XNT1 = XNT1d[kt % 2]
            XNT1v = XNT1.v("p (k c) -> p k c", k=8)
            def fm(e, own=own):
                groups = [(0, 128, PS(1)[:, 0:128]), (64, 96, PS(2)[0:96, 0:128]), (96, 96, PS(2)[0:96, 128:256])]
                if own:
                    groups += [(192, 128, PS(1)[:, 128:256]), (320, 128, PS(1)[:, 256:384])]
                for (c0, m, out) in groups:
                    for kc in range(8):
                        ins = e.matmul(out, lhsT=wA[:, kc, c0:c0 + m], rhs=XNT1v[:, kc, :],
                                       start=(kc == 0), stop=(kc == 7))
                return ins
            if phase == 0:
                P.op("pe", fm, reads=XNT1.r() + WA_.r(), writes=PSR(1) + PSR(2))
            def tm(e):
                for (bank, w) in ((3, wK), (4, wV[0]), (5, wV[1])):
                    for kc in range(8):
                        ins = e.matmul(PS(bank)[:, :], lhsT=XNT1v[:, kc, :], rhs=w[:, kc, :],
                                       start=(kc == 0), stop=(kc == 7))
                return ins
            if phase == 0:
                P.op("pe", tm, reads=XNT1.r() + WK_.r() + WV0.r() + WV1.r(), writes=PSR(3) + PSR(4) + PSR(5))
                return
            rkc = RKC[kt % 2]
            rv1 = RV1d[kt % 2]
            P.op("act", lambda e: e.copy(out=rkc.ap, in_=PS(3)[:, :]), reads=PSR(3), writes=rkc.r())
            P.op("act", lambda e: e.copy(out=rv1.ap[:, 0:512], in_=PS(4)[:, :]), reads=PSR(4), writes=rv1.r())
            P.op("act", lambda e: e.copy(out=rv1.ap[:, 512:1024], in_=PS(5)[:, :]), reads=PSR(5), writes=rv1.r())
            nsq = 3 if own else 1
            SQv = SQ.ap[:, 0:384].rearrange("p (a c) -> p a c", a=3)
            P.op("act", lambda e, nsq=nsq: e.activation(out=SQv[:, 0:nsq, :],
                                                        in_=PS(1)[:, 0:nsq * 128].rearrange("p (a c) -> p a c", a=nsq),
                                                        func=AF.Square),
                 reads=PSR(1), writes=SQ.r())

            def ssmm(e, own=own):
                ins = e.matmul(PS(2)[:, 256:384], lhsT=ONESB.ap, rhs=SQv[:, 0, :], start=True, stop=True)
                if own:
                    e.matmul(PS(2)[:, 384:512], lhsT=ONESB.ap, rhs=SQv[:, 1, :], start=True, stop=False)
                    ins = e.matmul(PS(2)[:, 384:512], lhsT=ONESB.ap, rhs=SQv[:, 2, :], start=False, stop=True)
                return ins
            P.op("pe", ssmm, reads=SQ.r() + ONESB.r(), writes=PSR(2))
            RBCv = RBC.v("p (a c) -> p a c", a=2)
            P.op("act", lambda e: e.activation(out=RBCv[:, 0, :], in_=PS(2)[:, 256:384], func=AF.Ln,
                                               scale=1.0 / 128, bias=EPST.ap[:, 0:1]),
                 reads=PSR(2) + EPST.r(), writes=RBC.r())
            if own:
                P.op("act", lambda e: e.activation(out=RBCv[:, 1, :], in_=PS(2)[:, 384:512], func=AF.Ln,
                                                   scale=1.0 / 256, bias=EPST.ap[:, 0:1]),
                     reads=PSR(2) + EPST.r(), writes=RBC.r())
            na = 2 if own else 1
            P.op("act", lambda e, na=na: e.activation(out=RBCv[:, 0:na, :], in_=RBCv[:, 0:na, :],
                                                      func=AF.Exp, scale=-0.5),
                 reads=RBC.r(), writes=RBC.r())
            P.op("dve", lambda e, kc0=kc0: e.tensor_tensor(out=LATv[:, kc0:kc0 + 128], in0=PS(1)[:, 0:128],
                                                           in1=RBCv[:, 0, :], op=ALU.mult),
                 reads=PSR(1) + RBC.r(), writes=LAT.r(kc0, kc0 + 128))
            if own:
                P.op("dve", lambda e, kc0=kc0: e.tensor_tensor(
                    out=CQTv[:, :, kc0:kc0 + 128], in0=PS(1)[:, 128:384].rearrange("p (a c) -> p a c", a=2),
                    in1=RBCv[:, 1, :].unsqueeze(1).broadcast_to([128, 2, 128]), op=ALU.mult),
                    reads=PSR(1) + RBC.r(), writes=CQT.r(kc0, kc0 + 128) + CQT.r(S_OWN + kc0, S_OWN + kc0 + 128))
            P.op("dve", lambda e, tmt=tmt: e.tensor_tensor(out=K1.ap[64:96, :], in0=PS(2)[64:96, 0:128],
                                                           in1=tmt[:, 0, :], op=ALU.mult),
                 reads=PSR(2) + TMT[kt % 2].r(), writes=K1.r())
            P.op("dve", lambda e, tmt=tmt: e.tensor_tensor(out=K2.ap[64:96, :], in0=PS(2)[64:96, 128:256],
                                                           in1=tmt[:, 1, :], op=ALU.mult),
                 reads=PSR(2) + TMT[kt % 2].r(), writes=K2.r())
            P.op("dve", lambda e, kc0=kc0: e.tensor_tensor(out=KTv[64:96, kc0:kc0 + 128], in0=K1.ap[64:96, :],
                                                            in1=K2.ap[64:96, :], op=ALU.add),
                 reads=K1.r() + K2.r(), writes=KT.r(kc0, kc0 + 128, half=1))

        def p1_C(kt):
            own = kt < NT
            ttt = TTT[kt % 2]
            rkc = RKC[kt % 2]
            rv1 = RV1d[kt % 2]
            rk4 = rkc.ap.rearrange("p (h a c) -> p h a c", h=8, a=2)
            cosb = ttt.ap[:, 0:32].unsqueeze(1).broadcast_to([128, 8, 32])
            sinb = ttt.ap[:, 32:64].unsqueeze(1).broadcast_to([128, 8, 32])
            RAv = RA.v("p (h c) -> p h c", h=8)
            RBv = RB_.v("p (h c) -> p h c", h=8)
            RKRv = RKR.v("p (h a c) -> p h a c", h=8, a=2)
            P.op("dve", lambda e: e.tensor_tensor(out=RAv, in0=rk4[:, :, 0, :], in1=cosb, op=ALU.mult),
                 reads=rkc.r() + ttt.r(), writes=RA.r())
            P.op("dve", lambda e: e.tensor_tensor(out=RBv, in0=rk4[:, :, 1, :], in1=sinb, op=ALU.mult),
                 reads=rkc.r() + ttt.r(), writes=RB_.r())
            P.op("dve", lambda e: e.tensor_tensor(out=RKRv[:, :, 0, :], in0=RAv, in1=RBv, op=ALU.subtract),
                 reads=RA.r() + RB_.r(), writes=RKR.r())
            P.op("dve", lambda e: e.tensor_tensor(out=RAv, in0=rk4[:, :, 0, :], in1=sinb, op=ALU.mult),
                 reads=rkc.r() + ttt.r() + RA.r(), writes=RA.r())
            P.op("dve", lambda e: e.tensor_tensor(out=RBv, in0=rk4[:, :, 1, :], in1=cosb, op=ALU.mult),
                 reads=rkc.r() + ttt.r() + RB_.r(), writes=RB_.r())
            P.op("dve", lambda e: e.tensor_tensor(out=RKRv[:, :, 1, :], in0=RAv, in1=RBv, op=ALU.add),
                 reads=RA.r() + RB_.r() + RKR.r(), writes=RKR.r())
            KSv = KS.v("p (h c) -> p h c", h=8)
            RKR3 = RKR.v("p (h c) -> p h c", h=8)
            for a in range(2):
                P.op("dve", lambda e, a=a: e.tensor_tensor(
                    out=KSv[:, :, a * 64:(a + 1) * 64], in0=RKR3,
                    in1=KDv[:, a, :].unsqueeze(2).broadcast_to([128, 8, 64]), op=ALU.mult),
                    reads=RKR.r() + KDEC.r(), writes=KS.r())

            def kvmm(e):
                for h in range(8):
                    ins = e.matmul(PS(6 + h // 4)[:, (h % 4) * 128:(h % 4 + 1) * 128], lhsT=KSv[:, h, :],
                                   rhs=rv1.ap[:, h * 128:(h + 1) * 128], start=True, stop=True)
                return ins
            P.op("pe", kvmm, reads=KS.r() + rv1.r(), writes=PSR(6) + PSR(7))
            if own:
                for hh in range(2):
                    P.op("dve", lambda e, hh=hh, kt=kt: e.tensor_copy(
                        out=STv[:, kt, hh * 4:(hh + 1) * 4, :],
                        in_=PS(6 + hh)[:, :].rearrange("p (h v) -> p h v", h=4)),
                        reads=PSR(6 + hh), writes=ST.r(kt * 1024, (kt + 1) * 1024))
            else:
                m = kt - NT
                TKVv = TKV.v("p (h v) -> p h v", h=8)
                for hh in range(2):
                    P.op("dve", lambda e, hh=hh, m=m: e.tensor_tensor(
                        out=TKVv[:, hh * 4:(hh + 1) * 4, :], in0=PS(6 + hh)[:, :].rearrange("p (h v) -> p h v", h=4),
                        in1=COEFv[:, m, hh * 4:(hh + 1) * 4].unsqueeze(2).broadcast_to([128, 4, 128]), op=ALU.mult),
                        reads=PSR(6 + hh) + COEF.r(), writes=TKV.r())
                P.op("dve", lambda e: e.tensor_tensor(out=ACC.ap, in0=ACC.ap, in1=TKV.ap, op=ALU.add),
                     reads=TKV.r() + ACC.r(), writes=ACC.r())

        p1_xload(0)
        if nkt > 1:
            p1_xload(1)
        p1_stageA(0)
        p1_load(0)
        p1_B(0, 0)
        p1_B(0, 1)
        for kt in range(nkt):
            if kt + 2 < nkt:
                p1_xload(kt + 2)
            if kt + 1 < nkt:
                p1_stageA(kt + 1)
                p1_load(kt + 1)
                p1_B(kt + 1, 0)
            p1_C(kt)
            if kt + 1 < nkt:
                p1_B(kt + 1, 1)
        WS.release(iA); WS.release(iK); WS.release(iV0); WS.release(iV1)

        if STOP == 1:
            key_off += nkt * 128
            ctx_off += nctx
            continue
        TKVv = TKV.v("p (h v) -> p h v", h=8)
        accs = [(ACC, ACCv), (TKV, TKVv)]
        for i_ in range(NT):
            ca, cav = accs[i_ % 2]
            cb, cbv = accs[(i_ + 1) % 2]
            for (half, n) in ((0, i_), (1, NT - 1 - i_)):
                p0, p1 = half * 64, half * 64 + 64
                rs = ST.r(n * 1024, (n + 1) * 1024, half=half)
                P.op("dve", lambda e, p0=p0, p1=p1, cav=cav, cbv=cbv: e.tensor_tensor(
                    out=cbv[p0:p1], in0=cav[p0:p1], in1=DECb[p0:p1], op=ALU.mult),
                    reads=ca.r(half=half) + DEC.r(), writes=cb.r(half=half))
                P.op("dve", lambda e, p0=p0, p1=p1, n=n, cbv=cbv: e.tensor_tensor(
                    out=cbv[p0:p1], in0=cbv[p0:p1], in1=STv[p0:p1, n], op=ALU.add),
                    reads=cb.r(half=half) + rs, writes=cb.r(half=half))
                P.op("act", lambda e, p0=p0, p1=p1, n=n, ca=ca: e.copy(
                    out=ST.ap[p0:p1, n * 1024:(n + 1) * 1024], in_=ca.ap[p0:p1, :]),
                    reads=ca.r(half=half), writes=rs)

        if STOP == 2:
            key_off += nkt * 128
            ctx_off += nctx
            continue
        VHv = VH.ap[:, 0:MAXKT * 65].rearrange("p (c d) -> p c d", d=65)
        P.op("pool", lambda e: e.memset(VH.ap, 1.0), writes=VH.r())
        WQv = WQ.v("p (k h c) -> p k h c", k=2, h=8)
        WKVv = WKV.v("p (h c) -> p h c", h=8)
        nkb = (NKEY + 511) // 512
        for h in range(8):
            for kb in range(nkb):
                c0 = kb * 512
                n = min(512, NKEY - c0)
                bank = kb % 2
                P.op("pe", lambda e, c0=c0, n=n, bank=bank, h=h: e.matmul(
                    PS(bank)[0:64, 0:n], lhsT=WKVv[:, h, 0:64], rhs=LATv[:, c0:c0 + n], start=True, stop=True),
                    reads=WKV.r() + LAT.r(c0, c0 + n), writes=PSR(bank))
                if kb % 2:
                    P.op("act", lambda e, c0=c0, n=n, bank=bank: e.copy(out=KTv[0:64, c0:c0 + n],
                                                                        in_=PS(bank)[0:64, 0:n]),
                         reads=PSR(bank), writes=KT.r(c0, c0 + n, half=0))
                else:
                    P.op("dve", lambda e, c0=c0, n=n, bank=bank: e.tensor_copy(out=KTv[0:64, c0:c0 + n],
                                                                               in_=PS(bank)[0:64, 0:n]),
                         reads=PSR(bank), writes=KT.r(c0, c0 + n, half=0))
            for g in range((nkt + 7) // 8):
                cc = list(range(g * 8, min(nkt, g * 8 + 8)))
                bank = 2 + g % 2

                def vmm(e, cc=cc, bank=bank, h=h):
                    for i, c in enumerate(cc):
                        ins = e.matmul(PS(bank)[:, i * 64:(i + 1) * 64], lhsT=LATv[:, c * 128:(c + 1) * 128],
                                       rhs=WKVv[:, h, 64:128], start=True, stop=True)
                    return ins
                P.op("pe", vmm, reads=WKV.r() + LAT.r(cc[0] * 128, (cc[-1] + 1) * 128), writes=PSR(bank))
                ncc = len(cc)
                P.op("dve" if g % 2 else "act",
                     (lambda e, cc=cc, bank=bank, ncc=ncc: e.tensor_copy(
                         out=VHv[:, cc[0]:cc[0] + ncc, 0:64],
                         in_=PS(bank)[:, 0:ncc * 64].rearrange("p (c d) -> p c d", d=64))) if g % 2 else
                     (lambda e, cc=cc, bank=bank, ncc=ncc: e.copy(
                         out=VHv[:, cc[0]:cc[0] + ncc, 0:64],
                         in_=PS(bank)[:, 0:ncc * 64].rearrange("p (c d) -> p c d", d=64))),
                     reads=PSR(bank), writes=VH.r())
            for b in range(NB):
                s0 = b * 512
                tb = TBM[b % 2]
                tbv = tb.ap[64:96, :].rearrange("p (a c) -> p a c", a=2)
                P.dma(tbv, tabM[:, :, key_off + s0: key_off + s0 + 512], writes=tb.r())

                def qmm(e, s0=s0, h=h):
                    for kc in range(2):
                        e.matmul(PS(4)[0:96, :], lhsT=WQv[:, kc, h, 0:96], rhs=CQTv[:, kc, s0:s0 + 512],
                                 start=(kc == 0), stop=(kc == 1))
                    for kc in range(2):
                        ins = e.matmul(PS(5)[0:96, :], lhsT=WQv[:, kc, h, 32:128], rhs=CQTv[:, kc, s0:s0 + 512],
                                       start=(kc == 0), stop=(kc == 1))
                    return ins
                P.op("pe", qmm, reads=WQ.r() + CQT.r(), writes=PSR(4) + PSR(5))
                P.op("dve", lambda e, tbv=tbv: e.tensor_tensor(out=T1.ap[64:96, :], in0=PS(4)[64:96, :],
                                                               in1=tbv[:, 0, :], op=ALU.mult),
                     reads=PSR(4) + tb.r(), writes=T1.r())
                P.op("dve", lambda e, tbv=tbv: e.tensor_tensor(out=T2.ap[64:96, :], in0=PS(5)[64:96, :],
                                                               in1=tbv[:, 1, :], op=ALU.mult),
                     reads=PSR(5) + tb.r(), writes=T2.r())
                P.op("dve", lambda e, s0=s0: e.tensor_tensor(out=QH.ap[64:96, s0:s0 + 512], in0=T1.ap[64:96, :],
                                                              in1=T2.ap[64:96, :], op=ALU.add),
                     reads=T1.r() + T2.r(), writes=QH.r(s0, s0 + 512, half=1))
                P.op("act", lambda e, s0=s0: e.copy(out=QH.ap[0:64, s0:s0 + 512], in_=PS(4)[0:64, :]),
                     reads=PSR(4), writes=QH.r(s0, s0 + 512, half=0))
            LA = 2
            steps = [(b, c) for b in range(NB) for c in range(nkt)]
            deferred = []

            def score_exp(idx, h=h):
                b, c = steps[idx]
                s0 = b * 512
                sbank = idx % 3
                pt = PT[idx % 3]
                P.op("pe", lambda e: e.matmul(
                    PS(sbank)[:, :], lhsT=KTv[0:96, c * 128:(c + 1) * 128], rhs=QH.ap[0:96, s0:s0 + 512],
                    start=True, stop=True),
                    reads=KT.r(c * 128, (c + 1) * 128) + QH.r(s0, s0 + 512), writes=PSR(sbank))
                P.op("act", lambda e: e.activation(out=pt.ap, in_=PS(sbank)[:, :], func=AF.Exp, scale=ATT_SCALE),
                     reads=PSR(sbank), writes=pt.r())

            def pv(idx, h=h):
                b, c = steps[idx]
                pt = PT[idx % 3]
                obank = 6 + b % 2
                P.op("pe", lambda e: e.matmul(
                    PS(obank)[0:65, :], lhsT=VHv[:, c, 0:65], rhs=pt.ap, start=(c == 0), stop=(c == nkt - 1)),
                    reads=VH.r() + pt.r(), writes=PSR(obank))
                if c == nkt - 1:
                    deferred.append((idx + LA + 2, b))

            def epilogue(b, h=h):
                s0 = b * 512
                obank = 6 + b % 2
                P.op("dve", lambda e: e.reciprocal(out=RS.ap[64:65, :], in_=PS(obank)[64:65, :]),
                     reads=PSR(obank), writes=RS.r())
                P.op("pe", lambda e: e.matmul(PS(3)[0:64, :], lhsT=ONESF.ap[64:65, 0:64], rhs=RS.ap[64:65, :],
                                              start=True, stop=True),
                     reads=RS.r() + ONESF.r(), writes=PSR(3))
                P.op("dve", lambda e: e.tensor_copy(out=BCS.ap[0:64, :], in_=PS(3)[0:64, :]), reads=PSR(3),
                     writes=BCS.r())
                P.op("dve", lambda e: e.tensor_tensor(
                    out=OTv[0:64, h, s0:s0 + 512], in0=PS(obank)[0:64, :], in1=BCS.ap[0:64, :], op=ALU.mult),
                    reads=PSR(obank) + BCS.r(), writes=OT.r(h * S_OWN + s0, h * S_OWN + s0 + 512))

            for i_ in range(len(steps) + LA):
                if i_ < len(steps):
                    score_exp(i_)
                if i_ >= LA:
                    pv(i_ - LA)
                while deferred and deferred[0][0] <= i_:
                    epilogue(deferred.pop(0)[1])
            while deferred:
                epilogue(deferred.pop(0)[1])

        if STOP == 3:
            key_off += nkt * 128
            ctx_off += nctx
            continue
        XNTv = XNT.v("p (k s) -> p k s", k=8)
        RQTv = RQT.v("p (a s) -> p a s", a=8)
        RKTv = RKT.v("p (a s) -> p a s", a=8)
        RVBv = RVB.v("p (t c) -> p t c", t=4)
        SRGv = SRG.v("p (t c) -> p t c", t=4)
        X1v = X1.v("p (t c) -> p t c", t=4)
        OBTv = OBT.v("p (k s) -> p k s", k=8)
        MTv = MT.v("p (k s) -> p k s", k=8)
        TFv = TF.v("p (a s) -> p a s", a=2)
        UTAv = UTA.v("p (k s) -> p k s", k=16)
        UTBv = UTB.v("p (k s) -> p k s", k=16)

        def UTc(ch):
            return (UTAv if ch < 16 else UTBv)[:, ch % 16, :]

        def UTr(ch):
            return (UTA if ch < 16 else UTB).r((ch % 16) * 512, (ch % 16 + 1) * 512)
        YF0v = YF0.v("p (t c) -> p t c", t=4)
        def p2_step1(b):
            g0 = own0 + b * 512
            for t in range(4):
                x2 = X2[t % 2]
                P.dma(x2.ap, xo[g0 + t * 128: g0 + (t + 1) * 128, :], writes=x2.r())
                ssap, ssr = SC(4 * t)
                P.op("act", lambda e: e.activation(out=TA.ap, in_=x2.ap, func=AF.Square, accum_out=ssap),
                     reads=x2.r(), writes=TA.r() + ssr)
                act_rstd(ssap, ssap, D, ssr, ssr)
                P.op("dve", lambda e: e.tensor_scalar(out=XN2.ap, in0=x2.ap, scalar1=ssap, scalar2=None,
                                                      op0=ALU.mult),
                     reads=x2.r() + ssr, writes=XN2.r())
                transpose_to(XN2, XNTv[:, :, t * 128:(t + 1) * 128], XNT.r(), 0, evac="act")

        for b in range(NB):
            s0 = b * 512
            g0 = own0 + s0
            if b == 0:
                p2_step1(0)
            if STOP == 4:
                break
            P.dma(TFv, tabF[:, :, g0:g0 + 512], writes=TF.r())

            def proj_rope(WAx, WBx, dstv, dst, is_q):
                wa = WAx.ap.rearrange("p (k c) -> p k c", k=8)
                wb = WBx.ap.rearrange("p (k c) -> p k c", k=8)
                for p_ in range(4):
                    ba, bb = (p_ % 2) * 2, (p_ % 2) * 2 + 1

                    def mm(e):
                        for kc in range(8):
                            e.matmul(PS(ba)[:, :], lhsT=wa[:, kc, p_ * 128:(p_ + 1) * 128], rhs=XNTv[:, kc, :],
                                     start=(kc == 0), stop=(kc == 7))
                        for kc in range(8):
                            ins = e.matmul(PS(bb)[:, :], lhsT=wb[:, kc, p_ * 128:(p_ + 1) * 128], rhs=XNTv[:, kc, :],
                                           start=(kc == 0), stop=(kc == 7))
                        return ins
                    P.op("pe", mm, reads=XNT.r() + WAx.r() + WBx.r(), writes=PSR(ba) + PSR(bb))
                    P.op("dve", lambda e: e.tensor_tensor(out=TA.ap[:, 0:512], in0=PS(ba)[:, :],
                                                          in1=TFv[:, 0, :], op=ALU.mult),
                         reads=PSR(ba) + TF.r(), writes=TA.r(0, 512))
                    P.op("dve", lambda e: e.tensor_tensor(out=TB.ap[:, 0:512], in0=PS(bb)[:, :],
                                                          in1=TFv[:, 1, :], op=ALU.mult),
                         reads=PSR(bb) + TF.r(), writes=TB.r(0, 512))
                    he, ho = 2 * p_, 2 * p_ + 1
                    P.op("dve", lambda e: e.tensor_tensor(out=dstv[0:64, he, :], in0=TA.ap[0:64, 0:512],
                                                          in1=TB.ap[0:64, 0:512], op=ALU.add),
                         reads=TA.r(0, 512, half=0) + TB.r(0, 512, half=0), writes=dst.r(he * 512, (he + 1) * 512, half=0))
                    P.op("dve", lambda e: e.tensor_tensor(out=dstv[64:128, ho, :], in0=TA.ap[64:128, 0:512],
                                                          in1=TB.ap[64:128, 0:512], op=ALU.add),
                         reads=TA.r(0, 512, half=1) + TB.r(0, 512, half=1), writes=dst.r(ho * 512, (ho + 1) * 512, half=1))
                    P.dma(dstv[0:64, ho, :], dstv[64:128, ho, :], reads=dst.r(ho * 512, (ho + 1) * 512, half=1),
                          writes=dst.r(ho * 512, (ho + 1) * 512, half=0))
                    if is_q:
                        P.dma(dstv[64:128, he, :], dstv[0:64, he, :], reads=dst.r(he * 512, (he + 1) * 512, half=0),
                              writes=dst.r(he * 512, (he + 1) * 512, half=1))
            iqa, Wqa = WS.get("rqa")
            iqb, Wqb = WS.get("rqb")
            proj_rope(Wqa, Wqb, RQTv, RQT, True)
            WS.release(iqa); WS.release(iqb)
            ika, Wka = WS.get("rka")
            ikb, Wkb = WS.get("rkb")
            proj_rope(Wka, Wkb, RKTv, RKT, False)
            WS.release(ika); WS.release(ikb)
            if STOP == 5:
                break
            for wi, nm in enumerate(("rv0", "rv1", "rg0", "rg1")):
                iw, Ww = WS.get(nm)
                wv = Ww.ap.rearrange("p (k c) -> p k c", k=8)
                hf = wi % 2
                for t in range(4):
                    bank = 4 + (t % 2)

                    def mm(e, t=t, bank=bank, wv=wv):
                        for kc in range(8):
                            ins = e.matmul(PS(bank)[:, :], lhsT=XNTv[:, kc, t * 128:(t + 1) * 128], rhs=wv[:, kc, :],
                                           start=(kc == 0), stop=(kc == 7))
                        return ins
                    P.op("pe", mm, reads=XNT.r() + Ww.r(), writes=PSR(bank))
                    if wi < 2:
                        P.op("act", lambda e, t=t, bank=bank, hf=hf: e.copy(out=RVBv[:, t, hf * 512:(hf + 1) * 512],
                                                                            in_=PS(bank)[:, :]),
                             reads=PSR(bank), writes=RVB.r(t * 1024 + hf * 512, t * 1024 + (hf + 1) * 512))
                    else:
                        P.op("act", lambda e, t=t, bank=bank, hf=hf: e.activation(
                            out=SRGv[:, t, hf * 512:(hf + 1) * 512], in_=PS(bank)[:, :], func=AF.Silu),
                            reads=PSR(bank), writes=SRG.r(t * 1024 + hf * 512, t * 1024 + (hf + 1) * 512))
                WS.release(iw)
            if STOP == 6:
                break
            if STOP >= 70:
                pass
            def ret_A(t, b=b):
                n = b * 4 + t
                tc0 = t * 128
                ob0 = 4 if t % 2 == 0 else 2
                for hg in range(2):
                    sb_ = hg
                    smb, qp = SMB[hg], QP[hg]
                    smv = smb.v("p (h c) -> p h c", h=4)
                    qpv = qp.v("p (h c) -> p h c", h=4)

                    def mm(e):
                        for hi in range(4):
                            h = hg * 4 + hi
                            ins = e.matmul(PS(sb_)[:, hi * 128:(hi + 1) * 128],
                                           lhsT=RKTv[0:64, h, tc0:tc0 + 128],
                                           rhs=RQTv[0:64, h, tc0:tc0 + 128], start=True, stop=True)
                        return ins
                    P.op("pe", mm, reads=RKT.r() + RQT.r(), writes=PSR(sb_))
                    P.op("dve", lambda e: e.tensor_tensor(
                        out=smv, in0=PS(sb_)[:, :].rearrange("p (h c) -> p h c", h=4),
                        in1=DTv[:, hg * 4:(hg + 1) * 4, :], op=ALU.mult),
                        reads=PSR(sb_) + DT_.r(), writes=smb.r())
                    P.op("dve", lambda e: e.tensor_tensor(
                        out=qpv, in0=RQTv[:, hg * 4:(hg + 1) * 4, tc0:tc0 + 128],
                        in1=QDv[:, hg * 4:(hg + 1) * 4, :], op=ALU.mult),
                        reads=RQT.r() + QDEC.r(), writes=qp.r())

                    def omm(e):
                        for hi in range(4):
                            h = hg * 4 + hi
                            e.matmul(PS(ob0 + hg)[:, hi * 128:(hi + 1) * 128], lhsT=smv[:, hi, :],
                                     rhs=RVBv[:, t, h * 128:(h + 1) * 128], start=True, stop=False)
                            ins = e.matmul(PS(ob0 + hg)[:, hi * 128:(hi + 1) * 128], lhsT=qpv[:, hi, :],
                                           rhs=STv[:, n, h, :], start=False, stop=True)
                        return ins
                    P.op("pe", omm, reads=smb.r() + qp.r() + RVB.r(t * 1024, (t + 1) * 1024) +
                         ST.r(n * 1024, (n + 1) * 1024), writes=PSR(ob0 + hg))

            def ret_B(t, b=b):
                tc0 = t * 128
                ob0 = 4 if t % 2 == 0 else 2
                TAv = TA.v("p (h v) -> p h v", h=8)
                for hg in range(2):
                    P.op("act", lambda e: e.activation(out=TB.ap[:, hg * 512:(hg + 1) * 512],
                                                       in_=PS(ob0 + hg)[:, :], func=AF.Square),
                         reads=PSR(ob0 + hg), writes=TB.r(hg * 512, (hg + 1) * 512))
                ss8, ss8r = SC(16 + 8 * (t % 2), 8)
                P.op("dve", lambda e: e.tensor_reduce(out=ss8, in_=TB.v("p (h v) -> p h v", h=8), axis=AX.X,
                                                      op=ALU.add),
                     reads=TB.r(), writes=ss8r)
                act_rstd(ss8, ss8, 128, ss8r, ss8r)
                for hg in range(2):
                    P.op("dve", lambda e: e.tensor_tensor(
                        out=TAv[:, hg * 4:(hg + 1) * 4, :], in0=PS(ob0 + hg)[:, :].rearrange("p (h v) -> p h v", h=4),
                        in1=ss8[:, hg * 4:(hg + 1) * 4].unsqueeze(2).broadcast_to([128, 4, 128]), op=ALU.mult),
                        reads=PSR(ob0 + hg) + ss8r, writes=TA.r(hg * 512, (hg + 1) * 512))
                P.op("dve", lambda e: e.tensor_tensor(out=OB.ap, in0=TA.ap, in1=SRGv[:, t, :], op=ALU.mult),
                     reads=TA.r() + SRG.r(t * 1024, (t + 1) * 1024), writes=OB.r())
                transpose_to(OB, OBTv[:, :, tc0:tc0 + 128], OBT.r(), 7, evac="act")

            ret_A(0)
            for t in range(4):
                if t + 1 < 4:
                    ret_A(t + 1)
                ret_B(t)
            if STOP in (7, 70, 71, 72, 73):
                break
            for c in range(8):
                im, Wm = WS.get("mix%d" % c)
                wG = Wm.ap[:, 0:2048].rearrange("p (k c) -> p k c", k=8)
                wB = Wm.ap[:, 2048:3072].rearrange("p (k c) -> p k c", k=8)
                wAh = Wm.ap[0:64, 3072:4096].rearrange("p (k c) -> p k c", k=8)
                bb = (c % 2) * 4

                def mm(e, wG=wG, wB=wB, wAh=wAh, bb=bb, s0=s0):
                    for kc in range(8):
                        e.matmul(PS(bb)[:, :], lhsT=wG[:, kc, 0:128], rhs=XNTv[:, kc, :], start=(kc == 0), stop=(kc == 7))
                    for kc in range(8):
                        e.matmul(PS(bb + 1)[:, :], lhsT=wG[:, kc, 128:256], rhs=XNTv[:, kc, :], start=(kc == 0),
                                 stop=(kc == 7))
                    for hh in range(8):
                        e.matmul(PS(bb + 2)[:, :], lhsT=wAh[:, hh, :], rhs=OTv[0:64, hh, s0:s0 + 512],
                                 start=(hh == 0), stop=(hh == 7))
                    for kc in range(8):
                        ins = e.matmul(PS(bb + 3)[:, :], lhsT=wB[:, kc, :], rhs=OBTv[:, kc, :], start=(kc == 0),
                                       stop=(kc == 7))
                    return ins
                P.op("pe", mm, reads=Wm.r() + XNT.r() + OT.r() + OBT.r(),
                     writes=PSR(bb) + PSR(bb + 1) + PSR(bb + 2) + PSR(bb + 3))
                for gi, (tmp, lo) in enumerate(((TA, 0), (TA, 512))):
                    tv = tmp.ap[:, lo:lo + 512]
                    rr_ = tmp.r(lo, lo + 512)
                    P.op("act", lambda e, tv=tv, gi=gi, bb=bb: e.activation(out=tv, in_=PS(bb + gi)[:, :],
                                                                           func=AF.Sigmoid),
                         reads=PSR(bb + gi), writes=rr_)
                    P.op("dve", lambda e, tv=tv, gi=gi, bb=bb: e.tensor_tensor(out=tv, in0=PS(bb + 2 + gi)[:, :],
                                                                               in1=tv, op=ALU.mult),
                         reads=PSR(bb + 2 + gi) + rr_, writes=rr_)
                P.op("dve", lambda e, c=c: e.tensor_tensor(out=MTv[:, c, :], in0=TA.ap[:, 0:512],
                                                            in1=TA.ap[:, 512:1024], op=ALU.add),
                     reads=TA.r(), writes=MT.r(c * 512, (c + 1) * 512))
                WS.release(im)
            if STOP == 8:
                break
            if dbg is not None and j == 0 and b == 0:
                P.dma(d_mt, MT.ap, reads=MT.r(), final=True)
                P.dma(d_obt, OBT.ap, reads=OBT.r(), final=True)
                P.dma(d_ot, OT.ap, reads=OT.r(), final=True)
                P.dma(d_rqt, RQT.ap, reads=RQT.r(), final=True)
                P.dma(d_rkt, RKT.ap, reads=RKT.r(), final=True)
                P.dma(d_st, ST.ap, reads=ST.r(), final=True)
            io0, Wo0 = WS.get("wo0")
            io1, Wo1 = WS.get("wo1")
            wo = [Wo0.ap.rearrange("p (k c) -> p k c", k=8), Wo1.ap.rearrange("p (k c) -> p k c", k=8)]
            for t in range(4):
                P.dma(X1v[:, t, :], xo[g0 + t * 128: g0 + (t + 1) * 128, :], writes=X1.r(t * 1024, (t + 1) * 1024))
            for t in range(4):
                b0 = (t % 2) * 2

                def mm(e, t=t, b0=b0):
                    for hf in range(2):
                        for kc in range(8):
                            ins = e.matmul(PS(b0 + hf)[:, :], lhsT=MTv[:, kc, t * 128:(t + 1) * 128],
                                           rhs=wo[hf][:, kc, :], start=(kc == 0), stop=(kc == 7))
                    return ins
                P.op("pe", mm, reads=MT.r() + Wo0.r() + Wo1.r(), writes=PSR(b0) + PSR(b0 + 1))
                s6, s6r = SC(32 + 4 * t, 3)
                for hf in range(2):
                    P.op("act", lambda e, hf=hf: e.activation(
                        out=TB.ap[:, hf * 512:(hf + 1) * 512], in_=PS(b0 + hf)[:, :], func=AF.Square,
                        accum_out=s6[:, hf:hf + 1]),
                        reads=PSR(b0 + hf), writes=TB.r(hf * 512, (hf + 1) * 512) + s6r)
                ssm = s6[:, 2:3]
                P.op("dve", lambda e: e.tensor_tensor(out=ssm, in0=s6[:, 0:1], in1=s6[:, 1:2], op=ALU.add),
                     reads=s6r, writes=s6r)
                act_rstd(ssm, ssm, D, s6r, s6r)
                for hf in range(2):
                    P.op("dve", lambda e, hf=hf: e.scalar_tensor_tensor(
                        out=TA.ap[:, hf * 512:(hf + 1) * 512], in0=PS(b0 + hf)[:, :], scalar=ssm,
                        in1=G1.ap[:, hf * 512:(hf + 1) * 512], op0=ALU.mult, op1=ALU.mult),
                        reads=PSR(b0 + hf) + s6r + G1.r(), writes=TA.r(hf * 512, (hf + 1) * 512))
                P.op("dve", lambda e, t=t: e.tensor_tensor(out=X1v[:, t, :], in0=X1v[:, t, :], in1=TA.ap, op=ALU.add),
                     reads=TA.r() + X1.r(t * 1024, (t + 1) * 1024), writes=X1.r(t * 1024, (t + 1) * 1024))
            WS.release(io0); WS.release(io1)
            if STOP == 81:
                break
            if dbg is not None:
                for t in range(4):
                    P.dma(dbg[g0 + t * 128: g0 + (t + 1) * 128, :], X1v[:, t, :], reads=X1.r(t * 1024, (t + 1) * 1024),
                          final=True)
            for t in range(4):
                ssap, s7r = SC(48 + 4 * t)
                P.op("act", lambda e, t=t, ssap=ssap: e.activation(out=TA.ap, in_=X1v[:, t, :], func=AF.Square,
                                                                    accum_out=ssap),
                     reads=X1.r(t * 1024, (t + 1) * 1024), writes=TA.r() + s7r)
                act_rstd(ssap, ssap, D, s7r, s7r)
                P.op("dve", lambda e, t=t, ssap=ssap: e.tensor_scalar(out=XN2.ap, in0=X1v[:, t, :], scalar1=ssap,
                                                                      scalar2=None, op0=ALU.mult),
                     reads=X1.r(t * 1024, (t + 1) * 1024) + s7r, writes=XN2.r())
                transpose_to(XN2, XNTv[:, :, t * 128:(t + 1) * 128], XNT.r(), 7, evac="act")
            if STOP == 82:
                break
            for g in range(8):
                iu, Wu = WS.get("wu%d" % g)
                wu = Wu.ap.rearrange("p (k c) -> p k c", k=8)
                for mc in range(4):
                    ch = g * 4 + mc
                    bank = ch % 4
                    rl = RL[ch % 4]

                    def mm(e, mc=mc, bank=bank, wu=wu):
                        for kc in range(8):
                            ins = e.matmul(PS(bank)[:, :], lhsT=wu[:, kc, mc * 128:(mc + 1) * 128], rhs=XNTv[:, kc, :],
                                           start=(kc == 0), stop=(kc == 7))
                        return ins
                    P.op("pe", mm, reads=Wu.r() + XNT.r(), writes=PSR(bank))
                    P.op("act", lambda e, bank=bank, rl=rl: e.activation(out=rl.ap, in_=PS(bank)[:, :], func=AF.Relu),
                         reads=PSR(bank), writes=rl.r())
                    P.op("dve", lambda e, ch=ch, rl=rl: e.tensor_tensor(
                        out=UTc(ch), in0=rl.ap, in1=rl.ap, op=ALU.mult),
                        reads=rl.r(), writes=UTr(ch))
                WS.release(iu)
            if STOP == 83:
                break
            for hf in range(2):
                for g in range(4):
                    idn, Wd = WS.get("wd%d_%d" % (hf, g))
                    wd = Wd.ap.rearrange("p (k c) -> p k c", k=8)
                    for t in range(4):
                        def mm(e, t=t, g=g, wd=wd):
                            for k8 in range(8):
                                kc = g * 8 + k8
                                ins = e.matmul(PS(4 + t)[:, :], lhsT=UTc(kc)[:, t * 128:(t + 1) * 128], rhs=wd[:, k8, :],
                                               start=(kc == 0), stop=(kc == 31))
                            return ins
                        P.op("pe", mm, reads=Wd.r() + (UTA if g < 2 else UTB).r((g % 2) * 4096, (g % 2 + 1) * 4096),
                             writes=PSR(4 + t))
                    WS.release(idn)
                if hf == 0:
                    for t in range(4):
                        sf, sfr = SC(64 + 4 * t, 3)
                        P.op("act", lambda e, t=t, sf=sf: e.activation(out=YF0v[:, t, :], in_=PS(4 + t)[:, :], func=AF.Square,
                                                                       accum_out=sf[:, 0:1]),
                             reads=PSR(4 + t), writes=YF0.r(t * 512, (t + 1) * 512) + sfr)
                        P.op("dve", lambda e, t=t: e.tensor_copy(out=YF0v[:, t, :], in_=PS(4 + t)[:, :]),
                             reads=PSR(4 + t) + YF0.r(t * 512, (t + 1) * 512), writes=YF0.r(t * 512, (t + 1) * 512))
            if STOP == 84:
                break
            if b + 1 < NB:
                p2_step1(b + 1)
            for t in range(4):
                sf, sfr = SC(64 + 4 * t, 3)
                tmp = TA if t % 2 == 0 else TB
                P.op("act", lambda e, t=t, sf=sf, tmp=tmp: e.activation(out=tmp.ap[:, 0:512], in_=PS(4 + t)[:, :],
                                                                        func=AF.Square, accum_out=sf[:, 1:2]),
                     reads=PSR(4 + t), writes=tmp.r(0, 512) + sfr)
                ssy = sf[:, 2:3]
                P.op("dve", lambda e, sf=sf, ssy=ssy: e.tensor_tensor(out=ssy, in0=sf[:, 0:1], in1=sf[:, 1:2], op=ALU.add),
                     reads=sfr, writes=sfr)
                act_rstd(ssy, ssy, D, sfr, sfr)
                P.op("dve", lambda e, t=t, ssy=ssy, tmp=tmp: e.scalar_tensor_tensor(
                    out=tmp.ap[:, 0:512], in0=YF0v[:, t, :], scalar=ssy, in1=G2.ap[:, 0:512], op0=ALU.mult,
                    op1=ALU.mult), reads=YF0.r(t * 512, (t + 1) * 512) + sfr + G2.r() + tmp.r(0, 512),
                    writes=tmp.r(0, 512))
                P.op("dve", lambda e, t=t, ssy=ssy, tmp=tmp: e.scalar_tensor_tensor(
                    out=tmp.ap[:, 512:1024], in0=PS(4 + t)[:, :], scalar=ssy, in1=G2.ap[:, 512:1024], op0=ALU.mult,
                    op1=ALU.mult), reads=PSR(4 + t) + sfr + G2.r(), writes=tmp.r(512, 1024))
                P.op("dve", lambda e, t=t, tmp=tmp: e.tensor_tensor(out=X1v[:, t, :], in0=tmp.ap, in1=X1v[:, t, :], op=ALU.add),
                     reads=tmp.r() + X1.r(t * 1024, (t + 1) * 1024), writes=X1.r(t * 1024, (t + 1) * 1024))
                P.dma(y[g0 + t * 128: g0 + (t + 1) * 128, :], X1v[:, t, :], reads=X1.r(t * 1024, (t + 1) * 1024),
                      final=True)

        key_off += nkt * 128
        ctx_off += nctx
        if 4 <= STOP < 9 or STOP >= 70:
            break

    P.finish()
    es.close()
    return nc, P


def rope_tab(pos, dim):
    inv = (1.0 / (10000.0 ** (np.arange(0, dim, 2, dtype=np.float32) / np.float32(dim)))).astype(np.float32)
    ang = pos.astype(np.float32)[:, None] * inv[None, :]
    return np.cos(ang).astype(np.float32), np.sin(ang).astype(np.float32)


def make_core_inputs(cfg, jobs, weights):
    S_OWN = cfg["S_OWN"]
    NT = S_OWN // 128
    xo = np.concatenate([jb["x_own"] for jb in jobs], 0)
    xcs = [jb["x_ctx"] for jb in jobs if jb["x_ctx"].shape[0] > 0]
    xc = np.concatenate(xcs, 0) if xcs else np.zeros((128, D), np.float32)
    tabT, tabM, tabF, cDl, cFl = [], [], [], [], []
    for jb in jobs:
        pos = np.concatenate([jb["pos_own"], jb["pos_ctx"]]).astype(np.int64)
        cr, sr = rope_tab(pos, 64)
        tabT.append(np.concatenate([cr, sr], 1))
        cm, sm = rope_tab(pos, 32)
        tabM.append(np.stack([np.concatenate([cm, cm], 1).T, np.concatenate([sm, sm], 1).T], 1))
        cro, sro = rope_tab(jb["pos_own"].astype(np.int64), 64)
        c64 = np.concatenate([cro, cro], 1).T
        s64 = np.concatenate([sro, sro], 1).T
        tabF.append(np.stack([np.concatenate([c64, c64], 0), np.concatenate([s64, s64], 0)], 1))
        nc_ = jb["pos_ctx"].shape[0] // 128
        if nc_:
            g0 = int(jb["pos_own"][0]) // 128
            gl = g0 + NT - 1
            gm = jb["pos_ctx"][::128].astype(np.int64) // 128
            dD = np.zeros((128, nc_), np.float32)
            dF = np.zeros((128, nc_), np.float32)
            left = gm < g0
            right = gm > gl
            dD[0:64, left] = 128.0 * (g0 - gm[left] - 1)
            dF[0:64, left] = 1.0
            dD[64:128, right] = 128.0 * (gm[right] - gl - 1)
            dF[64:128, right] = 1.0
            cDl.append(dD)
            cFl.append(dF)
    m = {
        "xo": np.ascontiguousarray(xo, np.float32),
        "xc": np.ascontiguousarray(xc, np.float32),
        "tabT": np.ascontiguousarray(np.concatenate(tabT, 0), np.float32),
        "tabM": np.ascontiguousarray(np.concatenate(tabM, 2), np.float32),
        "tabF": np.ascontiguousarray(np.concatenate(tabF, 2), np.float32),
        "cD": np.ascontiguousarray(np.concatenate(cDl, 1) if cDl else np.zeros((128, 1), np.float32)),
        "cF": np.ascontiguousarray(np.concatenate(cFl, 1) if cFl else np.zeros((128, 1), np.float32)),
    }
    m.update(weights)
    return m


def prep_weights(inp):
    w = {}
    for k in ("w_in", "w_q_up", "w_kv_up", "w_branch_a", "w_branch_b", "w_out", "w_up", "w_down",
              "g_pre_mix", "g_q_norm", "g_kv_norm", "g_post_mix", "g_pre_mlp", "g_post_mlp"):
        w[k] = np.ascontiguousarray(np.asarray(inp[k], np.float32)[0])
    w["lgf"] = np.ascontiguousarray(np.asarray(inp["ret_log_decay_fwd"], np.float32)[0])
    w["lgb"] = np.ascontiguousarray(np.asarray(inp["ret_log_decay_bwd"], np.float32)[0])
    return w


CFG = {"S_OWN": 2048, "nctx": [0, 0, 0, 0, 48, 48]}
_CACHE = {}


def kernel(**inputs):
    xp = np.asarray(inputs["x_prompt"], np.float32)
    xs = np.asarray(inputs["x_sample"], np.float32)
    weights = prep_weights(inputs)
    cfg = CFG
    S = cfg["S_OWN"]
    in_maps = []
    for core in range(8):
        jobs = []
        for i in range(4):
            jobs.append(dict(x_own=xp[core * 4 + i], pos_own=np.arange(S),
                             x_ctx=np.zeros((0, D), np.float32), pos_ctx=np.zeros((0,), np.int64)))
        sb, half = core // 2, core % 2
        for sj in range(2):
            a0 = half * 4096 + sj * S
            own_pos = np.arange(a0, a0 + S)
            ctx_pos = np.concatenate([np.arange(0, a0), np.arange(a0 + S, 8192)])
            jobs.append(dict(x_own=xs[sb, a0:a0 + S], pos_own=own_pos,
                             x_ctx=xs[sb][ctx_pos], pos_ctx=ctx_pos))
        in_maps.append(make_core_inputs(cfg, jobs, weights))
    if "nc" not in _CACHE:
        _CACHE["nc"] = build(cfg)[0]
    res = run_bass_kernel_spmd(_CACHE["nc"], in_maps, core_ids=list(range(8)))
    yp = np.zeros_like(xp)
    ys = np.zeros_like(xs)
    for core in range(8):
        yy = res.results[core]["y"]
        for i in range(4):
            yp[core * 4 + i] = yy[i * S:(i + 1) * S]
        sb, half = core // 2, core % 2
        for sj in range(2):
            a0 = half * 4096 + sj * S
            ys[sb, a0:a0 + S] = yy[(4 + sj) * S:(5 + sj) * S]
    return (yp, ys)
```

```python
import math
import types
from contextlib import ExitStack
import numpy as np
import concourse.bass as bass
import concourse.mybir as mybir
from concourse.bass_utils import run_bass_kernel_spmd

F32 = mybir.dt.float32
BF16 = mybir.dt.bfloat16
I32 = mybir.dt.int32
AF = mybir.ActivationFunctionType
ALU = mybir.AluOpType
AX = mybir.AxisListType

D = 1024
OFF = dict(cq=0, ckv=256, kr=384, rq=416, rk=928, rv=1440, rg=2464, ga=3488, gb=4512)
EPS = 1e-6
ATT_SCALE = 96 ** -0.5


def freeze(fn):
    if fn.__closure__ is None:
        return fn
    cells = []
    for c in fn.__closure__:
        try:
            cells.append(types.CellType(c.cell_contents))
        except ValueError:
            cells.append(c)
    return types.FunctionType(fn.__code__, fn.__globals__, fn.__name__, fn.__defaults__, tuple(cells))


class Res:
    __slots__ = ("w", "r")

    def __init__(self):
        self.w = None
        self.r = []


class Reg:
    def __init__(self, nbytes, gran):
        self.gran = gran
        self.n = (nbytes + gran - 1) // gran
        self.res = [(Res(), Res()) for _ in range(self.n)]

    def r(self, lo=0, hi=None, half=None):
        if hi is None:
            hi = self.n * self.gran
        out = []
        for s in range(lo // self.gran, (hi - 1) // self.gran + 1):
            if half in (None, 0):
                out.append(self.res[s][0])
            if half in (None, 1):
                out.append(self.res[s][1])
        return out


class LB:
    def __init__(self, tile, reg, off, size, dt):
        self.tile, self.reg, self.off, self.size, self.dt = tile, reg, off, size, dt
        self.isz = 4 if dt in (F32, I32) else 2
        a = tile[:, off // 2:(off + size) // 2]
        self.ap = a if dt == BF16 else a.bitcast(dt)

    def v(self, pat=None, **kw):
        return self.ap if pat is None else self.ap.rearrange(pat, **kw)

    def r(self, lo=0, hi=None, half=None):
        hi_b = self.size if hi is None else hi * self.isz
        return self.reg.r(self.off + lo * self.isz, self.off + hi_b, half)


class Prog:
    COMPUTE = ("pe", "dve", "act", "pool")
    ALL = ("pe", "dve", "act", "pool", "sp")

    def __init__(self, nc, n_dma_sems=14):
        self.nc = nc
        self.q = {e: [] for e in self.ALL}
        self.cnt = {e: 0 for e in self.COMPUTE}
        self.psem = {}
        self.seen = {e: {} for e in self.ALL}
        self._ctx = []
        for e in self.COMPUTE:
            g = nc.semaphore("prog_" + e)
            self.psem[e] = g.__enter__()
            self._ctx.append(g)
        self.dsem, self.dsem_val, self.dsem_next = {}, {}, {}
        for qn in ("sp", "pool"):
            lst = []
            for i in range(n_dma_sems):
                g = nc.semaphore("dma_%s_%d" % (qn, i))
                lst.append(g.__enter__())
                self._ctx.append(g)
            self.dsem[qn] = lst
            self.dsem_val[qn] = [0] * n_dma_sems
            self.dsem_next[qn] = 0
        self.final_tokens = []
        self.n_ops = 0

    def _deps(self, reads, writes):
        toks = []
        for r in reads:
            if r.w is not None:
                toks.append(r.w)
        for w in writes:
            if w.w is not None:
                toks.append(w.w)
            toks.extend(w.r)
        return toks

    def _emit_waits(self, e, toks):
        need = {}
        seen = self.seen[e]
        for (sem, val, src) in toks:
            if src == e and e == "pe":
                continue
            k = id(sem)
            if seen.get(k, 0) >= val:
                continue
            if k not in need or need[k][1] < val:
                need[k] = (sem, val)
        for k, (sem, val) in need.items():
            seen[k] = val
            self.q[e].append(("wait", sem, val))

    def _commit(self, tok, reads, writes):
        for r in reads:
            r.r.append(tok)
        for w in writes:
            w.w = tok
            w.r = []

    def op(self, e, fn, reads=(), writes=()):
        self._emit_waits(e, self._deps(reads, writes))
        self.cnt[e] += 1
        tok = (self.psem[e], self.cnt[e], e)
        self.q[e].append(("op", freeze(fn), self.psem[e]))
        self._commit(tok, reads, writes)
        self.n_ops += 1

    def dma(self, out_ap, in_ap, reads=(), writes=(), final=False, qn="sp", **kw):
        i = self.dsem_next[qn]
        self.dsem_next[qn] = (i + 1) % len(self.dsem[qn])
        sem = self.dsem[qn][i]
        toks = self._deps(reads, writes)
        prev = self.dsem_val[qn][i]
        if prev > 0:
            toks.append((sem, prev, None))
        self._emit_waits(qn, toks)
        val = prev + 16
        self.dsem_val[qn][i] = val
        tok = (sem, val, None)
        self.q[qn].append(("dma", out_ap, in_ap, sem, kw))
        self._commit(tok, reads, writes)
        if final:
            self.final_tokens.append(tok)
        self.n_ops += 1

    def finish(self):
        nc = self.nc
        self._emit_waits("sp", self.final_tokens)
        qs = self.q

        def run(e, engine):
            for item in qs[e]:
                if item[0] == "wait":
                    engine.wait_ge(item[1], item[2])
                elif item[0] == "op":
                    item[1](engine).then_inc(item[2], 1)
                else:
                    _, o, i_, sem, kw = item
                    engine.dma_start(out=o, in_=i_, **kw).then_inc(sem, 16)

        with nc.Block() as block:
            @block.tensor
            def _(t):
                run("pe", t)

            @block.vector
            def _(v):
                run("dve", v)

            @block.scalar
            def _(s):
                run("act", s)

            @block.gpsimd
            def _(g):
                run("pool", g)

            @block.sync
            def _(s):
                run("sp", s)
        for g in reversed(self._ctx):
            g.__exit__(None, None, None)


def build(cfg):
    S_OWN = cfg["S_OWN"]
    NCTX = list(cfg["nctx"])
    NJ = len(NCTX)
    NT = S_OWN // 128
    NB = S_OWN // 512
    NKT = [NT + c for c in NCTX]
    MAXKT = max(NKT)
    TOTKT = sum(NKT)
    TOTC = max(1, sum(NCTX))
    MAXC = max(1, max(NCTX))

    nc = bass.Bass("TRN2", target_bir_lowering=False)
    es = ExitStack()

    def din(name, shape, dt=F32):
        return nc.dram_tensor(name, list(shape), dt, kind="ExternalInput").ap()

    xo = din("xo", [NJ * S_OWN, D])
    xc = din("xc", [TOTC * 128, D])
    tabT = din("tabT", [TOTKT * 128, 64])
    tabM = din("tabM", [32, 2, TOTKT * 128])
    tabF = din("tabF", [128, 2, NJ * S_OWN])
    cD = din("cD", [128, TOTC])
    cF = din("cF", [128, TOTC])
    w_in = din("w_in", [D, 5536])
    w_q_up = din("w_q_up", [256, 768])
    w_kv_up = din("w_kv_up", [128, 1024])
    w_a = din("w_branch_a", [512, 1024])
    w_b = din("w_branch_b", [1024, 1024])
    w_out = din("w_out", [1024, 1024])
    w_up = din("w_up", [1024, 4096])
    w_down = din("w_down", [4096, 1024])
    g_pre_mix = din("g_pre_mix", [D])
    g_q_norm = din("g_q_norm", [256])
    g_kv_norm = din("g_kv_norm", [128])
    g_post_mix = din("g_post_mix", [D])
    g_pre_mlp = din("g_pre_mlp", [D])
    g_post_mlp = din("g_post_mlp", [D])
    lgf = din("lgf", [8])
    lgb = din("lgb", [8])
    y = nc.dram_tensor("y", [NJ * S_OWN, D], F32, kind="ExternalOutput").ap()

    P = Prog(nc)
    dbg = nc.dram_tensor("dbg", [NJ * S_OWN, D], F32, kind="ExternalOutput").ap() if cfg.get("dbg") else None
    if cfg.get("dbg"):
        d_mt = nc.dram_tensor("d_mt", [128, 8 * 512], BF16, kind="ExternalOutput").ap()
        d_obt = nc.dram_tensor("d_obt", [128, 8 * 512], BF16, kind="ExternalOutput").ap()
        d_ot = nc.dram_tensor("d_ot", [128, 8 * S_OWN], BF16, kind="ExternalOutput").ap()
        d_rqt = nc.dram_tensor("d_rqt", [128, 8 * 512], BF16, kind="ExternalOutput").ap()
        d_rkt = nc.dram_tensor("d_rkt", [128, 8 * 512], BF16, kind="ExternalOutput").ap()
        d_st = nc.dram_tensor("d_st", [128, NT * 1024], BF16, kind="ExternalOutput").ap()

    def sbuf(name, nbytes, gran=512):
        t = es.enter_context(nc.sbuf_tensor(name, [128, nbytes // 2], BF16))
        return t, Reg(nbytes, gran)

    def mk(name, nbytes, dt, gran=512):
        t, reg = sbuf(name, nbytes, gran)
        return LB(t, reg, 0, nbytes, dt)

    ARENA = 80 * 1024 - 512
    ar_t, ar_reg = sbuf("arena", ARENA, 512)

    def AR(off, size, dt):
        assert off + size <= ARENA, (off, size)
        assert off % 512 == 0
        return LB(ar_t, ar_reg, off, size, dt)

    K = 1024
    OT = mk("OT", 8 * S_OWN * 2, BF16, 1024)
    ST = mk("ST", NT * 2048, BF16, 2048)
    RING = [mk("ring%d" % i, 8192, BF16, 8192) for i in range(4)]
    IDENT = mk("ident", 256, BF16)
    ONESB = mk("onesb", 256, BF16)
    DUP = mk("dup", 256, BF16)
    ONESF = mk("onesf", 256, F32)
    DT_ = mk("dt", 8 * 128 * 4, F32, 4096)
    QDEC = mk("qdec", 8 * 128 * 4, F32, 4096)
    KDEC = mk("kdec", 64, F32)
    DEC = mk("dec", 32, F32)
    LG = mk("lg", 32, F32)
    LGF = mk("lgf_t", 32, F32)
    LGB = mk("lgb_t", 32, F32)
    EPST = mk("eps", 4, F32)
    G1 = mk("g1", 4096, F32, 4096)
    G2 = mk("g2", 4096, F32, 4096)
    WQ = mk("wq", 2 * 1024 * 2, BF16, 4096)
    WKV = mk("wkv", 8 * 128 * 2, BF16, 4096)
    ACC = mk("acc", 4096, F32, 4096)
    COEF = mk("coef", MAXC * 8 * 4, F32, MAXC * 32)
    CDT = mk("cdt", MAXC * 4, F32, MAXC * 4)
    CFT = mk("cft", MAXC * 4, F32, MAXC * 4)
    STAT = mk("stat", 64 * 4, F32, 256)
    STS = mk("sts", 512, F32, 16)

    def SC(c0, n=1):
        return STS.ap[:, c0:c0 + n], STS.r(c0, c0 + n)

    PSB = []
    for i in range(8):
        t = es.enter_context(nc.psum_tensor("ps%d" % i, [128, 512], F32))
        PSB.append((t, Reg(2048, 2048)))

    def PS(i):
        return PSB[i][0]

    def PSR(i, half=None):
        return PSB[i][1].r(half=half)

    def PSbf(i):
        return PSB[i][0][:, :].bitcast(BF16)

    packs = {}

    def pack_tensor(name, rows, cols):
        t = nc.dram_tensor("wp_" + name, [rows, cols], BF16, kind="Internal").ap()
        packs[name] = (t, Res())
        return t

    def act_rstd(out_ap, in_ap, dim, reads, writes):
        P.op("act", lambda e: e.activation(out=out_ap, in_=in_ap, func=AF.Ln,
                                           scale=1.0 / dim, bias=EPST.ap[:, 0:1]),
             reads=list(reads) + EPST.r(), writes=writes)
        P.op("act", lambda e: e.activation(out=out_ap, in_=out_ap, func=AF.Exp, scale=-0.5),
             reads=writes, writes=writes)

    rr = {"i": 0}

    def alt(engs=("dve", "pool")):
        rr["i"] += 1
        return engs[rr["i"] % len(engs)]

    NSTG = 4
    STG = [AR(i * 4096, 4096, F32) for i in range(NSTG)]
    STB = [AR(16384 + i * 2048, 2048, BF16) for i in range(NSTG)]
    o_ = 24576
    TMPA = AR(o_, 4096, F32); o_ += 4096
    TMPB = AR(o_, 4096, F32); o_ += 4096
    TMPI = AR(o_, 512, I32); o_ += 512
    DIFF = AR(o_, 512, F32); o_ += 512
    MGE = AR(o_, 512, F32); o_ += 512
    MLT = AR(o_, 512, F32); o_ += 512
    PPOS = AR(o_, 512, F32); o_ += 512
    PNEG = AR(o_, 512, F32); o_ += 512
    AQ = AR(o_, 512, F32); o_ += 512
    IFR = AR(o_, 512, F32); o_ += 512
    JP = AR(o_, 512, F32); o_ += 512
    IDF = AR(o_, 512, F32); o_ += 512

    P.op("pool", lambda e: e.memset(EPST.ap, EPS), writes=EPST.r())
    P.op("pool", lambda e: e.memset(ONESB.ap, 1.0), writes=ONESB.r())
    P.op("pool", lambda e: e.memset(ONESF.ap, 1.0), writes=ONESF.r())
    P.op("pool", lambda e: e.memset(IDF.ap, 0.0), writes=IDF.r())
    P.op("pool", lambda e: e.affine_select(out=IDF.ap, in_=IDF.ap, pattern=[[-1, 128]],
                                           compare_op=ALU.not_equal, fill=1.0, base=0,
                                           channel_multiplier=1),
         reads=IDF.r(), writes=IDF.r())
    P.op("dve", lambda e: e.tensor_copy(out=IDENT.ap, in_=IDF.ap), reads=IDF.r(), writes=IDENT.r())
    for (pr, cs, ps_, cs2) in ((0, 0, 0, 0), (0, 64, 0, 0), (64, 0, 64, 64), (64, 64, 64, 64)):
        P.op("dve", lambda e, pr=pr, cs=cs, cs2=cs2: e.tensor_copy(
            out=DUP.ap[pr:pr + 64, cs:cs + 64], in_=IDF.ap[pr:pr + 64, cs2:cs2 + 64]),
            reads=IDF.r(), writes=DUP.r())
    P.op("pool", lambda e: e.iota(TMPI.ap, pattern=[[1, 128]], base=0, channel_multiplier=-1),
         writes=TMPI.r())
    P.op("dve", lambda e: e.tensor_copy(out=DIFF.ap, in_=TMPI.ap), reads=TMPI.r(), writes=DIFF.r())
    P.op("pool", lambda e: e.iota(TMPI.ap, pattern=[[1, 128]], base=0, channel_multiplier=0),
         reads=TMPI.r(), writes=TMPI.r())
    P.op("dve", lambda e: e.tensor_copy(out=IFR.ap, in_=TMPI.ap), reads=TMPI.r(), writes=IFR.r())
    P.op("pool", lambda e: e.iota(TMPI.ap, pattern=[[0, 128]], base=0, channel_multiplier=1),
         reads=TMPI.r(), writes=TMPI.r())
    P.op("dve", lambda e: e.tensor_copy(out=JP.ap, in_=TMPI.ap), reads=TMPI.r(), writes=JP.r())
    P.op("dve", lambda e: e.tensor_scalar(out=MGE.ap, in0=DIFF.ap, scalar1=0.0, scalar2=None,
                                          op0=ALU.is_ge), reads=DIFF.r(), writes=MGE.r())
    P.op("dve", lambda e: e.tensor_scalar(out=MLT.ap, in0=DIFF.ap, scalar1=0.0, scalar2=None,
                                          op0=ALU.is_lt), reads=DIFF.r(), writes=MLT.r())
    P.op("dve", lambda e: e.tensor_scalar(out=PPOS.ap, in0=DIFF.ap, scalar1=0.0, scalar2=None,
                                          op0=ALU.max), reads=DIFF.r(), writes=PPOS.r())
    P.op("dve", lambda e: e.tensor_scalar(out=PNEG.ap, in0=DIFF.ap, scalar1=-1.0, scalar2=0.0,
                                          op0=ALU.mult, op1=ALU.max), reads=DIFF.r(), writes=PNEG.r())
    P.dma(LGF.ap, lgf.partition_broadcast(128), writes=LGF.r())
    P.dma(LGB.ap, lgb.partition_broadcast(128), writes=LGB.r())
    P.dma(LG.ap[0:64, :], lgf.partition_broadcast(64), writes=LG.r())
    P.dma(LG.ap[64:128, :], lgb.partition_broadcast(64), writes=LG.r())
    P.dma(G1.ap, g_post_mix.partition_broadcast(128), writes=G1.r())
    P.dma(G2.ap, g_post_mlp.partition_broadcast(128), writes=G2.r())
    P.op("dve", lambda e: e.tensor_scalar(out=AQ.ap[0:64, :], in0=IFR.ap[0:64, :], scalar1=1.0,
                                          scalar2=None, op0=ALU.add), reads=IFR.r(), writes=AQ.r())
    P.op("dve", lambda e: e.tensor_scalar(out=AQ.ap[64:128, :], in0=IFR.ap[64:128, :], scalar1=-1.0,
                                          scalar2=128.0, op0=ALU.mult, op1=ALU.add),
         reads=IFR.r(), writes=AQ.r())
    DTv = DT_.v("p (h i) -> p h i", h=8)
    QDv = QDEC.v("p (h i) -> p h i", h=8)
    for h in range(8):
        P.op("act", lambda e, h=h: e.activation(out=TMPA.ap[:, 0:128], in_=PPOS.ap, func=AF.Exp,
                                                scale=LGF.ap[:, h:h + 1]),
             reads=PPOS.r() + LGF.r(), writes=TMPA.r())
        P.op("act", lambda e, h=h: e.activation(out=TMPB.ap[:, 0:128], in_=PNEG.ap, func=AF.Exp,
                                                scale=LGB.ap[:, h:h + 1]),
             reads=PNEG.r() + LGB.r(), writes=TMPB.r())
        P.op("dve", lambda e: e.tensor_tensor(out=TMPA.ap[:, 0:128], in0=TMPA.ap[:, 0:128],
                                              in1=MGE.ap, op=ALU.mult),
             reads=TMPA.r() + MGE.r(), writes=TMPA.r())
        P.op("dve", lambda e: e.tensor_tensor(out=TMPB.ap[:, 0:128], in0=TMPB.ap[:, 0:128],
                                              in1=MLT.ap, op=ALU.mult),
             reads=TMPB.r() + MLT.r(), writes=TMPB.r())
        P.op("dve", lambda e: e.tensor_tensor(out=TMPA.ap[:, 0:128], in0=TMPA.ap[:, 0:128],
                                              in1=TMPB.ap[:, 0:128], op=ALU.add),
             reads=TMPA.r() + TMPB.r(), writes=TMPA.r())
        P.op("dve", lambda e, h=h: e.tensor_scalar(out=DTv[:, h, :], in0=TMPA.ap[:, 0:128],
                                                   scalar1=0.125, scalar2=None, op0=ALU.mult),
             reads=TMPA.r(), writes=DT_.r())
        P.op("act", lambda e, h=h: e.activation(out=QDv[:, h, :], in_=AQ.ap, func=AF.Exp,
                                                scale=LG.ap[:, h:h + 1]),
             reads=AQ.r() + LG.r(), writes=QDEC.r())
    KDv = KDEC.v("p (a h) -> p a h", a=2)
    P.op("dve", lambda e: e.tensor_scalar(out=TMPA.ap[:, 0:1], in0=JP.ap[:, 0:1], scalar1=-1.0,
                                          scalar2=127.0, op0=ALU.mult, op1=ALU.add),
         reads=JP.r() + TMPA.r(), writes=TMPA.r())
    P.op("act", lambda e: e.activation(out=KDv[:, 0, :], in_=LGF.ap, func=AF.Exp,
                                       scale=TMPA.ap[:, 0:1]),
         reads=TMPA.r() + LGF.r(), writes=KDEC.r())
    P.op("act", lambda e: e.activation(out=KDv[:, 1, :], in_=LGB.ap, func=AF.Exp,
                                       scale=JP.ap[:, 0:1]),
         reads=JP.r() + LGB.r(), writes=KDEC.r())
    P.op("dve", lambda e: e.tensor_scalar(out=KDEC.ap, in0=KDEC.ap, scalar1=0.125, scalar2=None,
                                          op0=ALU.mult), reads=KDEC.r(), writes=KDEC.r())
    P.op("act", lambda e: e.activation(out=DEC.ap, in_=LG.ap, func=AF.Exp, scale=128.0),
         reads=LG.r(), writes=DEC.r())

    chunks = []

    def build_pack(name, src, cs, kc_n, segs, gain):
        NC = sum(s_[1] for s_ in segs)
        dst, dres = pack_tensor(name, cs, kc_n * NC), packs[name][1]
        gt = None
        if gain is not None:
            gt = mk("gt_" + name, 128, F32, 128)
            P.dma(gt.ap[0:cs, 0:kc_n], gain.rearrange("(c p) -> p c", p=cs), writes=gt.r(),
                  allow_slow_non_contiguous=True)
        for kc in range(kc_n):
            off = 0
            for (c0, n, sign) in segs:
                a = 0
                while a < n:
                    m = min(1024, n - a)
                    chunks.append((src[kc * cs:(kc + 1) * cs, c0 + a:c0 + a + m], cs, m, gt, kc, sign,
                                   dst[:, kc * NC + off + a: kc * NC + off + a + m], dres))
                    a += m
                off += n
        return dst

    def emit_chunks():
        n_ = len(chunks)
        LAH = 3
        for k in range(n_ + LAH):
            if k < n_:
                src_, cs, m, gt, kc, sign, dst_, dres = chunks[k]
                sg = STG[k % NSTG]
                P.dma(sg.ap[0:cs, 0:m], src_, writes=sg.r())
            if k >= LAH:
                kk = k - LAH
                src_, cs, m, gt, kc, sign, dst_, dres = chunks[kk]
                sg, sb = STG[kk % NSTG], STB[kk % NSTG]
                use_act = (kk % 2 == 0) and sign > 0
                if gt is not None:
                    if use_act:
                        P.op("act", lambda e: e.activation(out=sb.ap[0:cs, 0:m], in_=sg.ap[0:cs, 0:m], func=AF.Copy,
                                                           scale=gt.ap[0:cs, kc:kc + 1]),
                             reads=sg.r() + gt.r(), writes=sb.r())
                    else:
                        P.op("dve", lambda e: e.tensor_scalar(
                            out=sb.ap[0:cs, 0:m], in0=sg.ap[0:cs, 0:m], scalar1=gt.ap[0:cs, kc:kc + 1],
                            scalar2=float(sign), op0=ALU.mult, op1=ALU.mult),
                            reads=sg.r() + gt.r(), writes=sb.r())
                else:
                    if use_act:
                        P.op("act", lambda e: e.copy(out=sb.ap[0:cs, 0:m], in_=sg.ap[0:cs, 0:m]),
                             reads=sg.r(), writes=sb.r())
                    else:
                        P.op("dve", lambda e: e.tensor_scalar(
                            out=sb.ap[0:cs, 0:m], in0=sg.ap[0:cs, 0:m], scalar1=float(sign),
                            scalar2=None, op0=ALU.mult),
                            reads=sg.r(), writes=sb.r())
                P.dma(dst_, sb.ap[0:cs, 0:m], reads=sb.r(), writes=[dres])

    OUTB = [AR(40960 + i * 6144, 6144, BF16) for i in range(2)]
    cjobs = []

    def custom_pack(name, src, kc_n, NC, loads, opsfn, gain):
        dst, dres = pack_tensor(name, 128, kc_n * NC), packs[name][1]
        gt = mk("gt_" + name, 128, F32, 128)
        P.dma(gt.ap[:, 0:kc_n], gain.rearrange("(c p) -> p c", p=128), writes=gt.r(),
              allow_slow_non_contiguous=True)
        for kc in range(kc_n):
            cjobs.append((src, kc, NC, loads, opsfn, gt, dst, dres))

    def emit_custom():
        k_stg = [0]
        for ji, (src, kc, NC, loads, opsfn, gt, dst, dres) in enumerate(cjobs):
            stgs = []
            for (c0, n) in loads:
                sg = STG[k_stg[0] % NSTG]
                k_stg[0] += 1
                P.dma(sg.ap[:, 0:n], src[kc * 128:(kc + 1) * 128, c0:c0 + n], writes=sg.r())
                stgs.append(sg)
            ob = OUTB[ji % 2]
            oap = ob.ap[:, 0:NC]
            rd = []
            for sg in stgs:
                rd += sg.r()
            for oi, (iap, oap_, sign) in enumerate(opsfn([sg.ap for sg in stgs], oap)):
                if sign > 0 and oi % 2 == 0:
                    P.op("act", lambda e: e.activation(out=oap_, in_=iap, func=AF.Copy, scale=gt.ap[:, kc:kc + 1]),
                         reads=rd + gt.r(), writes=ob.r())
                else:
                    P.op("dve", lambda e: e.tensor_scalar(out=oap_, in0=iap, scalar1=gt.ap[:, kc:kc + 1],
                                                          scalar2=float(sign), op0=ALU.mult, op1=ALU.mult),
                         reads=rd + gt.r(), writes=ob.r())
            P.dma(dst[:, kc * NC:(kc + 1) * NC], oap, reads=ob.r(), writes=[dres])

    def rot_segs(base, nheads, hd):
        half = hd // 2
        s = []
        for h in range(nheads):
            s.append((base + h * hd + half, half, -1))
            s.append((base + h * hd, half, 1))
        return s

    def ops_p1a(st, o):
        a = st[0]
        return [(a[:, 256:416], o[:, 0:160], 1), (a[:, 400:416], o[:, 160:176], -1),
                (a[:, 384:400], o[:, 176:192], 1), (a[:, 0:256], o[:, 192:448], 1)]
    custom_pack("p1a", w_in, 8, 448, [(0, 416)], ops_p1a, g_pre_mix)
    pP1A = packs["p1a"][0]
    pP1T = build_pack("p1t", w_in, 128, 8, [(OFF["rk"], 512, 1), (OFF["rv"], 1024, 1)], g_pre_mix)
    def ops_p2f(st, o):
        a = st[0]
        res = []
        for i in range(2):
            src3 = a[:, i * 512:(i + 1) * 512].rearrange("p (h c) -> p h c", h=8)
            rot4 = o[:, i * 1024 + 512:(i + 1) * 1024].rearrange("p (h a c) -> p h a c", h=8, a=2)
            res.append((a[:, i * 512:(i + 1) * 512], o[:, i * 1024:i * 1024 + 512], 1))
            res.append((src3[:, :, 32:64], rot4[:, :, 0, :], -1))
            res.append((src3[:, :, 0:32], rot4[:, :, 1, :], 1))
        return res
    custom_pack("p2f", w_in, 8, 2048, [(OFF["rq"], 1024)], ops_p2f, g_pre_mix)
    pP2F = packs["p2f"][0]
    pP2T = build_pack("p2t", w_in, 128, 8, [(OFF["rv"], 1024, 1), (OFF["rg"], 1024, 1)], g_pre_mix)
    def ops_pg(st, o):
        o4 = o.rearrange("p (c g n) -> p c g n", c=8, g=2)
        return [(st[0].rearrange("p (c n) -> p c n", c=8), o4[:, :, 0, :], 1),
                (st[1].rearrange("p (c n) -> p c n", c=8), o4[:, :, 1, :], 1)]
    custom_pack("pg", w_in, 8, 2048, [(OFF["ga"], 1024), (OFF["gb"], 1024)], ops_pg, g_pre_mix)
    pPG = packs["pg"][0]
    pPB = build_pack("pb", w_b, 128, 8, [(0, 1024, 1)], None)
    pPA = build_pack("pa", w_a, 64, 8, [(0, 1024, 1)], None)
    pPO = build_pack("po", w_out, 128, 8, [(0, 1024, 1)], None)
    pPU = build_pack("pu", w_up, 128, 8, [(0, 4096, 1)], g_pre_mlp)
    pPD = build_pack("pd", w_down, 128, 32, [(0, 1024, 1)], None)
    def ops_wq(st, o):
        a3 = st[0][:, 0:768].rearrange("p (h c) -> p h c", h=8)
        o3 = o.rearrange("p (h c) -> p h c", h=8)
        return [(a3, o3[:, :, 0:96], 1), (a3[:, :, 80:96], o3[:, :, 96:112], -1), (a3[:, :, 64:80], o3[:, :, 112:128], 1)]
    custom_pack("wq", w_q_up, 2, 1024, [(0, 768)], ops_wq, g_q_norm)
    pWQ = packs["wq"][0]
    pWKV = build_pack("wkv", w_kv_up, 128, 1, [(0, 1024, 1)], g_kv_norm)
    emit_custom()
    emit_chunks()
    P.dma(WQ.ap, pWQ, reads=[packs["wq"][1]], writes=WQ.r())
    P.dma(WKV.ap, pWKV, reads=[packs["wkv"][1]], writes=WKV.r())

    class WStream:
        def __init__(self, seq):
            self.seq = seq
            self.issued = 0
            self.free = [0, 1, 2, 3]
            self.slot_of = {}
            self.pos = 0

        def _issue(self):
            while self.free and self.issued < len(self.seq):
                s = self.free.pop(0)
                name, parts = self.seq[self.issued]
                for (dst_fn, src, res) in parts:
                    P.dma(dst_fn(RING[s]), src, reads=[res], writes=RING[s].r())
                self.slot_of[self.issued] = s
                self.issued += 1

        def get(self, name):
            i = self.pos
            assert self.seq[i][0] == name, (self.seq[i][0], name)
            if i not in self.slot_of:
                self._issue()
            assert i in self.slot_of, "ring exhausted at %s" % name
            self.pos += 1
            return i, RING[self.slot_of[i]]

        def release(self, i):
            self.free.append(self.slot_of.pop(i))
            self._issue()

    def wt3(pack, kc_n, NC, c0, n):
        src = packs[pack][0].rearrange("p (k c) -> p k c", k=kc_n)[:, :, c0:c0 + n]
        return (lambda rb: rb.ap[:, 0:kc_n * n].rearrange("p (k c) -> p k c", k=kc_n), src, packs[pack][1])

    seq = []
    for j in range(NJ):
        seq.append(("p1a", [wt3("p1a", 8, 448, 0, 448)]))
        seq.append(("p1k", [wt3("p1t", 8, 1536, 0, 512)]))
        seq.append(("p1v0", [wt3("p1t", 8, 1536, 512, 512)]))
        seq.append(("p1v1", [wt3("p1t", 8, 1536, 1024, 512)]))
        for b in range(NB if cfg.get("stop", 9) >= 4 else 0):
            for i, nm in enumerate(("rqa", "rqb", "rka", "rkb")):
                seq.append((nm, [wt3("p2f", 8, 2048, i * 512, 512)]))
            for i, nm in enumerate(("rv0", "rv1", "rg0", "rg1")):
                seq.append((nm, [wt3("p2t", 8, 2048, i * 512, 512)]))
            for c in range(8):
                srcG = pPG.rearrange("p (k c) -> p k c", k=8)[:, :, c * 256:(c + 1) * 256]
                srcB = pPB.rearrange("p (k c) -> p k c", k=8)[:, :, c * 128:(c + 1) * 128]
                srcA = pPA.rearrange("p (k c) -> p k c", k=8)[:, :, c * 128:(c + 1) * 128]
                seq.append(("mix%d" % c, [
                    (lambda rb: rb.ap[:, 0:2048].rearrange("p (k c) -> p k c", k=8), srcG, packs["pg"][1]),
                    (lambda rb: rb.ap[:, 2048:3072].rearrange("p (k c) -> p k c", k=8), srcB, packs["pb"][1]),
                    (lambda rb: rb.ap[0:64, 3072:4096].rearrange("p (k c) -> p k c", k=8), srcA, packs["pa"][1]),
                ]))
            for i in range(2):
                seq.append(("wo%d" % i, [wt3("po", 8, 1024, i * 512, 512)]))
            for i in range(8):
                seq.append(("wu%d" % i, [wt3("pu", 8, 4096, i * 512, 512)]))
            for half in range(2):
                for g in range(4):
                    src = pPD.rearrange("p (k c) -> p k c", k=32)[:, g * 8:(g + 1) * 8, half * 512:(half + 1) * 512]
                    seq.append(("wd%d_%d" % (half, g), [
                        (lambda rb: rb.ap[:, 0:4096].rearrange("p (k c) -> p k c", k=8), src, packs["pd"][1])]))
    WS = WStream(seq)

    KB = 1024
    NKEYMAX = MAXKT * 128
    o = 0
    LAT = AR(o, NKEYMAX * 2, BF16); o += NKEYMAX * 2
    KT = AR(o, NKEYMAX * 2, BF16); o += NKEYMAX * 2
    CQT = AR(o, 2 * S_OWN * 2, BF16); o += 2 * S_OWN * 2
    OV = o
    o = OV
    VH_SZ = ((MAXKT * 65 * 2 + 511) // 512) * 512
    VH = AR(o, VH_SZ, BF16); o += VH_SZ
    QH = AR(o, S_OWN * 2, BF16); o += S_OWN * 2
    TBM = [AR(o + i * 4096, 4096, F32) for i in range(2)]; o += 8192
    PT = [AR(o + i * 1024, 1024, BF16) for i in range(3)]; o += 3072
    T1 = AR(o, 2048, F32); o += 2048
    T2 = AR(o, 2048, F32); o += 2048
    BCS = AR(o, 2048, F32); o += 2048
    RS = AR(o, 2048, F32); o += 2048
    PHA_END = o
    assert PHA_END <= ARENA, PHA_END
    o = OV
    XT = [AR(o + i * 4096, 4096, F32) for i in range(2)]; o += 8192
    XNd = [AR(o + i * 2048, 2048, BF16) for i in range(2)]; o += 4096
    XNT1d = [AR(o + i * 2048, 2048, BF16) for i in range(2)]; o += 4096
    SQ = AR(o, 1024, BF16); o += 1024
    RBC = AR(o, 1024, F32); o += 1024
    TTT = [AR(o + i * 512, 512, F32) for i in range(2)]; o += 1024
    TMT = [AR(o + i * 1024, 1024, F32) for i in range(2)]; o += 2048
    K1 = AR(o, 512, F32); o += 512
    K2 = AR(o, 512, F32); o += 512
    RA = AR(o, 1024, F32); o += 1024
    RB_ = AR(o, 1024, F32); o += 1024
    RKR = AR(o, 2048, F32); o += 2048
    KS = AR(o, 2048, BF16); o += 2048
    RV1d = [AR(o + i * 2048, 2048, BF16) for i in range(2)]; o += 4096
    RKC = [AR(o + i * 1024, 1024, BF16) for i in range(2)]; o += 2048
    TKV = AR(o, 4096, F32); o += 4096
    assert o <= ARENA, o
    o = 0
    X2 = [AR(o + i * 4096, 4096, F32) for i in range(2)]; o += 8192
    XN2 = AR(o, 2048, BF16); o += 2048
    ST2 = AR(o, 512, F32); o += 512
    XNT = AR(o, 8192, BF16); o += 8192
    RQT = AR(o, 8192, BF16); o += 8192
    RKT = AR(o, 8192, BF16); o += 8192
    UTA = LB(ar_t, ar_reg, RQT.off, 16384, BF16)
    RVB = AR(o, 8192, BF16); o += 8192
    SRG = AR(o, 8192, BF16); o += 8192
    X1 = LB(ar_t, ar_reg, RVB.off, 16384, F32)
    TA = AR(o, 4096, F32); o += 4096
    UTB_OFF = o
    TB = AR(o, 4096, F32); o += 4096
    SMB = [AR(o + i * 1024, 1024, BF16) for i in range(2)]; o += 2048
    QP = [AR(o + i * 1024, 1024, BF16) for i in range(2)]; o += 2048
    OB = XN2
    OBT = AR(o, 8192, BF16); o += 8192
    MT = AR(o, 8192, BF16); o += 8192
    YF0 = LB(ar_t, ar_reg, MT.off, 8192, F32)
    TF = X2[0]
    JUNK2 = AR(o, 1024, BF16); o += 1024
    assert o <= ARENA, o
    UTB = AR(UTB_OFF, 16384, BF16)
    assert UTB_OFF + 16384 <= MT.off
    RL = [LB(ar_t, ar_reg, X2[0].off + i * 1024, 1024, BF16) for i in range(4)]
    OUTT = X2[1]

    def norm_x_tile(xt, xn, stat_col, src_reads):
        ssap = STAT.ap[:, stat_col:stat_col + 1]
        sr = STAT.r(stat_col * 4, stat_col * 4 + 4) if False else STAT.r()
        P.op("act", lambda e: e.activation(out=xn.ap, in_=xt.ap, func=AF.Square, accum_out=ssap),
             reads=xt.r(), writes=xn.r() + STAT.r())
        act_rstd(ssap, ssap, D, STAT.r(), STAT.r())
        P.op("dve", lambda e: e.tensor_scalar(out=xn.ap, in0=xt.ap, scalar1=ssap, scalar2=None,
                                              op0=ALU.mult),
             reads=xt.r() + STAT.r(), writes=xn.r())


    def transpose_to(xn, dst_ap3, dst_reads_writes, bank, evac="dve"):
        pb = PSbf(bank)

        def tr(e):
            for kc in range(8):
                ins = e.transpose(out=pb[:, kc * 128:(kc + 1) * 128], in_=xn.ap[:, kc * 128:(kc + 1) * 128],
                                  identity=IDENT.ap)
            return ins
        P.op("pe", tr, reads=xn.r() + IDENT.r(), writes=PSR(bank))
        if evac == "act":
            P.op("act", lambda e: e.copy(out=dst_ap3, in_=pb.rearrange("p (k c) -> p k c", k=8)),
                 reads=PSR(bank), writes=dst_reads_writes)
        else:
            P.op("dve", lambda e: e.tensor_copy(out=dst_ap3, in_=pb.rearrange("p (k c) -> p k c", k=8)),
                 reads=PSR(bank), writes=dst_reads_writes)

    key_off = 0
    ctx_off = 0
    LATv = LAT.ap
    KTv = KT.ap
    CQTv = CQT.v("p (k s) -> p k s", k=2)
    STv = ST.v("p (n h v) -> p n h v", n=NT, h=8)
    OTv = OT.v("p (h s) -> p h s", h=8)
    ACCv = ACC.v("p (h v) -> p h v", h=8)
    COEFv = COEF.v("p (m h) -> p m h", h=8)
    DECb = DEC.ap.unsqueeze(2).broadcast_to([128, 8, 128])

    STOP = cfg.get("stop", 9)
    for j in range(NJ):
        if STOP == 0:
            break
        nctx = NCTX[j]
        nkt = NKT[j]
        NKEY = nkt * 128
        own0 = j * S_OWN
        P.op("pool", lambda e: e.memset(ACC.ap, 0.0), writes=ACC.r())
        if nctx > 0:
            P.dma(CDT.ap[:, 0:nctx], cD[:, ctx_off:ctx_off + nctx], writes=CDT.r())
            P.dma(CFT.ap[:, 0:nctx], cF[:, ctx_off:ctx_off + nctx], writes=CFT.r())
            P.op("dve", lambda e, nctx=nctx: e.tensor_tensor(
                out=COEFv[:, 0:nctx, :], in0=CDT.ap[:, 0:nctx].unsqueeze(2).broadcast_to([128, nctx, 8]),
                in1=LG.ap.unsqueeze(1).broadcast_to([128, nctx, 8]), op=ALU.mult),
                reads=CDT.r() + LG.r(), writes=COEF.r())
            P.op("act", lambda e, nctx=nctx: e.activation(out=COEFv[:, 0:nctx, :], in_=COEFv[:, 0:nctx, :],
                                                          func=AF.Exp),
                 reads=COEF.r(), writes=COEF.r())
            P.op("dve", lambda e, nctx=nctx: e.tensor_tensor(
                out=COEFv[:, 0:nctx, :], in0=COEFv[:, 0:nctx, :],
                in1=CFT.ap[:, 0:nctx].unsqueeze(2).broadcast_to([128, nctx, 8]), op=ALU.mult),
                reads=COEF.r() + CFT.r(), writes=COEF.r())

        iA, WA_ = WS.get("p1a")
        iK, WK_ = WS.get("p1k")
        iV0, WV0 = WS.get("p1v0")
        iV1, WV1 = WS.get("p1v1")
        wA = WA_.ap[:, 0:8 * 448].rearrange("p (k c) -> p k c", k=8)
        wK = WK_.ap.rearrange("p (k c) -> p k c", k=8)
        wV = [WV0.ap.rearrange("p (k c) -> p k c", k=8), WV1.ap.rearrange("p (k c) -> p k c", k=8)]

        def x_src(kt):
            if kt < NT:
                return xo[own0 + kt * 128: own0 + (kt + 1) * 128, :]
            m = ctx_off + (kt - NT)
            return xc[m * 128:(m + 1) * 128, :]

        def p1_xload(kt):
            P.dma(XT[kt % 2].ap, x_src(kt), writes=XT[kt % 2].r())

        def p1_stageA(kt):
            norm_x_tile(XT[kt % 2], XNd[kt % 2], kt % 2, None)
            transpose_to(XNd[kt % 2], XNT1d[kt % 2].v("p (k c) -> p k c", k=8), XNT1d[kt % 2].r(), 0)

        def p1_load(kt):
            tk = key_off + kt * 128
            P.dma(TTT[kt % 2].ap[:, 0:64], tabT[tk:tk + 128, :], writes=TTT[kt % 2].r())
            P.dma(TMT[kt % 2].ap[64:96, :].rearrange("p (a c) -> p a c", a=2), tabM[:, :, tk:tk + 128],
                  writes=TMT[kt % 2].r())

        def p1_B(kt, phase):
            own = kt < NT
            tmt = TMT[kt % 2].ap[64:96, :].rearrange("p (a c) -> p a c", a=2)
            kc0 = kt * 128
            XNT1 = XNT1d[kt % 2]
            XNT1v = XNT1.v("p (k c) -> p k c", k=8)
            def fm(e, own=own):
                groups = [(0, 128, PS(1)[:, 0:128]), (64, 96, PS(2)[0:96, 0:128]), (96, 96, PS(2)[0:96, 128:256])]
                if own:
                    groups += [(192, 128, PS(1)[:, 128:256]), (320, 128, PS(1)[:, 256:384])]
                for (c0, m, out) in groups:
                    for kc in range(8):
                        ins = e.matmul(out, lhsT=wA[:, kc, c0:c0 + m], rhs=XNT1v[:, kc, :],
                                       start=(kc == 0), stop=(kc == 7))
                return ins
            def tm(e):
                for (bank, w) in ((3, wK), (4, wV[0]), (5, wV[1])):
                    for kc in range(8):
                        ins = e.matmul(PS(bank)[:, :], lhsT=XNT1v[:, kc, :], rhs=w[:, kc, :],
                                       start=(kc == 0), stop=(kc == 7))
                return ins
            if phase == 0:
                P.op("pe", tm, reads=XNT1.r() + WK_.r() + WV0.r() + WV1.r(), writes=PSR(3) + PSR(4) + PSR(5))
                P.op("pe", fm, reads=XNT1.r() + WA_.r(), writes=PSR(1) + PSR(2))
                return
            rkc = RKC[kt % 2]
            rv1 = RV1d[kt % 2]
            P.op("act", lambda e: e.copy(out=rkc.ap, in_=PS(3)[:, :]), reads=PSR(3), writes=rkc.r())
            P.op("act", lambda e: e.copy(out=rv1.ap[:, 0:512], in_=PS(4)[:, :]), reads=PSR(4), writes=rv1.r())
            P.op("act", lambda e: e.copy(out=rv1.ap[:, 512:1024], in_=PS(5)[:, :]), reads=PSR(5), writes=rv1.r())
            nsq = 3 if own else 1
            SQv = SQ.ap[:, 0:384].rearrange("p (a c) -> p a c", a=3)
            P.op("act", lambda e, nsq=nsq: e.activation(out=SQv[:, 0:nsq, :],
                                                        in_=PS(1)[:, 0:nsq * 128].rearrange("p (a c) -> p a c", a=nsq),
                                                        func=AF.Square),
                 reads=PSR(1), writes=SQ.r())

            def ssmm(e, own=own):
                ins = e.matmul(PS(2)[:, 256:384], lhsT=ONESB.ap, rhs=SQv[:, 0, :], start=True, stop=True)
                if own:
                    e.matmul(PS(2)[:, 384:512], lhsT=ONESB.ap, rhs=SQv[:, 1, :], start=True, stop=False)
                    ins = e.matmul(PS(2)[:, 384:512], lhsT=ONESB.ap, rhs=SQv[:, 2, :], start=False, stop=True)
                return ins
            P.op("pe", ssmm, reads=SQ.r() + ONESB.r(), writes=PSR(2))
            RBCv = RBC.v("p (a c) -> p a c", a=2)
            P.op("act", lambda e: e.activation(out=RBCv[:, 0, :], in_=PS(2)[:, 256:384], func=AF.Ln,
                                               scale=1.0 / 128, bias=EPST.ap[:, 0:1]),
                 reads=PSR(2) + EPST.r(), writes=RBC.r())
            if own:
                P.op("act", lambda e: e.activation(out=RBCv[:, 1, :], in_=PS(2)[:, 384:512], func=AF.Ln,
                                                   scale=1.0 / 256, bias=EPST.ap[:, 0:1]),
                     reads=PSR(2) + EPST.r(), writes=RBC.r())
            na = 2 if own else 1
            P.op("act", lambda e, na=na: e.activation(out=RBCv[:, 0:na, :], in_=RBCv[:, 0:na, :],
                                                      func=AF.Exp, scale=-0.5),
                 reads=RBC.r(), writes=RBC.r())
            P.op("dve", lambda e, kc0=kc0: e.tensor_tensor(out=LATv[:, kc0:kc0 + 128], in0=PS(1)[:, 0:128],
                                                           in1=RBCv[:, 0, :], op=ALU.mult),
                 reads=PSR(1) + RBC.r(), writes=LAT.r(kc0, kc0 + 128))
            if own:
                P.op("dve", lambda e, kc0=kc0: e.tensor_tensor(
                    out=CQTv[:, :, kc0:kc0 + 128], in0=PS(1)[:, 128:384].rearrange("p (a c) -> p a c", a=2),
                    in1=RBCv[:, 1, :].unsqueeze(1).broadcast_to([128, 2, 128]), op=ALU.mult),
                    reads=PSR(1) + RBC.r(), writes=CQT.r(kc0, kc0 + 128) + CQT.r(S_OWN + kc0, S_OWN + kc0 + 128))
            P.op("dve", lambda e, tmt=tmt: e.tensor_tensor(out=K1.ap[64:96, :], in0=PS(2)[64:96, 0:128],
                                                           in1=tmt[:, 0, :], op=ALU.mult),
                 reads=PSR(2) + TMT[kt % 2].r(), writes=K1.r())
            P.op("dve", lambda e, tmt=tmt: e.tensor_tensor(out=K2.ap[64:96, :], in0=PS(2)[64:96, 128:256],
                                                           in1=tmt[:, 1, :], op=ALU.mult),
                 reads=PSR(2) + TMT[kt % 2].r(), writes=K2.r())
            P.op("dve", lambda e, kc0=kc0: e.tensor_tensor(out=KTv[64:96, kc0:kc0 + 128], in0=K1.ap[64:96, :],
                                                            in1=K2.ap[64:96, :], op=ALU.add),
                 reads=K1.r() + K2.r(), writes=KT.r(kc0, kc0 + 128, half=1))

        def p1_C(kt):
            own = kt < NT
            ttt = TTT[kt % 2]
            rkc = RKC[kt % 2]
            rv1 = RV1d[kt % 2]
            rk4 = rkc.ap.rearrange("p (h a c) -> p h a c", h=8, a=2)
            cosb = ttt.ap[:, 0:32].unsqueeze(1).broadcast_to([128, 8, 32])
            sinb = ttt.ap[:, 32:64].unsqueeze(1).broadcast_to([128, 8, 32])
            RAv = RA.v("p (h c) -> p h c", h=8)
            RBv = RB_.v("p (h c) -> p h c", h=8)
            RKRv = RKR.v("p (h a c) -> p h a c", h=8, a=2)
            P.op("dve", lambda e: e.tensor_tensor(out=RAv, in0=rk4[:, :, 0, :], in1=cosb, op=ALU.mult),
                 reads=rkc.r() + ttt.r(), writes=RA.r())
            P.op("dve", lambda e: e.tensor_tensor(out=RBv, in0=rk4[:, :, 1, :], in1=sinb, op=ALU.mult),
                 reads=rkc.r() + ttt.r(), writes=RB_.r())
            P.op("dve", lambda e: e.tensor_tensor(out=RKRv[:, :, 0, :], in0=RAv, in1=RBv, op=ALU.subtract),
                 reads=RA.r() + RB_.r(), writes=RKR.r())
            P.op("dve", lambda e: e.tensor_tensor(out=RAv, in0=rk4[:, :, 0, :], in1=sinb, op=ALU.mult),
                 reads=rkc.r() + ttt.r() + RA.r(), writes=RA.r())
            P.op("dve", lambda e: e.tensor_tensor(out=RBv, in0=rk4[:, :, 1, :], in1=cosb, op=ALU.mult),
                 reads=rkc.r() + ttt.r() + RB_.r(), writes=RB_.r())
            P.op("dve", lambda e: e.tensor_tensor(out=RKRv[:, :, 1, :], in0=RAv, in1=RBv, op=ALU.add),
                 reads=RA.r() + RB_.r() + RKR.r(), writes=RKR.r())
            KSv = KS.v("p (h c) -> p h c", h=8)
            RKR3 = RKR.v("p (h c) -> p h c", h=8)
            for a in range(2):
                P.op("dve", lambda e, a=a: e.tensor_tensor(
                    out=KSv[:, :, a * 64:(a + 1) * 64], in0=RKR3,
                    in1=KDv[:, a, :].unsqueeze(2).broadcast_to([128, 8, 64]), op=ALU.mult),
                    reads=RKR.r() + KDEC.r(), writes=KS.r())

            def kvmm(e):
                for h in range(8):
                    ins = e.matmul(PS(6 + h // 4)[:, (h % 4) * 128:(h % 4 + 1) * 128], lhsT=KSv[:, h, :],
                                   rhs=rv1.ap[:, h * 128:(h + 1) * 128], start=True, stop=True)
                return ins
            P.op("pe", kvmm, reads=KS.r() + rv1.r(), writes=PSR(6) + PSR(7))
            if own:
                for hh in range(2):
                    P.op("dve", lambda e, hh=hh, kt=kt: e.tensor_copy(
                        out=STv[:, kt, hh * 4:(hh + 1) * 4, :],
                        in_=PS(6 + hh)[:, :].rearrange("p (h v) -> p h v", h=4)),
                        reads=PSR(6 + hh), writes=ST.r(kt * 1024, (kt + 1) * 1024))
            else:
                m = kt - NT
                TKVv = TKV.v("p (h v) -> p h v", h=8)
                for hh in range(2):
                    P.op("dve", lambda e, hh=hh, m=m: e.tensor_tensor(
                        out=TKVv[:, hh * 4:(hh + 1) * 4, :], in0=PS(6 + hh)[:, :].rearrange("p (h v) -> p h v", h=4),
                        in1=COEFv[:, m, hh * 4:(hh + 1) * 4].unsqueeze(2).broadcast_to([128, 4, 128]), op=ALU.mult),
                        reads=PSR(6 + hh) + COEF.r(), writes=TKV.r())
                P.op("dve", lambda e: e.tensor_tensor(out=ACC.ap, in0=ACC.ap, in1=TKV.ap, op=ALU.add),
                     reads=TKV.r() + ACC.r(), writes=ACC.r())

        p1_xload(0)
        if nkt > 1:
            p1_xload(1)
        p1_stageA(0)
        if nkt > 2:
            p1_xload(2)
        if nkt > 1:
            p1_stageA(1)
        p1_load(0)
        p1_B(0, 0)
        p1_B(0, 1)
        for kt in range(nkt):
            if kt + 3 < nkt:
                p1_xload(kt + 3)
            if kt + 2 < nkt:
                p1_stageA(kt + 2)
            if kt + 1 < nkt:
                p1_load(kt + 1)
                p1_B(kt + 1, 0)
            p1_C(kt)
            if kt + 1 < nkt:
                p1_B(kt + 1, 1)
        WS.release(iA); WS.release(iK); WS.release(iV0); WS.release(iV1)

        if STOP == 1:
            key_off += nkt * 128
            ctx_off += nctx
            continue
        TKVv = TKV.v("p (h v) -> p h v", h=8)
        accs = [(ACC, ACCv), (TKV, TKVv)]
        for i_ in range(NT):
            ca, cav = accs[i_ % 2]
            cb, cbv = accs[(i_ + 1) % 2]
            for (half, n) in ((0, i_), (1, NT - 1 - i_)):
                p0, p1 = half * 64, half * 64 + 64
                rs = ST.r(n * 1024, (n + 1) * 1024, half=half)
                P.op("dve", lambda e, p0=p0, p1=p1, cav=cav, cbv=cbv: e.tensor_tensor(
                    out=cbv[p0:p1], in0=cav[p0:p1], in1=DECb[p0:p1], op=ALU.mult),
                    reads=ca.r(half=half) + DEC.r(), writes=cb.r(half=half))
                P.op("dve", lambda e, p0=p0, p1=p1, n=n, cbv=cbv: e.tensor_tensor(
                    out=cbv[p0:p1], in0=cbv[p0:p1], in1=STv[p0:p1, n], op=ALU.add),
                    reads=cb.r(half=half) + rs, writes=cb.r(half=half))
                P.op("act", lambda e, p0=p0, p1=p1, n=n, ca=ca: e.copy(
                    out=ST.ap[p0:p1, n * 1024:(n + 1) * 1024], in_=ca.ap[p0:p1, :]),
                    reads=ca.r(half=half), writes=rs)

        if STOP == 2:
            key_off += nkt * 128
            ctx_off += nctx
            continue
        VHv = VH.ap[:, 0:MAXKT * 65].rearrange("p (c d) -> p c d", d=65)
        P.op("pool", lambda e: e.memset(VH.ap, 1.0), writes=VH.r())
        WQv = WQ.v("p (k h c) -> p k h c", k=2, h=8)
        WKVv = WKV.v("p (h c) -> p h c", h=8)
        nkb = (NKEY + 511) // 512
        for h in range(8):
            for kb in range(nkb):
                c0 = kb * 512
                n = min(512, NKEY - c0)
                bank = kb % 2
                P.op("pe", lambda e, c0=c0, n=n, bank=bank, h=h: e.matmul(
                    PS(bank)[0:64, 0:n], lhsT=WKVv[:, h, 0:64], rhs=LATv[:, c0:c0 + n], start=True, stop=True),
                    reads=WKV.r() + LAT.r(c0, c0 + n), writes=PSR(bank))
                if kb % 2:
                    P.op("act", lambda e, c0=c0, n=n, bank=bank: e.copy(out=KTv[0:64, c0:c0 + n],
                                                                        in_=PS(bank)[0:64, 0:n]),
                         reads=PSR(bank), writes=KT.r(c0, c0 + n, half=0))
                else:
                    P.op("dve", lambda e, c0=c0, n=n, bank=bank: e.tensor_copy(out=KTv[0:64, c0:c0 + n],
                                                                               in_=PS(bank)[0:64, 0:n]),
                         reads=PSR(bank), writes=KT.r(c0, c0 + n, half=0))
            for g in range((nkt + 7) // 8):
                cc = list(range(g * 8, min(nkt, g * 8 + 8)))
                bank = 2 + g % 2

                def vmm(e, cc=cc, bank=bank, h=h):
                    for i, c in enumerate(cc):
                        ins = e.matmul(PS(bank)[:, i * 64:(i + 1) * 64], lhsT=LATv[:, c * 128:(c + 1) * 128],
                                       rhs=WKVv[:, h, 64:128], start=True, stop=True)
                    return ins
                P.op("pe", vmm, reads=WKV.r() + LAT.r(cc[0] * 128, (cc[-1] + 1) * 128), writes=PSR(bank))
                ncc = len(cc)
                P.op("dve" if g % 2 else "act",
                     (lambda e, cc=cc, bank=bank, ncc=ncc: e.tensor_copy(
                         out=VHv[:, cc[0]:cc[0] + ncc, 0:64],
                         in_=PS(bank)[:, 0:ncc * 64].rearrange("p (c d) -> p c d", d=64))) if g % 2 else
                     (lambda e, cc=cc, bank=bank, ncc=ncc: e.copy(
                         out=VHv[:, cc[0]:cc[0] + ncc, 0:64],
                         in_=PS(bank)[:, 0:ncc * 64].rearrange("p (c d) -> p c d", d=64))),
                     reads=PSR(bank), writes=VH.r())
            for b in range(NB):
                s0 = b * 512
                tb = TBM[b % 2]
                tbv = tb.ap[64:96, :].rearrange("p (a c) -> p a c", a=2)
                P.dma(tbv, tabM[:, :, key_off + s0: key_off + s0 + 512], writes=tb.r())

                def qmm(e, s0=s0, h=h):
                    for kc in range(2):
                        e.matmul(PS(4)[0:96, :], lhsT=WQv[:, kc, h, 0:96], rhs=CQTv[:, kc, s0:s0 + 512],
                                 start=(kc == 0), stop=(kc == 1))
                    for kc in range(2):
                        ins = e.matmul(PS(5)[0:96, :], lhsT=WQv[:, kc, h, 32:128], rhs=CQTv[:, kc, s0:s0 + 512],
                                       start=(kc == 0), stop=(kc == 1))
                    return ins
                P.op("pe", qmm, reads=WQ.r() + CQT.r(), writes=PSR(4) + PSR(5))
                P.op("dve", lambda e, tbv=tbv: e.tensor_tensor(out=T1.ap[64:96, :], in0=PS(4)[64:96, :],
                                                               in1=tbv[:, 0, :], op=ALU.mult),
                     reads=PSR(4) + tb.r(), writes=T1.r())
                P.op("dve", lambda e, tbv=tbv: e.tensor_tensor(out=T2.ap[64:96, :], in0=PS(5)[64:96, :],
                                                               in1=tbv[:, 1, :], op=ALU.mult),
                     reads=PSR(5) + tb.r(), writes=T2.r())
                P.op("dve", lambda e, s0=s0: e.tensor_tensor(out=QH.ap[64:96, s0:s0 + 512], in0=T1.ap[64:96, :],
                                                              in1=T2.ap[64:96, :], op=ALU.add),
                     reads=T1.r() + T2.r(), writes=QH.r(s0, s0 + 512, half=1))
                P.op("act", lambda e, s0=s0: e.copy(out=QH.ap[0:64, s0:s0 + 512], in_=PS(4)[0:64, :]),
                     reads=PSR(4), writes=QH.r(s0, s0 + 512, half=0))
            LA = 2
            steps = [(b, c) for b in range(NB) for c in range(nkt)]
            deferred = []

            def score_exp(idx, h=h):
                b, c = steps[idx]
                s0 = b * 512
                sbank = idx % 3
                pt = PT[idx % 3]
                P.op("pe", lambda e: e.matmul(
                    PS(sbank)[:, :], lhsT=KTv[0:96, c * 128:(c + 1) * 128], rhs=QH.ap[0:96, s0:s0 + 512],
                    start=True, stop=True),
                    reads=KT.r(c * 128, (c + 1) * 128) + QH.r(s0, s0 + 512), writes=PSR(sbank))
                P.op("act", lambda e: e.activation(out=pt.ap, in_=PS(sbank)[:, :], func=AF.Exp, scale=ATT_SCALE),
                     reads=PSR(sbank), writes=pt.r())

            def pv(idx, h=h):
                b, c = steps[idx]
                pt = PT[idx % 3]
                obank = 6 + b % 2
                P.op("pe", lambda e: e.matmul(
                    PS(obank)[0:65, :], lhsT=VHv[:, c, 0:65], rhs=pt.ap, start=(c == 0), stop=(c == nkt - 1)),
                    reads=VH.r() + pt.r(), writes=PSR(obank))
                if c == nkt - 1:
                    deferred.append((idx + LA + 2, b))

            def epilogue(b, h=h):
                s0 = b * 512
                obank = 6 + b % 2
                P.op("dve", lambda e: e.reciprocal(out=RS.ap[64:65, :], in_=PS(obank)[64:65, :]),
                     reads=PSR(obank), writes=RS.r())
                P.op("pe", lambda e: e.matmul(PS(3)[0:64, :], lhsT=ONESF.ap[64:65, 0:64], rhs=RS.ap[64:65, :],
                                              start=True, stop=True),
                     reads=RS.r() + ONESF.r(), writes=PSR(3))
                P.op("dve", lambda e: e.tensor_copy(out=BCS.ap[0:64, :], in_=PS(3)[0:64, :]), reads=PSR(3),
                     writes=BCS.r())
                P.op("dve", lambda e: e.tensor_tensor(
                    out=OTv[0:64, h, s0:s0 + 512], in0=PS(obank)[0:64, :], in1=BCS.ap[0:64, :], op=ALU.mult),
                    reads=PSR(obank) + BCS.r(), writes=OT.r(h * S_OWN + s0, h * S_OWN + s0 + 512))

            for i_ in range(len(steps) + LA):
                if i_ < len(steps):
                    score_exp(i_)
                if i_ >= LA:
                    pv(i_ - LA)
                while deferred and deferred[0][0] <= i_:
                    epilogue(deferred.pop(0)[1])
            while deferred:
                epilogue(deferred.pop(0)[1])

        if STOP == 3:
            key_off += nkt * 128
            ctx_off += nctx
            continue
        XNTv = XNT.v("p (k s) -> p k s", k=8)
        RQTv = RQT.v("p (a s) -> p a s", a=8)
        RKTv = RKT.v("p (a s) -> p a s", a=8)
        RVBv = RVB.v("p (t c) -> p t c", t=4)
        SRGv = SRG.v("p (t c) -> p t c", t=4)
        X1v = X1.v("p (t c) -> p t c", t=4)
        OBTv = OBT.v("p (k s) -> p k s", k=8)
        MTv = MT.v("p (k s) -> p k s", k=8)
        TFv = TF.v("p (a s) -> p a s", a=2)
        UTAv = UTA.v("p (k s) -> p k s", k=16)
        UTBv = UTB.v("p (k s) -> p k s", k=16)

        def UTc(ch):
            return (UTAv if ch < 16 else UTBv)[:, ch % 16, :]

        def UTr(ch):
            return (UTA if ch < 16 else UTB).r((ch % 16) * 512, (ch % 16 + 1) * 512)
        YF0v = YF0.v("p (t c) -> p t c", t=4)
        def p2_step1(b):
            g0 = own0 + b * 512
            for t in range(4):
                x2 = X2[t % 2]
                P.dma(x2.ap, xo[g0 + t * 128: g0 + (t + 1) * 128, :], writes=x2.r())
                ssap, ssr = SC(4 * t)
                P.op("act", lambda e: e.activation(out=TA.ap, in_=x2.ap, func=AF.Square, accum_out=ssap),
                     reads=x2.r(), writes=TA.r() + ssr)
                act_rstd(ssap, ssap, D, ssr, ssr)
                P.op("dve", lambda e: e.tensor_scalar(out=XN2.ap, in0=x2.ap, scalar1=ssap, scalar2=None,
                                                      op0=ALU.mult),
                     reads=x2.r() + ssr, writes=XN2.r())
                transpose_to(XN2, XNTv[:, :, t * 128:(t + 1) * 128], XNT.r(), 0, evac="act")

        for b in range(NB):
            s0 = b * 512
            g0 = own0 + s0
            if b == 0:
                p2_step1(0)
            if STOP == 4:
                break
            P.dma(TFv, tabF[:, :, g0:g0 + 512], writes=TF.r())

            def proj_rope(WAx, WBx, dstv, dst, is_q):
                wa = WAx.ap.rearrange("p (k c) -> p k c", k=8)
                wb = WBx.ap.rearrange("p (k c) -> p k c", k=8)
                for p_ in range(4):
                    ba, bb = (p_ % 2) * 2, (p_ % 2) * 2 + 1

                    def mm(e):
                        for kc in range(8):
                            e.matmul(PS(ba)[:, :], lhsT=wa[:, kc, p_ * 128:(p_ + 1) * 128], rhs=XNTv[:, kc, :],
                                     start=(kc == 0), stop=(kc == 7))
                        for kc in range(8):
                            ins = e.matmul(PS(bb)[:, :], lhsT=wb[:, kc, p_ * 128:(p_ + 1) * 128], rhs=XNTv[:, kc, :],
                                           start=(kc == 0), stop=(kc == 7))
                        return ins
                    P.op("pe", mm, reads=XNT.r() + WAx.r() + WBx.r(), writes=PSR(ba) + PSR(bb))
                    P.op("dve", lambda e: e.tensor_tensor(out=TA.ap[:, 0:512], in0=PS(ba)[:, :],
                                                          in1=TFv[:, 0, :], op=ALU.mult),
                         reads=PSR(ba) + TF.r(), writes=TA.r(0, 512))
                    P.op("dve", lambda e: e.tensor_tensor(out=TB.ap[:, 0:512], in0=PS(bb)[:, :],
                                                          in1=TFv[:, 1, :], op=ALU.mult),
                         reads=PSR(bb) + TF.r(), writes=TB.r(0, 512))
                    he, ho = 2 * p_, 2 * p_ + 1
                    P.op("dve", lambda e: e.tensor_tensor(out=dstv[0:64, he, :], in0=TA.ap[0:64, 0:512],
                                                          in1=TB.ap[0:64, 0:512], op=ALU.add),
                         reads=TA.r(0, 512, half=0) + TB.r(0, 512, half=0), writes=dst.r(he * 512, (he + 1) * 512, half=0))
                    P.op("dve", lambda e: e.tensor_tensor(out=dstv[64:128, ho, :], in0=TA.ap[64:128, 0:512],
                                                          in1=TB.ap[64:128, 0:512], op=ALU.add),
                         reads=TA.r(0, 512, half=1) + TB.r(0, 512, half=1), writes=dst.r(ho * 512, (ho + 1) * 512, half=1))
                    P.dma(dstv[0:64, ho, :], dstv[64:128, ho, :], reads=dst.r(ho * 512, (ho + 1) * 512, half=1),
                          writes=dst.r(ho * 512, (ho + 1) * 512, half=0))
                    if is_q:
                        P.dma(dstv[64:128, he, :], dstv[0:64, he, :], reads=dst.r(he * 512, (he + 1) * 512, half=0),
                              writes=dst.r(he * 512, (he + 1) * 512, half=1))
            iqa, Wqa = WS.get("rqa")
            iqb, Wqb = WS.get("rqb")
            proj_rope(Wqa, Wqb, RQTv, RQT, True)
            WS.release(iqa); WS.release(iqb)
            ika, Wka = WS.get("rka")
            ikb, Wkb = WS.get("rkb")
            proj_rope(Wka, Wkb, RKTv, RKT, False)
            WS.release(ika); WS.release(ikb)
            if STOP == 5:
                break
            for wi, nm in enumerate(("rv0", "rv1", "rg0", "rg1")):
                iw, Ww = WS.get(nm)
                wv = Ww.ap.rearrange("p (k c) -> p k c", k=8)
                hf = wi % 2
                for t in range(4):
                    bank = 4 + (t % 2)

                    def mm(e, t=t, bank=bank, wv=wv):
                        for kc in range(8):
                            ins = e.matmul(PS(bank)[:, :], lhsT=XNTv[:, kc, t * 128:(t + 1) * 128], rhs=wv[:, kc, :],
                                           start=(kc == 0), stop=(kc == 7))
                        return ins
                    P.op("pe", mm, reads=XNT.r() + Ww.r(), writes=PSR(bank))
                    if wi < 2:
                        P.op("act", lambda e, t=t, bank=bank, hf=hf: e.copy(out=RVBv[:, t, hf * 512:(hf + 1) * 512],
                                                                            in_=PS(bank)[:, :]),
                             reads=PSR(bank), writes=RVB.r(t * 1024 + hf * 512, t * 1024 + (hf + 1) * 512))
                    else:
                        P.op("act", lambda e, t=t, bank=bank, hf=hf: e.activation(
                            out=SRGv[:, t, hf * 512:(hf + 1) * 512], in_=PS(bank)[:, :], func=AF.Silu),
                            reads=PSR(bank), writes=SRG.r(t * 1024 + hf * 512, t * 1024 + (hf + 1) * 512))
                WS.release(iw)
            if STOP == 6:
                break
            if STOP >= 70:
                pass
            def ret_A(t, b=b):
                n = b * 4 + t
                tc0 = t * 128
                ob0 = 4 if t % 2 == 0 else 2
                for hg in range(2):
                    sb_ = hg
                    smb, qp = SMB[hg], QP[hg]
                    smv = smb.v("p (h c) -> p h c", h=4)
                    qpv = qp.v("p (h c) -> p h c", h=4)

                    def mm(e):
                        for hi in range(4):
                            h = hg * 4 + hi
                            ins = e.matmul(PS(sb_)[:, hi * 128:(hi + 1) * 128],
                                           lhsT=RKTv[0:64, h, tc0:tc0 + 128],
                                           rhs=RQTv[0:64, h, tc0:tc0 + 128], start=True, stop=True)
                        return ins
                    P.op("pe", mm, reads=RKT.r() + RQT.r(), writes=PSR(sb_))
                    P.op("dve", lambda e: e.tensor_tensor(
                        out=smv, in0=PS(sb_)[:, :].rearrange("p (h c) -> p h c", h=4),
                        in1=DTv[:, hg * 4:(hg + 1) * 4, :], op=ALU.mult),
                        reads=PSR(sb_) + DT_.r(), writes=smb.r())
                    P.op("dve", lambda e: e.tensor_tensor(
                        out=qpv, in0=RQTv[:, hg * 4:(hg + 1) * 4, tc0:tc0 + 128],
                        in1=QDv[:, hg * 4:(hg + 1) * 4, :], op=ALU.mult),
                        reads=RQT.r() + QDEC.r(), writes=qp.r())

                    def omm(e):
                        for hi in range(4):
                            h = hg * 4 + hi
                            e.matmul(PS(ob0 + hg)[:, hi * 128:(hi + 1) * 128], lhsT=smv[:, hi, :],
                                     rhs=RVBv[:, t, h * 128:(h + 1) * 128], start=True, stop=False)
                            ins = e.matmul(PS(ob0 + hg)[:, hi * 128:(hi + 1) * 128], lhsT=qpv[:, hi, :],
                                           rhs=STv[:, n, h, :], start=False, stop=True)
                        return ins
                    P.op("pe", omm, reads=smb.r() + qp.r() + RVB.r(t * 1024, (t + 1) * 1024) +
                         ST.r(n * 1024, (n + 1) * 1024), writes=PSR(ob0 + hg))

            def ret_B(t, b=b):
                tc0 = t * 128
                ob0 = 4 if t % 2 == 0 else 2
                TAv = TA.v("p (h v) -> p h v", h=8)
                for hg in range(2):
                    P.op("act", lambda e: e.activation(out=TB.ap[:, hg * 512:(hg + 1) * 512],
                                                       in_=PS(ob0 + hg)[:, :], func=AF.Square),
                         reads=PSR(ob0 + hg), writes=TB.r(hg * 512, (hg + 1) * 512))
                ss8, ss8r = SC(16 + 8 * (t % 2), 8)
                P.op("dve", lambda e: e.tensor_reduce(out=ss8, in_=TB.v("p (h v) -> p h v", h=8), axis=AX.X,
                                                      op=ALU.add),
                     reads=TB.r(), writes=ss8r)
                act_rstd(ss8, ss8, 128, ss8r, ss8r)
                for hg in range(2):
                    P.op("dve", lambda e: e.tensor_tensor(
                        out=TAv[:, hg * 4:(hg + 1) * 4, :], in0=PS(ob0 + hg)[:, :].rearrange("p (h v) -> p h v", h=4),
                        in1=ss8[:, hg * 4:(hg + 1) * 4].unsqueeze(2).broadcast_to([128, 4, 128]), op=ALU.mult),
                        reads=PSR(ob0 + hg) + ss8r, writes=TA.r(hg * 512, (hg + 1) * 512))
                P.op("dve", lambda e: e.tensor_tensor(out=OB.ap, in0=TA.ap, in1=SRGv[:, t, :], op=ALU.mult),
                     reads=TA.r() + SRG.r(t * 1024, (t + 1) * 1024), writes=OB.r())
                transpose_to(OB, OBTv[:, :, tc0:tc0 + 128], OBT.r(), 7, evac="act")

            ret_A(0)
            for t in range(4):
                if t + 1 < 4:
                    ret_A(t + 1)
                ret_B(t)
            if STOP in (7, 70, 71, 72, 73):
                break
            for c in range(8):
                im, Wm = WS.get("mix%d" % c)
                wG = Wm.ap[:, 0:2048].rearrange("p (k c) -> p k c", k=8)
                wB = Wm.ap[:, 2048:3072].rearrange("p (k c) -> p k c", k=8)
                wAh = Wm.ap[0:64, 3072:4096].rearrange("p (k c) -> p k c", k=8)
                bb = (c % 2) * 4

                def mm(e, wG=wG, wB=wB, wAh=wAh, bb=bb, s0=s0):
                    for kc in range(8):
                        e.matmul(PS(bb)[:, :], lhsT=wG[:, kc, 0:128], rhs=XNTv[:, kc, :], start=(kc == 0), stop=(kc == 7))
                    for kc in range(8):
                        e.matmul(PS(bb + 1)[:, :], lhsT=wG[:, kc, 128:256], rhs=XNTv[:, kc, :], start=(kc == 0),
                                 stop=(kc == 7))
                    for hh in range(8):
                        e.matmul(PS(bb + 2)[:, :], lhsT=wAh[:, hh, :], rhs=OTv[0:64, hh, s0:s0 + 512],
                                 start=(hh == 0), stop=(hh == 7))
                    for kc in range(8):
                        ins = e.matmul(PS(bb + 3)[:, :], lhsT=wB[:, kc, :], rhs=OBTv[:, kc, :], start=(kc == 0),
                                       stop=(kc == 7))
                    return ins
                P.op("pe", mm, reads=Wm.r() + XNT.r() + OT.r() + OBT.r(),
                     writes=PSR(bb) + PSR(bb + 1) + PSR(bb + 2) + PSR(bb + 3))
                for gi, (tmp, lo) in enumerate(((TA, 0), (TA, 512))):
                    tv = tmp.ap[:, lo:lo + 512]
                    rr_ = tmp.r(lo, lo + 512)
                    P.op("act", lambda e, tv=tv, gi=gi, bb=bb: e.activation(out=tv, in_=PS(bb + gi)[:, :],
                                                                           func=AF.Sigmoid),
                         reads=PSR(bb + gi), writes=rr_)
                    P.op("dve", lambda e, tv=tv, gi=gi, bb=bb: e.tensor_tensor(out=tv, in0=PS(bb + 2 + gi)[:, :],
                                                                               in1=tv, op=ALU.mult),
                         reads=PSR(bb + 2 + gi) + rr_, writes=rr_)
                P.op("dve", lambda e, c=c: e.tensor_tensor(out=MTv[:, c, :], in0=TA.ap[:, 0:512],
                                                            in1=TA.ap[:, 512:1024], op=ALU.add),
                     reads=TA.r(), writes=MT.r(c * 512, (c + 1) * 512))
                WS.release(im)
            if STOP == 8:
                break
            if dbg is not None and j == 0 and b == 0:
                P.dma(d_mt, MT.ap, reads=MT.r(), final=True)
                P.dma(d_obt, OBT.ap, reads=OBT.r(), final=True)
                P.dma(d_ot, OT.ap, reads=OT.r(), final=True)
                P.dma(d_rqt, RQT.ap, reads=RQT.r(), final=True)
                P.dma(d_rkt, RKT.ap, reads=RKT.r(), final=True)
                P.dma(d_st, ST.ap, reads=ST.r(), final=True)
            io0, Wo0 = WS.get("wo0")
            io1, Wo1 = WS.get("wo1")
            wo = [Wo0.ap.rearrange("p (k c) -> p k c", k=8), Wo1.ap.rearrange("p (k c) -> p k c", k=8)]
            for t in range(4):
                P.dma(X1v[:, t, :], xo[g0 + t * 128: g0 + (t + 1) * 128, :], writes=X1.r(t * 1024, (t + 1) * 1024))
            for t in range(4):
                b0 = (t % 2) * 2

                def mm(e, t=t, b0=b0):
                    for hf in range(2):
                        for kc in range(8):
                            ins = e.matmul(PS(b0 + hf)[:, :], lhsT=MTv[:, kc, t * 128:(t + 1) * 128],
                                           rhs=wo[hf][:, kc, :], start=(kc == 0), stop=(kc == 7))
                    return ins
                P.op("pe", mm, reads=MT.r() + Wo0.r() + Wo1.r(), writes=PSR(b0) + PSR(b0 + 1))
                s6, s6r = SC(32 + 4 * t, 3)
                for hf in range(2):
                    P.op("act", lambda e, hf=hf: e.activation(
                        out=TB.ap[:, hf * 512:(hf + 1) * 512], in_=PS(b0 + hf)[:, :], func=AF.Square,
                        accum_out=s6[:, hf:hf + 1]),
                        reads=PSR(b0 + hf), writes=TB.r(hf * 512, (hf + 1) * 512) + s6r)
                ssm = s6[:, 2:3]
                P.op("dve", lambda e: e.tensor_tensor(out=ssm, in0=s6[:, 0:1], in1=s6[:, 1:2], op=ALU.add),
                     reads=s6r, writes=s6r)
                act_rstd(ssm, ssm, D, s6r, s6r)
                for hf in range(2):
                    P.op("dve", lambda e, hf=hf: e.scalar_tensor_tensor(
                        out=TA.ap[:, hf * 512:(hf + 1) * 512], in0=PS(b0 + hf)[:, :], scalar=ssm,
                        in1=G1.ap[:, hf * 512:(hf + 1) * 512], op0=ALU.mult, op1=ALU.mult),
                        reads=PSR(b0 + hf) + s6r + G1.r(), writes=TA.r(hf * 512, (hf + 1) * 512))
                P.op("dve", lambda e, t=t: e.tensor_tensor(out=X1v[:, t, :], in0=X1v[:, t, :], in1=TA.ap, op=ALU.add),
                     reads=TA.r() + X1.r(t * 1024, (t + 1) * 1024), writes=X1.r(t * 1024, (t + 1) * 1024))
            WS.release(io0); WS.release(io1)
            if STOP == 81:
                break
            if dbg is not None:
                for t in range(4):
                    P.dma(dbg[g0 + t * 128: g0 + (t + 1) * 128, :], X1v[:, t, :], reads=X1.r(t * 1024, (t + 1) * 1024),
                          final=True)
            for t in range(4):
                ssap, s7r = SC(48 + 4 * t)
                P.op("act", lambda e, t=t, ssap=ssap: e.activation(out=TA.ap, in_=X1v[:, t, :], func=AF.Square,
                                                                    accum_out=ssap),
                     reads=X1.r(t * 1024, (t + 1) * 1024), writes=TA.r() + s7r)
                act_rstd(ssap, ssap, D, s7r, s7r)
                P.op("dve", lambda e, t=t, ssap=ssap: e.tensor_scalar(out=XN2.ap, in0=X1v[:, t, :], scalar1=ssap,
                                                                      scalar2=None, op0=ALU.mult),
                     reads=X1.r(t * 1024, (t + 1) * 1024) + s7r, writes=XN2.r())
                transpose_to(XN2, XNTv[:, :, t * 128:(t + 1) * 128], XNT.r(), 7, evac="act")
            if STOP == 82:
                break
            for g in range(8):
                iu, Wu = WS.get("wu%d" % g)
                wu = Wu.ap.rearrange("p (k c) -> p k c", k=8)
                for mc in range(4):
                    ch = g * 4 + mc
                    bank = ch % 4
                    rl = RL[ch % 4]

                    def mm(e, mc=mc, bank=bank, wu=wu):
                        for kc in range(8):
                            ins = e.matmul(PS(bank)[:, :], lhsT=wu[:, kc, mc * 128:(mc + 1) * 128], rhs=XNTv[:, kc, :],
                                           start=(kc == 0), stop=(kc == 7))
                        return ins
                    P.op("pe", mm, reads=Wu.r() + XNT.r(), writes=PSR(bank))
                    P.op("act", lambda e, bank=bank, rl=rl: e.activation(out=rl.ap, in_=PS(bank)[:, :], func=AF.Relu),
                         reads=PSR(bank), writes=rl.r())
                    P.op("dve", lambda e, ch=ch, rl=rl: e.tensor_tensor(
                        out=UTc(ch), in0=rl.ap, in1=rl.ap, op=ALU.mult),
                        reads=rl.r(), writes=UTr(ch))
                WS.release(iu)
            if STOP == 83:
                break
            for hf in range(2):
                for g in range(4):
                    idn, Wd = WS.get("wd%d_%d" % (hf, g))
                    wd = Wd.ap.rearrange("p (k c) -> p k c", k=8)
                    for t in range(4):
                        def mm(e, t=t, g=g, wd=wd):
                            for k8 in range(8):
                                kc = g * 8 + k8
                                ins = e.matmul(PS(4 + t)[:, :], lhsT=UTc(kc)[:, t * 128:(t + 1) * 128], rhs=wd[:, k8, :],
                                               start=(kc == 0), stop=(kc == 31))
                            return ins
                        P.op("pe", mm, reads=Wd.r() + (UTA if g < 2 else UTB).r((g % 2) * 4096, (g % 2 + 1) * 4096),
                             writes=PSR(4 + t))
                    WS.release(idn)
                if hf == 0:
                    for t in range(4):
                        sf, sfr = SC(64 + 4 * t, 3)
                        P.op("act", lambda e, t=t, sf=sf: e.activation(out=YF0v[:, t, :], in_=PS(4 + t)[:, :], func=AF.Square,
                                                                       accum_out=sf[:, 0:1]),
                             reads=PSR(4 + t), writes=YF0.r(t * 512, (t + 1) * 512) + sfr)
                        P.op("dve", lambda e, t=t: e.tensor_copy(out=YF0v[:, t, :], in_=PS(4 + t)[:, :]),
                             reads=PSR(4 + t) + YF0.r(t * 512, (t + 1) * 512), writes=YF0.r(t * 512, (t + 1) * 512))
            if STOP == 84:
                break
            if b + 1 < NB:
                p2_step1(b + 1)
            for t in range(4):
                sf, sfr = SC(64 + 4 * t, 3)
                tmp = TA if t % 2 == 0 else TB
                P.op("act", lambda e, t=t, sf=sf, tmp=tmp: e.activation(out=tmp.ap[:, 0:512], in_=PS(4 + t)[:, :],
                                                                        func=AF.Square, accum_out=sf[:, 1:2]),
                     reads=PSR(4 + t), writes=tmp.r(0, 512) + sfr)
                ssy = sf[:, 2:3]
                P.op("dve", lambda e, sf=sf, ssy=ssy: e.tensor_tensor(out=ssy, in0=sf[:, 0:1], in1=sf[:, 1:2], op=ALU.add),
                     reads=sfr, writes=sfr)
                act_rstd(ssy, ssy, D, sfr, sfr)
                P.op("dve", lambda e, t=t, ssy=ssy, tmp=tmp: e.scalar_tensor_tensor(
                    out=tmp.ap[:, 0:512], in0=YF0v[:, t, :], scalar=ssy, in1=G2.ap[:, 0:512], op0=ALU.mult,
                    op1=ALU.mult), reads=YF0.r(t * 512, (t + 1) * 512) + sfr + G2.r() + tmp.r(0, 512),
                    writes=tmp.r(0, 512))
                P.op("dve", lambda e, t=t, ssy=ssy, tmp=tmp: e.scalar_tensor_tensor(
                    out=tmp.ap[:, 512:1024], in0=PS(4 + t)[:, :], scalar=ssy, in1=G2.ap[:, 512:1024], op0=ALU.mult,
                    op1=ALU.mult), reads=PSR(4 + t) + sfr + G2.r(), writes=tmp.r(512, 1024))
                P.op("dve", lambda e, t=t, tmp=tmp: e.tensor_tensor(out=X1v[:, t, :], in0=tmp.ap, in1=X1v[:, t, :], op=ALU.add),
                     reads=tmp.r() + X1.r(t * 1024, (t + 1) * 1024), writes=X1.r(t * 1024, (t + 1) * 1024))
                P.dma(y[g0 + t * 128: g0 + (t + 1) * 128, :], X1v[:, t, :], reads=X1.r(t * 1024, (t + 1) * 1024),
                      final=True)

        key_off += nkt * 128
        ctx_off += nctx
        if 4 <= STOP < 9 or STOP >= 70:
            break

    P.finish()
    es.close()
    return nc, P


def rope_tab(pos, dim):
    inv = (1.0 / (10000.0 ** (np.arange(0, dim, 2, dtype=np.float32) / np.float32(dim)))).astype(np.float32)
    ang = pos.astype(np.float32)[:, None] * inv[None, :]
    return np.cos(ang).astype(np.float32), np.sin(ang).astype(np.float32)


def make_core_inputs(cfg, jobs, weights):
    S_OWN = cfg["S_OWN"]
    NT = S_OWN // 128
    xo = np.concatenate([jb["x_own"] for jb in jobs], 0)
    xcs = [jb["x_ctx"] for jb in jobs if jb["x_ctx"].shape[0] > 0]
    xc = np.concatenate(xcs, 0) if xcs else np.zeros((128, D), np.float32)
    tabT, tabM, tabF, cDl, cFl = [], [], [], [], []
    for jb in jobs:
        pos = np.concatenate([jb["pos_own"], jb["pos_ctx"]]).astype(np.int64)
        cr, sr = rope_tab(pos, 64)
        tabT.append(np.concatenate([cr, sr], 1))
        cm, sm = rope_tab(pos, 32)
        tabM.append(np.stack([np.concatenate([cm, cm], 1).T, np.concatenate([sm, sm], 1).T], 1))
        cro, sro = rope_tab(jb["pos_own"].astype(np.int64), 64)
        c64 = np.concatenate([cro, cro], 1).T
        s64 = np.concatenate([sro, sro], 1).T
        tabF.append(np.stack([np.concatenate([c64, c64], 0), np.concatenate([s64, s64], 0)], 1))
        nc_ = jb["pos_ctx"].shape[0] // 128
        if nc_:
            g0 = int(jb["pos_own"][0]) // 128
            gl = g0 + NT - 1
            gm = jb["pos_ctx"][::128].astype(np.int64) // 128
            dD = np.zeros((128, nc_), np.float32)
            dF = np.zeros((128, nc_), np.float32)
            left = gm < g0
            right = gm > gl
            dD[0:64, left] = 128.0 * (g0 - gm[left] - 1)
            dF[0:64, left] = 1.0
            dD[64:128, right] = 128.0 * (gm[right] - gl - 1)
            dF[64:128, right] = 1.0
            cDl.append(dD)
            cFl.append(dF)
    m = {
        "xo": np.ascontiguousarray(xo, np.float32),
        "xc": np.ascontiguousarray(xc, np.float32),
        "tabT": np.ascontiguousarray(np.concatenate(tabT, 0), np.float32),
        "tabM": np.ascontiguousarray(np.concatenate(tabM, 2), np.float32),
        "tabF": np.ascontiguousarray(np.concatenate(tabF, 2), np.float32),
        "cD": np.ascontiguousarray(np.concatenate(cDl, 1) if cDl else np.zeros((128, 1), np.float32)),
        "cF": np.ascontiguousarray(np.concatenate(cFl, 1) if cFl else np.zeros((128, 1), np.float32)),
    }
    m.update(weights)
    return m


def prep_weights(inp):
    w = {}
    for k in ("w_in", "w_q_up", "w_kv_up", "w_branch_a", "w_branch_b", "w_out", "w_up", "w_down",
              "g_pre_mix", "g_q_norm", "g_kv_norm", "g_post_mix", "g_pre_mlp", "g_post_mlp"):
        w[k] = np.ascontiguousarray(np.asarray(inp[k], np.float32)[0])
    w["lgf"] = np.ascontiguousarray(np.asarray(inp["ret_log_decay_fwd"], np.float32)[0])
    w["lgb"] = np.ascontiguousarray(np.asarray(inp["ret_log_decay_bwd"], np.float32)[0])
    return w


CFG = {"S_OWN": 2048, "nctx": [0, 0, 0, 0, 48, 48]}
_CACHE = {}


def kernel(**inputs):
    xp = np.asarray(inputs["x_prompt"], np.float32)
    xs = np.asarray(inputs["x_sample"], np.float32)
    weights = prep_weights(inputs)
    cfg = CFG
    S = cfg["S_OWN"]
    in_maps = []
    for core in range(8):
        jobs = []
        for i in range(4):
            jobs.append(dict(x_own=xp[core * 4 + i], pos_own=np.arange(S),
                             x_ctx=np.zeros((0, D), np.float32), pos_ctx=np.zeros((0,), np.int64)))
        sb, half = core // 2, core % 2
        for sj in range(2):
            a0 = half * 4096 + sj * S
            own_pos = np.arange(a0, a0 + S)
            ctx_pos = np.concatenate([np.arange(0, a0), np.arange(a0 + S, 8192)])
            jobs.append(dict(x_own=xs[sb, a0:a0 + S], pos_own=own_pos,
                             x_ctx=xs[sb][ctx_pos], pos_ctx=ctx_pos))
        in_maps.append(make_core_inputs(cfg, jobs, weights))
    if "nc" not in _CACHE:
        _CACHE["nc"] = build(cfg)[0]
    res = run_bass_kernel_spmd(_CACHE["nc"], in_maps, core_ids=list(range(8)))
    yp = np.zeros_like(xp)
    ys = np.zeros_like(xs)
    for core in range(8):
        yy = res.results[core]["y"]
        for i in range(4):
            yp[core * 4 + i] = yy[i * S:(i + 1) * S]
        sb, half = core // 2, core % 2
        for sj in range(2):
            a0 = half * 4096 + sj * S
            ys[sb, a0:a0 + S] = yy[(4 + sj) * S:(5 + sj) * S]
    return (yp, ys)
```
